# Optimizing a Trainium2 kernel written in Bass

```python
import math
import jax, jax.numpy as jnp
from jax import lax
import numpy as np

D_MODEL = 1024
BATCH = 4
SEQ = 8192
DEPTH = 2
DEC_BATCH = 8
DEC_SEQ = 64
PAST_LEN = 2048

CHUNK = 64
EPS = 1e-6
D_FF = 2816
SGU_WIDTH = D_MODEL // 4
SGU_HEADS = 4
SGU_HEAD_DIM = SGU_WIDTH // SGU_HEADS
SGU_CHUNK = 128
POOL_WIDTH = D_MODEL // 4
POOL_WINDOWS = (2, 4, 8, 16)
POOL_GROUPS = len(POOL_WINDOWS)
POOL_GROUP_DIM = POOL_WIDTH // POOL_GROUPS
POOL_HIST = max(POOL_WINDOWS) - 1
GDN_WIDTH = D_MODEL // 2
GDN_HEADS = 4
GDN_DK = GDN_WIDTH // GDN_HEADS
GDN_DV = GDN_WIDTH // GDN_HEADS
GDN_KEY = GDN_HEADS * GDN_DK
GDN_QKV = 2 * GDN_KEY + GDN_WIDTH
GDN_CONV = 4
GDN_CHUNK = 64
MIX_WIDTH = SGU_WIDTH + POOL_WIDTH + GDN_WIDTH
MEM_LEN = 256
MEM_HEADS = 4
MEM_HEAD_DIM = 128
MEM_WIDTH = MEM_HEADS * MEM_HEAD_DIM
OFF_POOL = 2 * SGU_WIDTH
OFF_QKV = OFF_POOL + POOL_WIDTH
OFF_Z = OFF_QKV + GDN_QKV
OFF_BETA = OFF_Z + GDN_WIDTH
OFF_A = OFF_BETA + GDN_HEADS
N_IN = OFF_A + GDN_HEADS

kernel_name = 'hybrid_streaming_encoder_step'


def rms_norm(x, g):
    xf = x.astype(jnp.float32)
    y = xf * lax.rsqrt(jnp.mean(xf * xf, axis=-1, keepdims=True) + EPS)
    return (y * g.astype(jnp.float32)).astype(x.dtype)


def _head_layer_norm(v, g):
    vf = v.astype(jnp.float32)
    vc = vf - jnp.mean(vf, axis=-1, keepdims=True)
    y = vc * lax.rsqrt(jnp.mean(vc * vc, axis=-1, keepdims=True) + EPS)
    return (y * g.astype(jnp.float32)).astype(v.dtype)


def _l2norm(t):
    return t * lax.rsqrt(jnp.sum(t * t, axis=-1, keepdims=True) + 1e-6)


def swiglu(h, w_gate, w_up, w_down):
    return (jax.nn.silu(h @ w_gate) * (h @ w_up)) @ w_down


def sgu_mix(a, w_s, b_s, g_n):
    B, L, _ = a.shape
    C = min(SGU_CHUNK, L)
    u = a[..., :SGU_WIDTH]
    v = _head_layer_norm(a[..., SGU_WIDTH:].reshape(B, L, SGU_HEADS, SGU_HEAD_DIM),
                         g_n.reshape(SGU_HEADS, SGU_HEAD_DIM))
    blk = np.arange(C) // CHUNK
    mask = blk[:, None] >= blk[None, :]
    w = jnp.where(mask, w_s[:, :C, :C], 0.0).astype(v.dtype)
    s = jnp.einsum('hij,bnjhc->bnihc', w, v.reshape(B, L // C, C, SGU_HEADS, SGU_HEAD_DIM))
    s = s + b_s[:, :C].T[None, None, :, :, None].astype(s.dtype)
    return u * s.reshape(B, L, SGU_WIDTH), v.reshape(B, L, SGU_WIDTH)


def pool_mix(p, hist, pos, w_pool, scale):
    B, L, W = p.shape
    xp = jnp.concatenate([hist.astype(p.dtype), p], axis=1)
    cs = jnp.concatenate([jnp.zeros((B, 1, W), jnp.float32),
                          jnp.cumsum(xp.astype(jnp.float32), axis=1)], axis=1)
    end = cs[:, POOL_HIST + 1:]
    means = []
    for gi, w in enumerate(POOL_WINDOWS):
        sl = slice(gi * POOL_GROUP_DIM, (gi + 1) * POOL_GROUP_DIM)
        start = cs[:, POOL_HIST + 1 - w:POOL_HIST + 1 - w + L, sl]
        cnt = jnp.minimum(w, pos + 1).astype(jnp.float32)[None, :, None]
        means.append((end[..., sl] - start) / cnt)
    d = (jnp.concatenate(means, axis=-1) - p.astype(jnp.float32)).astype(p.dtype)
    y = jnp.einsum('blgc,gcd->blgd', d.reshape(B, L, POOL_GROUPS, POOL_GROUP_DIM), w_pool)
    return y.reshape(B, L, W) * scale, xp[:, -POOL_HIST:]


def causal_conv(x, hist, w):
    L = x.shape[1]
    xc = jnp.concatenate([hist.astype(x.dtype), x], axis=1)
    y = xc[:, 0:L] * w[0]
    for j in range(1, GDN_CONV):
        y = y + xc[:, j:j + L] * w[j]
    return y, xc[:, -(GDN_CONV - 1):]


def gated_delta_rule(q, k, v, g, beta, s0):
    B, L, H, Dk = q.shape
    C = min(GDN_CHUNK, L)
    N = L // C

    def chunks(t):
        return jnp.moveaxis(t.reshape((B, N, C, H) + t.shape[3:]), (1, 3), (0, 2))

    qc, kc, vc = chunks(q * (Dk ** -0.5)), chunks(k), chunks(v)
    gc = jnp.cumsum(chunks(g), axis=-1)
    bc = chunks(beta)
    kb = kc * bc[..., None]
    vb = vc * bc[..., None]
    incl = np.tril(np.ones((C, C), dtype=bool))
    strict = np.tril(np.ones((C, C), dtype=bool), -1)
    decay = jnp.exp(jnp.where(incl, gc[..., :, None] - gc[..., None, :], -jnp.inf))
    lmat = jnp.where(strict, jnp.einsum('nbhid,nbhjd->nbhij', kb, kc) * decay, 0.0)
    eye = jnp.eye(C, dtype=jnp.float32)
    t_inv = lax.linalg.triangular_solve(lmat + eye, jnp.broadcast_to(eye, lmat.shape),
                                        left_side=True, lower=True, unit_diagonal=True)
    u = t_inv @ vb
    w = t_inv @ (kb * jnp.exp(gc)[..., None])
    qk = jnp.einsum('nbhid,nbhjd->nbhij', qc, kc) * decay
    qg = qc * jnp.exp(gc)[..., None]
    kg = kc * jnp.exp(gc[..., -1:] - gc)[..., None]
    gl = jnp.exp(gc[..., -1])

    def step(S, xs):
        u_i, w_i, qg_i, qk_i, kg_i, gl_i = xs
        v_new = u_i - w_i @ S
        o_i = qg_i @ S + qk_i @ v_new
        S = S * gl_i[..., None, None] + jnp.einsum('bhcd,bhce->bhde', kg_i, v_new)
        return S, o_i

    S, o = lax.scan(step, s0, (u, w, qg, qk, kg, gl))
    o = jnp.moveaxis(o, (0, 2), (1, 3)).reshape(B, L, H, v.shape[-1])
    return o, S


def gdn_mix(proj, conv_hist, s0, conv_w, a_log, dt_bias, norm_g):
    B, L, _ = proj.shape
    qkv, conv_new = causal_conv(proj[..., OFF_QKV:OFF_Z], conv_hist, conv_w)
    qkv = jax.nn.silu(qkv.astype(jnp.float32))
    q = _l2norm(qkv[..., :GDN_KEY].reshape(B, L, GDN_HEADS, GDN_DK))
    k = _l2norm(qkv[..., GDN_KEY:2 * GDN_KEY].reshape(B, L, GDN_HEADS, GDN_DK))
    v = qkv[..., 2 * GDN_KEY:].reshape(B, L, GDN_HEADS, GDN_DV)
    beta = jax.nn.sigmoid(proj[..., OFF_BETA:OFF_A].astype(jnp.float32))
    g = -jnp.exp(a_log.astype(jnp.float32)) * jax.nn.softplus(
        proj[..., OFF_A:N_IN].astype(jnp.float32) + dt_bias.astype(jnp.float32))
    o, s_new = gated_delta_rule(q, k, v, g, beta, s0.astype(jnp.float32))
    z = proj[..., OFF_Z:OFF_BETA].reshape(B, L, GDN_HEADS, GDN_DV).astype(jnp.float32)
    o = rms_norm(o, norm_g) * jax.nn.silu(z)
    return o.reshape(B, L, GDN_WIDTH).astype(proj.dtype), conv_new, s_new


def mem_kv(mem, g, w_k, w_v):
    B, M, _ = mem.shape
    m = rms_norm(mem, g)
    return ((m @ w_k).reshape(B, M, MEM_HEADS, MEM_HEAD_DIM),
            (m @ w_v).reshape(B, M, MEM_HEADS, MEM_HEAD_DIM))


def mem_attend(h, mk, mv, w_q, w_o):
    B, L, _ = h.shape
    q = (h @ w_q).reshape(B, L, MEM_HEADS, MEM_HEAD_DIM)
    s = jnp.einsum('blhd,bmhd->bhlm', q, mk.astype(q.dtype)).astype(jnp.float32) * (MEM_HEAD_DIM ** -0.5)
    p = jax.nn.softmax(s, axis=-1).astype(q.dtype)
    o = jnp.einsum('bhlm,bmhd->blhd', p, mv.astype(q.dtype)).reshape(B, L, MEM_WIDTH)
    return o @ w_o


def layer(x, mk, mv, pool_hist, conv_hist, s0, pos,
          ffn1_norm, ffn1_w_gate, ffn1_w_up, ffn1_w_down,
          mix_norm, w_in, sgu_norm, sgu_w, sgu_b, pool_w, pool_scale,
          gdn_conv_w, gdn_a_log, gdn_dt_bias, gdn_out_norm, w_out,
          xattn_norm, w_mq, w_mo,
          ffn2_norm, ffn2_w_gate, ffn2_w_up, ffn2_w_down):
    h = x + 0.5 * swiglu(rms_norm(x, ffn1_norm), ffn1_w_gate, ffn1_w_up, ffn1_w_down)
    proj = rms_norm(h, mix_norm) @ w_in
    y_a, v_sgu = sgu_mix(jax.nn.gelu(proj[..., :OFF_POOL]), sgu_w, sgu_b, sgu_norm)
    y_b, pool_new = pool_mix(proj[..., OFF_POOL:OFF_QKV], pool_hist, pos, pool_w, pool_scale)
    y_c, conv_new, s_new = gdn_mix(proj, conv_hist, s0, gdn_conv_w, gdn_a_log, gdn_dt_bias, gdn_out_norm)
    h = h + jnp.concatenate([y_a, y_b, y_c], axis=-1) @ w_out
    h = h + mem_attend(rms_norm(h, xattn_norm), mk, mv, w_mq, w_mo)
    h = h + 0.5 * swiglu(rms_norm(h, ffn2_norm), ffn2_w_gate, ffn2_w_up, ffn2_w_down)
    return h, v_sgu, pool_new, conv_new, s_new


def setup_inputs(seed: int = 0) -> dict:
    key = jax.random.key(seed)
    ks = iter(jax.random.split(key, 48))
    f32 = jnp.float32
    L = DEPTH

    def nrm(shape, scale):
        return jax.random.normal(next(ks), shape, f32) * scale

    def gain(shape, s=0.02):
        return 1.0 + s * jax.random.normal(next(ks), shape, f32)

    inp = {}
    inp['x_prompt'] = nrm((BATCH, SEQ, D_MODEL), 1.0)
    inp['x_sample'] = nrm((DEC_BATCH, DEC_SEQ, D_MODEL), 1.0)
    inp['mem_prompt'] = nrm((BATCH, MEM_LEN, D_MODEL), 1.0)
    inp['cache_mem_k'] = nrm((L, DEC_BATCH, MEM_LEN, MEM_HEADS, MEM_HEAD_DIM), 1.0)
    inp['cache_mem_v'] = nrm((L, DEC_BATCH, MEM_LEN, MEM_HEADS, MEM_HEAD_DIM), 1.0)
    inp['state_pool'] = nrm((L, DEC_BATCH, POOL_HIST, POOL_WIDTH), 1.0)
    inp['state_conv'] = nrm((L, DEC_BATCH, GDN_CONV - 1, GDN_QKV), 1.0)
    inp['state_ssm'] = nrm((L, DEC_BATCH, GDN_HEADS, GDN_DK, GDN_DV), GDN_DK ** -0.5)
    inp['ffn1_norm'] = gain((L, D_MODEL))
    inp['ffn1_w_gate'] = nrm((L, D_MODEL, D_FF), D_MODEL ** -0.5)
    inp['ffn1_w_up'] = nrm((L, D_MODEL, D_FF), D_MODEL ** -0.5)
    inp['ffn1_w_down'] = nrm((L, D_FF, D_MODEL), D_FF ** -0.5)
    inp['mix_norm'] = gain((L, D_MODEL))
    inp['w_in'] = nrm((L, D_MODEL, N_IN), D_MODEL ** -0.5)
    inp['sgu_norm'] = gain((L, SGU_WIDTH))
    inp['sgu_w'] = nrm((L, SGU_HEADS, SGU_CHUNK, SGU_CHUNK), SGU_CHUNK ** -0.5)
    inp['sgu_b'] = gain((L, SGU_HEADS, SGU_CHUNK), 0.01)
    inp['pool_w'] = nrm((L, POOL_GROUPS, POOL_GROUP_DIM, POOL_GROUP_DIM), POOL_GROUP_DIM ** -0.5)
    inp['pool_scale'] = gain((L, POOL_WIDTH), 0.1)
    inp['gdn_conv_w'] = nrm((L, GDN_CONV, GDN_QKV), 0.5)
    inp['gdn_a_log'] = jnp.log(jax.random.uniform(next(ks), (L, GDN_HEADS), f32, 1.0, 16.0))
    dt = jnp.exp(jax.random.uniform(next(ks), (L, GDN_HEADS), f32, math.log(1e-3), math.log(1e-1)))
    inp['gdn_dt_bias'] = dt + jnp.log(-jnp.expm1(-dt))
    inp['gdn_out_norm'] = gain((L, GDN_DV))
    inp['w_out'] = nrm((L, MIX_WIDTH, D_MODEL), MIX_WIDTH ** -0.5)
    inp['xattn_norm'] = gain((L, D_MODEL))
    inp['mem_norm'] = gain((L, D_MODEL))
    inp['w_mq'] = nrm((L, D_MODEL, MEM_WIDTH), D_MODEL ** -0.5)
    inp['w_mk'] = nrm((L, D_MODEL, MEM_WIDTH), D_MODEL ** -0.5)
    inp['w_mv'] = nrm((L, D_MODEL, MEM_WIDTH), D_MODEL ** -0.5)
    inp['w_mo'] = nrm((L, MEM_WIDTH, D_MODEL), MEM_WIDTH ** -0.5)
    inp['ffn2_norm'] = gain((L, D_MODEL))
    inp['ffn2_w_gate'] = nrm((L, D_MODEL, D_FF), D_MODEL ** -0.5)
    inp['ffn2_w_up'] = nrm((L, D_MODEL, D_FF), D_MODEL ** -0.5)
    inp['ffn2_w_down'] = nrm((L, D_FF, D_MODEL), D_FF ** -0.5)
    inp['final_norm'] = gain((D_MODEL,))
    return inp


def reference(x_prompt, x_sample, mem_prompt, cache_mem_k, cache_mem_v, state_pool, state_conv, state_ssm,
              ffn1_norm, ffn1_w_gate, ffn1_w_up, ffn1_w_down,
              mix_norm, w_in, sgu_norm, sgu_w, sgu_b, pool_w, pool_scale,
              gdn_conv_w, gdn_a_log, gdn_dt_bias, gdn_out_norm, w_out,
              xattn_norm, mem_norm, w_mq, w_mk, w_mv, w_mo,
              ffn2_norm, ffn2_w_gate, ffn2_w_up, ffn2_w_down, final_norm):
    Bp, Lp, _ = x_prompt.shape
    Ls = x_sample.shape[1]
    pos_p = jnp.arange(Lp, dtype=jnp.int32)
    pos_s = PAST_LEN + jnp.arange(Ls, dtype=jnp.int32)
    pool0 = jnp.zeros((Bp, POOL_HIST, POOL_WIDTH), x_prompt.dtype)
    conv0 = jnp.zeros((Bp, GDN_CONV - 1, GDN_QKV), x_prompt.dtype)
    ssm0 = jnp.zeros((Bp, GDN_HEADS, GDN_DK, GDN_DV), jnp.float32)
    xp, xs = x_prompt, x_sample
    p_pool, p_conv, p_ssm, p_mk, p_mv = [], [], [], [], []
    s_pool, s_conv, s_ssm, s_v = [], [], [], []
    for l in range(DEPTH):
        lw = (ffn1_norm[l], ffn1_w_gate[l], ffn1_w_up[l], ffn1_w_down[l],
              mix_norm[l], w_in[l], sgu_norm[l], sgu_w[l], sgu_b[l], pool_w[l], pool_scale[l],
              gdn_conv_w[l], gdn_a_log[l], gdn_dt_bias[l], gdn_out_norm[l], w_out[l],
              xattn_norm[l], w_mq[l], w_mo[l],
              ffn2_norm[l], ffn2_w_gate[l], ffn2_w_up[l], ffn2_w_down[l])
        mk, mv = mem_kv(mem_prompt, mem_norm[l], w_mk[l], w_mv[l])
        xp, _, pp, pc, ps = layer(xp, mk, mv, pool0, conv0, ssm0, pos_p, *lw)
        xs, sv, sp, sc, ss = layer(xs, cache_mem_k[l], cache_mem_v[l], state_pool[l], state_conv[l],
                                   state_ssm[l], pos_s, *lw)
        p_pool.append(pp); p_conv.append(pc); p_ssm.append(ps); p_mk.append(mk); p_mv.append(mv)
        s_pool.append(sp); s_conv.append(sc); s_ssm.append(ss); s_v.append(sv)
    y_prompt = rms_norm(xp, final_norm)
    y_sample = rms_norm(xs, final_norm)
    prompt_state_pool = jnp.stack(p_pool)
    prompt_state_conv = jnp.stack(p_conv)
    prompt_state_ssm = jnp.stack(p_ssm)
    prompt_mem_k = jnp.stack(p_mk)
    prompt_mem_v = jnp.stack(p_mv)
    sample_state_pool = jnp.stack(s_pool)
    sample_state_conv = jnp.stack(s_conv)
    sample_state_ssm = jnp.stack(s_ssm)
    sample_sgu_v = jnp.stack(s_v)
    return (y_prompt, y_sample, prompt_state_pool, prompt_state_conv, prompt_state_ssm, prompt_mem_k, prompt_mem_v,
            sample_state_pool, sample_state_conv, sample_state_ssm, sample_sgu_v)
```

```python
import numpy as np
from contextlib import ExitStack
import concourse.bass as bass
import concourse.mybir as mybir
from concourse.bass_utils import run_bass_kernel_spmd

F32 = mybir.dt.float32
BF16 = mybir.dt.bfloat16
ALU = mybir.AluOpType
AF = mybir.ActivationFunctionType
AX = mybir.AxisListType

D = 1024
KC = 8
DFF = 2816
FC = 22
NIN = 2824
DEPTH = 2
NCORES = 8
EPS = 1e-6
MEM = 256
SAME_ENGINE_SYNC = True
DBG = None
STAGE = 99
MIXSTOP = 99
PIPELINE = True
POLICY = 'ratio'
FRATIO = 1
HRATIO = 3
FFN_SHARED = True
FFN_EXPLN = True
CONV_POOL = False
SGU_STOP = 99
GDN_STOP = 99
DBG_MIX = None
GDT = F32

ENGS = ['pe', 'act', 'dve', 'pool', 'sp']


class Holder:
    view = None
    key = None
    refs = 0
    slot = None


class LazyKey:
    def __init__(self, h):
        self.h = h

    @property
    def key(self):
        return self.h.key


class LazyView:
    def __init__(self, h):
        self.h = h

    def __getitem__(self, idx):
        return self.h.view[idx]


class Prog:
    def __init__(self, nc, es):
        self.nc = nc
        self.es = es
        self.ops = {e: [] for e in ENGS}
        self.last_w = {}
        self.readers = {}
        self.dma_cnt = {}
        self.dma_keys_written = {}
        self.rec = None
        self.slot_live = {}

    def sb(self, name, shape, dt):
        return self.es.enter_context(self.nc.sbuf_tensor(name, list(shape), dt))

    def ps(self, name, shape, dt):
        return self.es.enter_context(self.nc.psum_tensor(name, list(shape), dt))

    def _deps(self, r, w):
        deps = []
        for k in r:
            if k in self.last_w:
                deps.append(self.last_w[k])
        for k in w:
            if k in self.last_w:
                deps.append(self.last_w[k])
            rd = self.readers.get(k)
            if rd:
                for (kind, tgt), val in rd.items():
                    deps.append((kind, tgt, val, 'war'))
        return deps

    def _commit(self, ev, r, w):
        for k in r:
            rd = self.readers.setdefault(k, {})
            kk = (ev[0], ev[1])
            if rd.get(kk, -1) < ev[2]:
                rd[kk] = ev[2]
        for k in w:
            self.last_w[k] = ev
            self.readers[k] = {}

    def op(self, eng, fn, r=(), w=()):
        if self.rec is not None:
            for k in r:
                if isinstance(k, LazyKey):
                    k.h.refs += 1
            self.rec.append(('op', eng, fn, list(r), list(w)))
            return
        for k in r:
            if isinstance(k, LazyKey) and k.h.refs > 0:
                k.h.refs -= 1
                if k.h.refs == 0:
                    self.slot_live[k.h.slot] = False
        r = [k.key if isinstance(k, LazyKey) else k for k in r]
        w = [k.key if isinstance(k, LazyKey) else k for k in w]
        pr = [k for k in r if k.startswith('ps')]
        if pr:
            w = list(w) + pr
        deps = self._deps(r, w)
        idx = len(self.ops[eng])
        self.ops[eng].append(dict(fn=fn, deps=deps, dma=None))
        self._commit(('c', eng, idx), r, w)

    def custom(self, fn):
        if self.rec is not None:
            self.rec.append(('custom', fn))
            return
        fn()

    def mark(self, typ):
        if self.rec is not None:
            self.rec.append(('mark', typ))

    def feed_one(self, recd):
        kind = recd[0]
        if kind == 'op':
            self.op(recd[1], recd[2], recd[3], recd[4])
            return True
        if kind == 'dma':
            self.dma(recd[1], recd[2], recd[3], recd[4], recd[5], recd[6])
            return True
        if kind == 'custom':
            return recd[1]() is not False
        return False

    def dma(self, q, out, in_, r=(), w=(), key=None):
        if self.rec is not None:
            self.rec.append(('dma', q, out, in_, list(r), list(w), key))
            return
        r = [k.key if isinstance(k, LazyKey) else k for k in r]
        w = [k.key if isinstance(k, LazyKey) else k for k in w]
        deps = self._deps(r, w)
        if key is None:
            key = w[0] if w else r[0]
        key = q + ':' + key
        cnt = self.dma_cnt.get(key, 0) + 1
        self.dma_cnt[key] = cnt
        self.ops[q].append(dict(fn=(lambda e, o=out, i=in_: e.dma_start(out=o, in_=i)), deps=deps, dma=key))
        self._commit(('d', key, cnt), r, w)
        self.dma_keys_written.setdefault(key, set()).update(w)

    def finalize_group(self, key):
        key = [q + ':' + key for q in ENGS if (q + ':' + key) in self.dma_cnt][0]
        tot = self.dma_cnt.get(key, 0)
        for k in self.dma_keys_written.get(key, ()):
            self.last_w[k] = ('d', key, tot)

    def emit(self):
        nc = self.nc
        need = {e: set() for e in ENGS}
        for e in ENGS:
            for op in self.ops[e]:
                for d in op['deps']:
                    if d[0] == 'c':
                        if d[1] == e and (e == 'pe' or not SAME_ENGINE_SYNC or len(d) > 3):
                            continue
                        need[d[1]].add(d[2])
        sig = {}
        for e in ENGS:
            c = 0
            m = {}
            for i, op in enumerate(self.ops[e]):
                if i in need[e]:
                    c += 1
                    m[i] = c
            sig[e] = m
        sems = {e: self.es.enter_context(nc.semaphore('s_' + e)) for e in ENGS if e != 'sp'}
        dsems = {}
        for k in self.dma_cnt:
            dsems[k] = self.es.enter_context(nc.semaphore('d%d' % len(dsems)))
        block = self.es.enter_context(nc.Block())
        prog = self

        def run(eng_name, e):
            waited = {}
            for i, op in enumerate(prog.ops[eng_name]):
                wl = {}
                for d in op['deps']:
                    if d[0] == 'c':
                        if d[1] == eng_name and (eng_name == 'pe' or not SAME_ENGINE_SYNC or len(d) > 3):
                            continue
                        tgt = ('c', d[1])
                        val = sig[d[1]][d[2]]
                    else:
                        tgt = ('d', d[1])
                        val = 16 * d[2]
                    if waited.get(tgt, 0) >= val:
                        continue
                    if wl.get(tgt, 0) < val:
                        wl[tgt] = val
                for tgt, val in wl.items():
                    s = sems[tgt[1]] if tgt[0] == 'c' else dsems[tgt[1]]
                    e.wait_ge(s, val)
                    waited[tgt] = val
                ins = op['fn'](e)
                if op['dma'] is not None:
                    ins.then_inc(dsems[op['dma']], 16)
                elif i in sig[eng_name]:
                    ins.then_inc(sems[eng_name], 1)
            if eng_name == 'sp':
                for k, c in prog.dma_cnt.items():
                    e.wait_ge(dsems[k], 16 * c)

        @block.tensor
        def _(e):
            run('pe', e)

        @block.scalar
        def _(e):
            run('act', e)

        @block.vector
        def _(e):
            run('dve', e)

        @block.gpsimd
        def _(e):
            run('pool', e)

        @block.sync
        def _(e):
            run('sp', e)


def build_program(SEQ, TP, LS=64, NS=2):
    nc = bass.Bass("TRN2", target_bir_lowering=False)
    L = DEPTH

    def din(name, shape):
        return nc.dram_tensor(name, list(shape), F32, kind="ExternalInput").ap()

    def dout(name, shape):
        return nc.dram_tensor(name, list(shape), F32, kind="ExternalOutput").ap()

    def dscr(name, shape, dt=BF16):
        return nc.dram_tensor(name, list(shape), dt).ap()

    I = {}
    I['xp'] = din('xp', [SEQ, D])
    I['xs'] = din('xs', [NS, LS, D])
    I['memp'] = din('memp', [MEM, D])
    I['cmk'] = din('cmk', [L, NS, MEM, 512])
    I['cmv'] = din('cmv', [L, NS, MEM, 512])
    I['spool'] = din('spool', [L, NS, 15, 256])
    I['sconv'] = din('sconv', [L, NS, 3, 1536])
    I['sssm'] = din('sssm', [L, NS, 4, 128, 128])
    for nm, shp in [('ffn1_norm', [L, D]), ('ffn1_w_gate', [L, D, DFF]), ('ffn1_w_up', [L, D, DFF]),
                    ('ffn1_w_down', [L, DFF, D]), ('mix_norm', [L, D]), ('w_in', [L, D, NIN]),
                    ('sgu_norm', [L, 256]), ('sgu_w', [L, 4, 128, 128]), ('sgu_b', [L, 4, 128]),
                    ('pool_w', [L, 4, 64, 64]), ('pool_scale', [L, 256]), ('gdn_conv_w', [L, 4, 1536]),
                    ('gdn_a_log', [L, 4]), ('gdn_dt_bias', [L, 4]), ('gdn_out_norm', [L, 128]),
                    ('w_out', [L, D, D]), ('xattn_norm', [L, D]), ('mem_norm', [L, D]),
                    ('w_mq', [L, D, 512]), ('w_mk', [L, D, 512]), ('w_mv', [L, D, 512]), ('w_mo', [L, 512, D]),
                    ('ffn2_norm', [L, D]), ('ffn2_w_gate', [L, D, DFF]), ('ffn2_w_up', [L, D, DFF]),
                    ('ffn2_w_down', [L, DFF, D]), ('final_norm', [D])]:
        I[nm] = din(nm, shp)
    O = {}
    O['y_p'] = dout('y_p', [SEQ, D])
    O['y_s'] = dout('y_s', [NS, LS, D])
    O['p_pool'] = dout('p_pool', [L, 15, 256])
    O['p_conv'] = dout('p_conv', [L, 3, 1536])
    O['p_ssm'] = dout('p_ssm', [L, 4, 128, 128])
    O['p_mk'] = dout('p_mk', [L, MEM, 512])
    O['p_mv'] = dout('p_mv', [L, MEM, 512])
    O['s_pool'] = dout('s_pool', [L, NS, 15, 256])
    O['s_conv'] = dout('s_conv', [L, NS, 3, 1536])
    O['s_ssm'] = dout('s_ssm', [L, NS, 4, 128, 128])
    O['s_v'] = dout('s_v', [L, NS, LS, 256])

    SCR = {}
    for nm in ['ffn1_w_gate', 'ffn1_w_up', 'ffn2_w_gate', 'ffn2_w_up', 'w_in']:
        SCR[nm] = dscr('scr_' + nm, [L, 6, 128, 8, 512])
    for nm in ['ffn1_w_down', 'ffn2_w_down']:
        SCR[nm] = dscr('scr_' + nm, [L, 8, 128, 22, 128])
    SCR['w_out'] = dscr('scr_w_out', [L, 2, 128, 8, 512])
    for nm in ['w_mq', 'w_mk', 'w_mv']:
        SCR[nm] = dscr('scr_' + nm, [L, 1, 128, 8, 512])
    SCR['w_mo'] = dscr('scr_w_mo', [L, 1, 128, 4, 1024])

    TMAX = max(TP, LS)
    with ExitStack() as es:
        P = Prog(nc, es)
        sb, ps = P.sb, P.ps
        xTs = [sb('xT%d' % i, [128, KC, TMAX], F32) for i in range(2)]
        hns = [sb('hn%d' % i, [128, KC, TMAX], BF16) for i in range(2)]
        sqb = sb('sqb', [128, KC, TMAX], BF16)
        rstd = sb('rstd', [128, TMAX], F32)
        ynk = sb('ynk', [128, TMAX], F32)
        act = sb('act', [128, FC, TMAX], BF16)
        sgt = [sb('sgt%d' % i, [128, TMAX], F32) for i in range(2)]
        NSLOT = 4
        wsl = [sb('wsl%d' % i, [128, 4096], BF16) for i in range(NSLOT)]
        xst = [sb('xst%d' % i, [128, D], F32) for i in range(2)]
        yst = xst
        uv = sb('uv', [128, 4, TMAX], F32)
        gt = sb('gt', [128, 4, TMAX], F32)
        vc = sb('vc', [128, 2, TMAX], F32)
        vn = sb('vn', [128, 2, TMAX], F32)
        vtok = sb('vtok', [128, 256], F32)
        vtokb = sb('vtokb', [128, 256], BF16)
        pbuf = sb('pbuf', [128, 2, 16 + TMAX], F32)
        pw1 = sb('pw1', [128, 2, 16 + TMAX], F32)
        pw2 = sb('pw2', [128, 2, 16 + TMAX], F32)
        dTb = sb('dTb', [128, 2, TMAX], BF16)
        cbuf = sb('cbuf', [128, 12, 3 + TMAX], F32)
        qkvs = sb('qkvs', [128, 12, TMAX], F32)
        sz = sb('sz', [128, 4, TMAX], F32)
        mix = sb('mix', [128, 8, TMAX], BF16)
        w8 = [sb('w8_%d' % l, [128, KC, 8], BF16) for l in range(L)]
        g_bg = sb('g_bg', [128, 8], F32)
        g_beta = sb('g_beta', [128, 4], F32)
        g_nbeta = sb('g_nbeta', [128, 4], F32)
        g_g = sb('g_g', [128, 4], F32)
        g_t4 = sb('g_t4', [128, 4], F32)
        g_gc = sb('g_gc', [128, 4], F32)
        g_egc = sb('g_egc', [128, 4], F32)
        g_bexp = sb('g_bexp', [128, 4], F32)
        g_ekg = sb('g_ekg', [128, 4], F32)
        g_gl = sb('g_gl', [128, 4], F32)
        g_trig = sb('g_trig', [128, 4, 128], F32)
        g_egrow = sb('g_egrow', [128, 4, 128], F32)
        g_A = sb('g_A', [128, 4, 128], F32)
        g_DT = sb('g_DT', [128, 4, 128], F32)
        g_NB = sb('g_NB', [128, 4, 128], F32)
        g_sq = g_trig[:, :, :].rearrange("p a b -> p (a b)").bitcast(BF16).rearrange("p (a b) -> p a b", a=8)
        g_rq = sb('g_rq', [128, 4, 128], F32)
        g_rk = sb('g_rk', [128, 4, 128], F32)
        g_knT = sb('g_knT', [128, 4, 128], GDT)
        g_qnT = sb('g_qnT', [128, 4, 128], GDT)
        g_qgT = sb('g_qgT', [128, 4, 128], GDT)
        g_kbg = sb('g_kbg', [128, 4, 128], GDT)
        g_kg = sb('g_kg', [128, 4, 128], GDT)
        g_vb = sb('g_vb', [128, 4, 128], GDT)
        g_u = g_NB
        g_wT = g_rk
        g_M = [sb('g_M%d' % i, [128, 4, 128], GDT) for i in range(2)]
        g_MT = [sb('g_MT%d' % i, [128, 4, 128], GDT) for i in range(2)]
        g_PT = [sb('g_PT%d' % i, [128, 4, 128], GDT) for i in range(2)]
        g_qkT = g_egrow
        g_vnew = g_rq
        g_o = g_A
        g_o2 = g_trig
        g_ss = sb('g_ss', [128, 4], F32)
        S_st = [sb('S%d' % l, [128, 4, 128], F32) for l in range(L)]
        chist = [sb('chist%d' % l, [128, 12, 3], F32) for l in range(L)]
        phist = [sb('phist%d' % l, [128, 2, 15], F32) for l in range(L)]
        mkT = [sb('mkT%d' % l, [128, 4, MEM], BF16) for l in range(L)]
        mvb = [sb('mvb%d' % l, [128, 2, 512], BF16) for l in range(L)]
        qTb = sb('qTb', [128, 4, TMAX], BF16)
        pTb = [sb('pTb%d' % i, [128, 2, TMAX], BF16) for i in range(2)]
        oTb = sb('oTb', [128, 4, TMAX], BF16)
        rsum = sb('rsum', [128, TMAX], F32)
        assert TMAX == 256
        memtok = qkvs[:, 0:8, :].rearrange("p a b -> p (a b)").rearrange("p (m f) -> p m f", m=2)
        memT = cbuf[:, 0:8, 0:MEM]
        memn = hns[0]
        kvst = gt[:, 0:2, :].rearrange("p a b -> p (a b)")
        sttok = cbuf[0:16, 0:6, :].rearrange("p a b -> p (a b)")[:, 0:1536]
        ident = sb('ident', [128, 128], F32)
        ones_b = sb('ones_b', [128, 128], BF16)
        ones_f = sb('ones_f', [128, 128], F32)
        mskSL = sb('mskSL', [128, 128], F32)
        mskUI = sb('mskUI', [128, 128], F32)
        tri = sb('tri', [128, 128], F32)
        bavg = sb('bavg', [128, 128], F32)
        rcfix = sb('rcfix', [128, 2, 16], F32)
        vtab = [sb('vtab%d' % l, [128, 128], F32) for l in range(L)]
        vcol = [sb('vcol%d' % l, [128, 128], F32) for l in range(L)]
        wsT = [sb('wsT%d' % l, [128, 4, 128], BF16) for l in range(L)]
        wstg = sb('wstg', [128, 4, 128], F32)
        sgub = [sb('sgub%d' % l, [128, 2, 128], F32) for l in range(L)]
        wpf = sb('wpf', [128, 2, 128], F32)
        wpb = [sb('wpb%d' % l, [128, 2, 128], BF16) for l in range(L)]
        dtb = [sb('dtb%d' % l, [128, 4], F32) for l in range(L)]
        negA = [sb('negA%d' % l, [128, 4], F32) for l in range(L)]
        PSB = [ps('psb%d' % i, [128, 512], F32) for i in range(8)]

        psctr = {0: 0, 1: 0}
        cur_par = [0]

        def psum(group):
            p = cur_par[0]
            i = 4 * p + psctr[p] % 4
            psctr[p] += 1
            return PSB[i], 'ps%d' % i

        VC_FFN1, VC_MIX, VC_XATT, VC_MEM, VC_FFN2 = 0, 8, 16, 24, 32
        VC_SGU, VC_PSC, VC_CONV, VC_ONORM, VC_FINAL = 40, 42, 44, 92, 93

        def cst(eng, fn, w):
            P.op(eng, fn, r=(), w=w)

        cst('pool', lambda e: e.memset(ident[:], 0.0), ['ident'])
        cst('pool', lambda e: e.affine_select(out=ident[:], in_=ident[:], pattern=[[-1, 128]], compare_op=ALU.not_equal,
                                               fill=1.0, base=0, channel_multiplier=1), ['ident'])
        cst('pool', lambda e: e.memset(ones_b[:], 1.0), ['ones_b'])
        cst('pool', lambda e: e.memset(ones_f[:], 1.0), ['ones_f'])
        cst('pool', lambda e: e.memset(mskSL[:], 1.0), ['mskSL'])
        cst('pool', lambda e: e.affine_select(out=mskSL[:], in_=mskSL[:], pattern=[[-1, 128]], compare_op=ALU.is_gt,
                                               fill=0.0, base=0, channel_multiplier=1), ['mskSL'])
        cst('pool', lambda e: e.memset(tri[:], 1.0), ['tri'])
        cst('pool', lambda e: e.affine_select(out=tri[:], in_=tri[:], pattern=[[1, 128]], compare_op=ALU.is_ge,
                                               fill=0.0, base=0, channel_multiplier=-1), ['tri'])
        cst('pool', lambda e: e.memset(bavg[:], 0.0), ['bavg'])
        cst('pool', lambda e: e.memset(bavg[0:64, 0:64], 1.0 / 64), ['bavg'])
        cst('pool', lambda e: e.memset(bavg[64:128, 64:128], 1.0 / 64), ['bavg'])
        for c in range(2):
            cst('pool', lambda e, c=c: e.iota(rcfix[:, c, :], pattern=[[1, 16]], base=1, channel_multiplier=0,
                                              allow_small_or_imprecise_dtypes=True), ['rcfix'])
        for c in range(2):
            for hf in range(2):
                wdw = float([2, 4, 8, 16][2 * c + hf])
                cst('pool', lambda e, c=c, hf=hf, wdw=wdw: e.tensor_scalar_min(
                    out=rcfix[64 * hf:64 * hf + 64, c, :], in0=rcfix[64 * hf:64 * hf + 64, c, :], scalar1=wdw), ['rcfix'])
        P.op('dve', lambda e: e.reciprocal(out=rcfix[:], in_=rcfix[:]), r=['rcfix'], w=['rcfix'])
        for l in range(L):
            cst('pool', lambda e, l=l: e.memset(vtab[l][:], 0.0), ['vtab%d' % l])

        cst('pool', lambda e: e.memset(pbuf[:], 0.0), ['pbuf'])
        cst('pool', lambda e: e.memset(pw1[:], 0.0), ['pw1'])
        cst('pool', lambda e: e.memset(pw2[:], 0.0), ['pw2'])
        cst('pool', lambda e: e.memset(cbuf[:], 0.0), ['cbuf'])
        def sload(out, in_, w):
            P.dma('sp', out, in_, r=(), w=w, key='setup')

        for l in range(L):
            vk = 'vtab%d' % l
            for nm, r0 in [('ffn1_norm', VC_FFN1), ('mix_norm', VC_MIX), ('xattn_norm', VC_XATT),
                           ('mem_norm', VC_MEM), ('ffn2_norm', VC_FFN2)]:
                sload(vtab[l][r0:r0 + 8, :], I[nm][l].rearrange("(k p) -> k p", p=128), [vk])
            sload(vtab[l][VC_SGU:VC_SGU + 2, :], I['sgu_norm'][l].rearrange("(k p) -> k p", p=128), [vk])
            sload(vtab[l][VC_PSC:VC_PSC + 2, :], I['pool_scale'][l].rearrange("(k p) -> k p", p=128), [vk])
            sload(vtab[l][VC_CONV:VC_CONV + 48, :], I['gdn_conv_w'][l].rearrange("j (k p) -> (j k) p", p=128), [vk])
            sload(vtab[l][VC_ONORM:VC_ONORM + 1, :], I['gdn_out_norm'][l].rearrange("(k p) -> k p", p=128), [vk])
            if l == 0:
                sload(vtab[l][VC_FINAL:VC_FINAL + 8, :], I['final_norm'].rearrange("(k p) -> k p", p=128), [vk])
            sload(dtb[l][:], I['gdn_dt_bias'][l].partition_broadcast(128), ['dtb%d' % l])
            sload(negA[l][:], I['gdn_a_log'][l].partition_broadcast(128), ['negA%d' % l])
            for h in range(4):
                sload(sgub[l][64 * (h % 2):64 * (h % 2) + 64, h // 2, :], I['sgu_b'][l, h].partition_broadcast(64), ['sgub%d' % l])
        P.finalize_group('setup')

        for l in range(L):
            P.dma('pool', w8[l][:], I['w_in'][l].rearrange("(k p) n -> p k n", p=128)[:, :, 2816:2824], r=(), w=['w8_%d' % l], key='w8')
        P.finalize_group('w8')

        def cast_k1024(nm, l, ncols):
            src = I[nm][l].rearrange("(k p) n -> p k n", p=128)
            npan = (ncols + 511) // 512
            for j in range(npan):
                nw = min(512, ncols - j * 512)
                P.dma('pool', SCR[nm][l, j][:, :, 0:nw], src[:, :, j * 512:j * 512 + nw], r=(), w=['scr_%s_%d' % (nm, l)],
                      key='scr_%s_%d' % (nm, l))

        def cast_down(nm, l):
            src = I[nm][l].rearrange("(k p) n -> p k n", p=128)
            for m in range(8):
                for hf in range(2):
                    P.dma('pool', SCR[nm][l, m][:, 11 * hf:11 * hf + 11, :], src[:, 11 * hf:11 * hf + 11, m * 128:(m + 1) * 128],
                          r=(), w=['scr_%s_%d' % (nm, l)], key='scr_%s_%d' % (nm, l))

        def cast_mo(l):
            src = I['w_mo'][l].rearrange("(k p) n -> p k n", p=128)
            P.dma('pool', SCR['w_mo'][l, 0], src, r=(), w=['scr_w_mo_%d' % l], key='scr_w_mo_%d' % l)

        for l in range(L if STAGE >= 1 else 0):
            cast_k1024('ffn1_w_gate', l, DFF)
            cast_k1024('ffn1_w_up', l, DFF)
            cast_down('ffn1_w_down', l)
            cast_k1024('w_in', l, DFF)
            cast_k1024('w_mk', l, 512)
            cast_k1024('w_mv', l, 512)
            cast_k1024('w_out', l, D)
            cast_k1024('w_mq', l, 512)
            cast_mo(l)
            cast_k1024('ffn2_w_gate', l, DFF)
            cast_k1024('ffn2_w_up', l, DFF)
            cast_down('ffn2_w_down', l)
        for l in range(L if STAGE >= 1 else 0):
            for nm in ['ffn1_w_gate', 'ffn1_w_up', 'ffn1_w_down', 'w_in', 'w_mk', 'w_mv', 'w_out', 'w_mq', 'w_mo',
                       'ffn2_w_gate', 'ffn2_w_up', 'ffn2_w_down']:
                P.finalize_group('scr_%s_%d' % (nm, l))

        wctr = [0]

        def wload(nm, l, j, shape):
            h = Holder()

            def assign():
                for tries in range(NSLOT):
                    i = (wctr[0] + tries) % NSLOT
                    if not P.slot_live.get(i, False):
                        break
                else:
                    return False
                wctr[0] = i + 1
                h.slot = i
                if h.refs > 0:
                    P.slot_live[i] = True
                k, n = shape
                h.view = wsl[i][:, 0:k * n].rearrange("p (k n) -> p k n", k=k)
                h.key = 'wsl%d' % i
                P.dma('sp', h.view, SCR[nm][l, j][:, 0:k, 0:n], r=['scr_%s_%d' % (nm, l)], w=['wsl%d' % i])
            P.custom(assign)
            return LazyView(h), LazyKey(h)

        for l in range(L):
            pt, pk = psum('misc')
            P.op('pe', lambda e, l=l, pt=pt: e.transpose(out=pt[:, 0:128], in_=vtab[l][:, :], identity=ident[:, :]),
                 r=['vtab%d' % l, 'ident'], w=[pk])
            P.op('dve', lambda e, l=l, pt=pt: e.tensor_copy(out=vcol[l][:], in_=pt[:, 0:128]), r=[pk], w=['vcol%d' % l])
            P.op('act', lambda e, l=l: e.activation(out=negA[l][:], in_=negA[l][:], func=AF.Exp), r=['negA%d' % l], w=['negA%d' % l])
            P.op('dve', lambda e, l=l: e.tensor_scalar(out=negA[l][:], in0=negA[l][:], scalar1=-1.0, scalar2=None, op0=ALU.mult),
                 r=['negA%d' % l], w=['negA%d' % l])
            P.dma('sp', wstg[:], I['sgu_w'][l].rearrange("h i j -> i h j"), r=(), w=['wstg'])
            pt, pk = psum('misc')
            for h in range(4):
                P.op('pe', lambda e, h=h, pt=pt: e.transpose(out=pt[:, h * 128:(h + 1) * 128], in_=wstg[:, h, :], identity=ident[:, :]),
                     r=['wstg', 'ident'], w=[pk])
            P.op('dve', lambda e, l=l, pt=pt: e.tensor_copy(out=wsT[l][:], in_=pt[:, :].rearrange("p (h i) -> p h i", h=4)),
                 r=[pk], w=['wsT%d' % l])
            P.op('dve', lambda e, l=l: e.memset(wsT[l][64:128, :, 0:64], 0.0), r=(), w=['wsT%d' % l])
            P.op('dve', lambda e: e.memset(wpf[:], 0.0), r=(), w=['wpf'])
            for g in range(4):
                P.dma('sp', wpf[64 * (g % 2):64 * (g % 2) + 64, g // 2, 64 * (g % 2):64 * (g % 2) + 64], I['pool_w'][l, g],
                      r=(), w=['wpf'])
            P.op('dve', lambda e, l=l: e.tensor_copy(out=wpb[l][:], in_=wpf[:]), r=['wpf'], w=['wpb%d' % l])

        def rmsnorm(T, gl, gcol0, out, outkey, par=0):
            xT = xTs[par]

            def xk(k):
                return 'xT%d_%d' % (par, k)
            P.mark('AB')
            P.op('act', lambda e: e.activation(out=sqb[:, :, 0:T], in_=xT[:, :, 0:T], func=AF.Square),
                 r=[xk(k) for k in range(KC)], w=['sqb'])
            pt, pk = psum('misc')
            for k in range(KC):
                P.op('pe', lambda e, k=k, pt=pt: e.matmul(pt[:, 0:T], lhsT=ones_b[:, :], rhs=sqb[:, k, 0:T], start=(k == 0), stop=(k == KC - 1)),
                     r=['sqb', 'ones_b'], w=[pk])
            P.op('act', lambda e, pt=pt: e.activation(out=rstd[:, 0:T], in_=pt[:, 0:T], func=AF.Ln, bias=EPS, scale=1.0 / D),
                 r=[pk], w=['rstd'])
            P.op('act', lambda e: e.activation(out=rstd[:, 0:T], in_=rstd[:, 0:T], func=AF.Exp, scale=-0.5), r=['rstd'], w=['rstd'])
            for k in range(KC):
                P.op('dve', lambda e, k=k: e.scalar_tensor_tensor(out=out[:, k, 0:T], in0=xT[:, k, 0:T],
                                                                     scalar=vcol[gl][:, gcol0 + k:gcol0 + k + 1], in1=rstd[:, 0:T],
                                                                     op0=ALU.mult, op1=ALU.mult),
                     r=[xk(k), 'rstd', 'vcol%d' % gl], w=[outkey])
            P.mark('AE')

        def ffn(l, which, T, par=0):
            xT = xTs[par]
            hn = hns[par]
            hk = 'hn%d' % par

            def xk(k):
                return 'xT%d_%d' % (par, k)
            pre = 'ffn%d_' % which
            rmsnorm(T, l, VC_FFN1 if which == 1 else VC_FFN2, hn, hk, par)
            si = 0
            for j in range(6):
                nch = 4 if j < 5 else 2
                wg, wgk = wload(pre + 'w_gate', l, j, (8, nch * 128))
                wu, wuk = wload(pre + 'w_up', l, j, (8, nch * 128))
                for c in range(nch):
                    n = j * 4 + c
                    if FFN_SHARED:
                        pgu, pgk = psum('big')
                        puk = pgk
                        pg = pgu[:, 0:256]
                        pu = pgu[:, 256:512]
                    else:
                        pg, pgk = psum('big')
                        pu, puk = psum('big')
                    for k in range(KC):
                        P.op('pe', lambda e, k=k, c=c, pg=pg, wg=wg: e.matmul(pg[:, 0:T], lhsT=wg[:, k, c * 128:(c + 1) * 128], rhs=hn[:, k, 0:T],
                                                                             start=(k == 0), stop=(k == KC - 1)), r=[hk, wgk], w=[pgk])
                    for k in range(KC):
                        P.op('pe', lambda e, k=k, c=c, pu=pu, wu=wu: e.matmul(pu[:, 0:T], lhsT=wu[:, k, c * 128:(c + 1) * 128], rhs=hn[:, k, 0:T],
                                                                             start=(k == 0), stop=(k == KC - 1)), r=[hk, wuk], w=[puk])
                    st_ = sgt[si % 2]
                    sk = 'sgt%d' % (si % 2)
                    si += 1
                    if FFN_EXPLN:
                        P.op('act', lambda e, pg=pg, st_=st_: e.activation(out=st_[:, 0:T], in_=pg[:, 0:T], func=AF.Exp, scale=-1.0), r=[pgk], w=[sk])
                        P.op('act', lambda e, st_=st_: e.activation(out=st_[:, 0:T], in_=st_[:, 0:T], func=AF.Ln, bias=1.0), r=[sk], w=[sk])
                        P.op('act', lambda e, st_=st_: e.activation(out=st_[:, 0:T], in_=st_[:, 0:T], func=AF.Exp, scale=-1.0), r=[sk], w=[sk])
                        P.op('dve', lambda e, pg=pg, st_=st_: e.tensor_tensor(out=st_[:, 0:T], in0=pg[:, 0:T], in1=st_[:, 0:T], op=ALU.mult), r=[pgk, sk], w=[sk])
                    else:
                        P.op('act', lambda e, pg=pg, st_=st_: e.activation(out=st_[:, 0:T], in_=pg[:, 0:T], func=AF.Silu), r=[pgk], w=[sk])
                    P.op('dve', lambda e, n=n, pu=pu, st_=st_: e.tensor_tensor(out=act[:, n, 0:T], in0=pu[:, 0:T], in1=st_[:, 0:T], op=ALU.mult),
                         r=[puk, sk], w=['act%d' % n])
            for m in range(KC):
                wd, wdk = wload(pre + 'w_down', l, m, (22, 128))
                pd, pdk = psum('big')
                for k in range(FC):
                    P.op('pe', lambda e, k=k, pd=pd, wd=wd: e.matmul(pd[:, 0:T], lhsT=wd[:, k, :], rhs=act[:, k, 0:T],
                                                                   start=(k == 0), stop=(k == FC - 1)), r=['act%d' % k, wdk], w=[pdk])
                P.op('dve', lambda e, m=m, pd=pd: e.scalar_tensor_tensor(out=xT[:, m, 0:T], in0=pd[:, 0:T], scalar=0.5, in1=xT[:, m, 0:T],
                                                                       op0=ALU.mult, op1=ALU.add), r=[pdk, xk(m)], w=[xk(m)])

        def proj_residual(l, nm, kin, src, srckeys, T, npan, par=0):
            xT = xTs[par]

            def xk(k):
                return 'xT%d_%d' % (par, k)
            for j in range(npan):
                if nm == 'w_mo':
                    wv, wk = wload(nm, l, 0, (4, 1024)) if j == 0 else (wv, wk)
                else:
                    wv, wk = wload(nm, l, j, (8, 512))
                for c in range(4):
                    m = j * 4 + c
                    pd, pdk = psum('big')
                    kl = list(range(kin))
                    if nm == 'w_out' and DBG_MIX is not None:
                        kl = [k for k in kl if {0: 'a', 1: 'a', 2: 'b', 3: 'b'}.get(k, 'c') in DBG_MIX]
                    for k in kl:
                        P.op('pe', lambda e, k=k, m=m, c=c, pd=pd, wv=wv: e.matmul(
                            pd[:, 0:T], lhsT=(wv[:, k, m * 128:(m + 1) * 128] if nm == 'w_mo' else wv[:, k, c * 128:(c + 1) * 128]),
                            rhs=src[:, k, 0:T], start=(k == kl[0]), stop=(k == kl[-1])), r=list(srckeys) + [wk], w=[pdk])
                    P.op('dve', lambda e, m=m, pd=pd: e.tensor_tensor(out=xT[:, m, 0:T], in0=pd[:, 0:T], in1=xT[:, m, 0:T], op=ALU.add),
                         r=[pdk, xk(m)], w=[xk(m)])

        def gdn_chunk(l, c0, C, par=0):
            hn = hns[par]
            hk = 'hn%d' % par
            J = 6 if C == 128 else 5
            C4 = 4 * C
            P.op('act', lambda e: e.activation(out=g_sq[:, :, 0:C], in_=qkvs[:, 0:8, c0:c0 + C], func=AF.Square), r=['qkvs'], w=['g_trig'])
            pq, pqk = psum('misc')
            pqv = pq[:, 0:C4].rearrange("p (h c) -> p h c", h=4)
            P.op('pe', lambda e, pqv=pqv: e.matmul(pqv, lhsT=ones_b[:, :], rhs=g_sq[:, 0:4, 0:C], start=True, stop=True),
                 r=['ones_b', 'g_trig'], w=[pqk])
            pkk_, pkkk = psum('misc')
            pkv = pkk_[:, 0:C4].rearrange("p (h c) -> p h c", h=4)
            P.op('pe', lambda e, pkv=pkv: e.matmul(pkv, lhsT=ones_b[:, :], rhs=g_sq[:, 4:8, 0:C], start=True, stop=True),
                 r=['ones_b', 'g_trig'], w=[pkkk])
            P.op('act', lambda e, pqv=pqv: e.activation(out=g_rq[:, :, 0:C], in_=pqv, func=AF.Ln, bias=1e-6), r=[pqk], w=['g_rq'])
            P.op('act', lambda e, pkv=pkv: e.activation(out=g_rk[:, :, 0:C], in_=pkv, func=AF.Ln, bias=1e-6), r=[pkkk], w=['g_rk'])
            P.op('act', lambda e: e.activation(out=g_rq[:, :, 0:C], in_=g_rq[:, :, 0:C], func=AF.Exp, scale=-0.5), r=['g_rq'], w=['g_rq'])
            P.op('act', lambda e: e.activation(out=g_rk[:, :, 0:C], in_=g_rk[:, :, 0:C], func=AF.Exp, scale=-0.5), r=['g_rk'], w=['g_rk'])
            P.op('dve', lambda e: e.scalar_tensor_tensor(out=g_qnT[:, :, 0:C], in0=qkvs[:, 0:4, c0:c0 + C], scalar=float(128 ** -0.5),
                                                          in1=g_rq[:, :, 0:C], op0=ALU.mult, op1=ALU.mult), r=['qkvs', 'g_rq'], w=['g_qnT'])
            P.op('dve', lambda e: e.tensor_tensor(out=g_knT[:, :, 0:C], in0=qkvs[:, 4:8, c0:c0 + C], in1=g_rk[:, :, 0:C], op=ALU.mult),
                 r=['qkvs', 'g_rk'], w=['g_knT'])
            pt, pk = psum('misc')
            for k in range(KC):
                P.op('pe', lambda e, k=k, pt=pt: e.matmul(pt[0:C, 0:8], lhsT=hn[:, k, c0:c0 + C], rhs=w8[l][:, k, :],
                                                         start=(k == 0), stop=(k == KC - 1)), r=[hk, 'w8_%d' % l], w=[pk])
            P.op('act', lambda e, pt=pt: e.activation(out=g_beta[0:C, :], in_=pt[0:C, 0:4], func=AF.Exp, scale=-1.0), r=[pk], w=['g_beta'])
            P.op('act', lambda e: e.activation(out=g_beta[0:C, :], in_=g_beta[0:C, :], func=AF.Ln, bias=1.0), r=['g_beta'], w=['g_beta'])
            P.op('act', lambda e: e.activation(out=g_beta[0:C, :], in_=g_beta[0:C, :], func=AF.Exp, scale=-1.0), r=['g_beta'], w=['g_beta'])
            P.op('dve', lambda e, pt=pt: e.tensor_tensor(out=g_t4[0:C, :], in0=pt[0:C, 4:8], in1=dtb[l][0:C, :], op=ALU.add),
                 r=[pk, 'dtb%d' % l], w=['g_t4'])
            P.op('act', lambda e: e.activation(out=g_t4[0:C, :], in_=g_t4[0:C, :], func=AF.Exp), r=['g_t4'], w=['g_t4'])
            P.op('act', lambda e: e.activation(out=g_t4[0:C, :], in_=g_t4[0:C, :], func=AF.Ln, bias=1.0), r=['g_t4'], w=['g_t4'])
            P.op('dve', lambda e: e.tensor_tensor(out=g_g[0:C, :], in0=g_t4[0:C, :], in1=negA[l][0:C, :], op=ALU.mult),
                 r=['g_t4', 'negA%d' % l], w=['g_g'])
            P.op('dve', lambda e: e.tensor_scalar(out=g_nbeta[0:C, :], in0=g_beta[0:C, :], scalar1=-1.0, scalar2=None, op0=ALU.mult),
                 r=['g_beta'], w=['g_nbeta'])
            pgc, pgck = psum('misc')
            P.op('pe', lambda e, pgc=pgc: e.matmul(pgc[0:C, 0:4], lhsT=tri[0:C, 0:C], rhs=g_g[0:C, :], start=True, stop=True),
                 r=['tri', 'g_g'], w=[pgck])
            P.op('act', lambda e, pgc=pgc: e.copy(out=g_gc[0:C, :], in_=pgc[0:C, 0:4]), r=[pgck], w=['g_gc'])
            P.op('act', lambda e, pgc=pgc: e.activation(out=g_egc[0:C, :], in_=pgc[0:C, 0:4], func=AF.Exp), r=[pgck], w=['g_egc'])
            for h in range(4):
                P.op('dve', lambda e, h=h: e.tensor_scalar(out=g_trig[0:C, h, 0:C], in0=tri[0:C, 0:C], scalar1=g_g[0:C, h:h + 1], scalar2=None,
                                                          op0=ALU.mult), r=['tri', 'g_g'], w=['g_trig'])
            prow, prowk = psum('misc')
            prv = prow[:, 0:C4].rearrange("p (h c) -> p h c", h=4)
            P.op('pe', lambda e, prv=prv: e.matmul(prv, lhsT=ones_f[0:C, :], rhs=g_trig[0:C, :, 0:C], start=True, stop=True),
                 r=['ones_f', 'g_trig'], w=[prowk])
            P.op('act', lambda e, prv=prv: e.activation(out=g_egrow[:, :, 0:C], in_=prv, func=AF.Exp), r=[prowk], w=['g_egrow'])
            P.op('act', lambda e, prv=prv: e.activation(out=g_gl[:, :], in_=prv[:, :, C - 1], func=AF.Exp), r=[prowk], w=['g_gl'])
            P.op('dve', lambda e, prv=prv: e.tensor_tensor(out=g_ekg[0:C, :], in0=prv[0:C, :, C - 1], in1=g_gc[0:C, :], op=ALU.subtract),
                 r=[prowk, 'g_gc'], w=['g_ekg'])
            P.op('act', lambda e: e.activation(out=g_ekg[0:C, :], in_=g_ekg[0:C, :], func=AF.Exp), r=['g_ekg'], w=['g_ekg'])
            P.op('dve', lambda e: e.tensor_tensor(out=g_bexp[0:C, :], in0=g_beta[0:C, :], in1=g_egc[0:C, :], op=ALU.mult),
                 r=['g_beta', 'g_egc'], w=['g_bexp'])
            P.op('dve', lambda e, prv=prv: e.tensor_tensor(out=g_A[0:C, :, 0:C], in0=prv[0:C, :, :],
                                                          in1=g_gc[0:C, :].unsqueeze(2).to_broadcast([C, 4, C]), op=ALU.subtract),
                 r=[prowk, 'g_gc'], w=['g_A'])
            P.op('dve', lambda e: e.tensor_scalar(out=g_DT[0:C, :, 0:C], in0=g_A[0:C, :, 0:C], scalar1=0.0, scalar2=None, op0=ALU.min),
                 r=['g_A'], w=['g_DT'])
            P.op('act', lambda e: e.activation(out=g_DT[0:C, :, 0:C], in_=g_DT[0:C, :, 0:C], func=AF.Exp), r=['g_DT'], w=['g_DT'])
            P.op('dve', lambda e: e.tensor_tensor(out=g_DT[0:C, :, 0:C], in0=g_DT[0:C, :, 0:C],
                                                  in1=tri[0:C, 0:C].unsqueeze(1).to_broadcast([C, 4, C]), op=ALU.mult),
                 r=['g_DT', 'tri'], w=['g_DT'])
            P.op('dve', lambda e: e.tensor_scalar(out=g_NB[0:C, :, 0:C], in0=g_A[0:C, :, 0:C], scalar1=0.0, scalar2=None, op0=ALU.max),
                 r=['g_A'], w=['g_NB'])
            P.op('act', lambda e: e.activation(out=g_NB[0:C, :, 0:C], in_=g_NB[0:C, :, 0:C], func=AF.Exp, scale=-1.0), r=['g_NB'], w=['g_NB'])
            P.op('dve', lambda e: e.tensor_tensor(out=g_NB[0:C, :, 0:C], in0=g_NB[0:C, :, 0:C],
                                                  in1=mskSL[0:C, 0:C].unsqueeze(1).to_broadcast([C, 4, C]), op=ALU.mult),
                 r=['g_NB', 'mskSL'], w=['g_NB'])
            P.op('dve', lambda e: e.tensor_tensor(out=g_NB[0:C, :, 0:C], in0=g_NB[0:C, :, 0:C],
                                                  in1=g_nbeta[0:C, :].unsqueeze(2).to_broadcast([C, 4, C]), op=ALU.mult),
                 r=['g_NB', 'g_nbeta'], w=['g_NB'])
            P.op('dve', lambda e: e.tensor_tensor(out=g_qgT[:, :, 0:C], in0=g_qnT[:, :, 0:C], in1=g_egrow[:, :, 0:C], op=ALU.mult),
                 r=['g_qnT', 'g_egrow'], w=['g_qgT'])
            ptk, ptkk = psum('misc')
            for h in range(4):
                P.op('pe', lambda e, h=h, ptk=ptk: e.transpose(out=ptk[0:C, h * 128:(h + 1) * 128], in_=g_knT[:, h, 0:C], identity=ident[:, :]),
                     r=['g_knT', 'ident'], w=[ptkk])
            ptkv = ptk[0:C, :].rearrange("p (h d) -> p h d", h=4)
            P.op('dve', lambda e, ptkv=ptkv: e.tensor_tensor(out=g_kbg[0:C, :, :], in0=ptkv, in1=g_bexp[0:C, :].unsqueeze(2).to_broadcast([C, 4, 128]),
                                                            op=ALU.mult), r=[ptkk, 'g_bexp'], w=['g_kbg'])
            P.op('dve', lambda e, ptkv=ptkv: e.tensor_tensor(out=g_kg[0:C, :, :], in0=ptkv, in1=g_ekg[0:C, :].unsqueeze(2).to_broadcast([C, 4, 128]),
                                                            op=ALU.mult), r=[ptkk, 'g_ekg'], w=['g_kg'])
            ptv, ptvk = psum('misc')
            for h in range(4):
                P.op('pe', lambda e, h=h, ptv=ptv: e.transpose(out=ptv[0:C, h * 128:(h + 1) * 128], in_=qkvs[:, 8 + h, c0:c0 + C], identity=ident[:, :]),
                     r=['qkvs', 'ident'], w=[ptvk])
            ptvv = ptv[0:C, :].rearrange("p (h d) -> p h d", h=4)
            P.op('dve', lambda e, ptvv=ptvv: e.tensor_tensor(out=g_vb[0:C, :, :], in0=ptvv, in1=g_beta[0:C, :].unsqueeze(2).to_broadcast([C, 4, 128]),
                                                            op=ALU.mult), r=[ptvk, 'g_beta'], w=['g_vb'])
            pk2, pk2k = psum('misc')
            for h in range(4):
                P.op('pe', lambda e, h=h, pk2=pk2: e.matmul(pk2[0:C, h * C:(h + 1) * C], lhsT=g_knT[:, h, 0:C], rhs=g_knT[:, h, 0:C], start=True, stop=True),
                     r=['g_knT'], w=[pk2k])
            pk2v = pk2[0:C, 0:C4].rearrange("p (h c) -> p h c", h=4)
            P.op('dve', lambda e, pk2v=pk2v: e.tensor_tensor(out=g_M[0][0:C, :, 0:C], in0=pk2v, in1=g_NB[0:C, :, 0:C], op=ALU.mult),
                 r=[pk2k, 'g_NB'], w=['g_M0'])
            pq2, pq2k = psum('misc')
            for h in range(4):
                P.op('pe', lambda e, h=h, pq2=pq2: e.matmul(pq2[0:C, h * C:(h + 1) * C], lhsT=g_knT[:, h, 0:C], rhs=g_qnT[:, h, 0:C], start=True, stop=True),
                     r=['g_knT', 'g_qnT'], w=[pq2k])
            pq2v = pq2[0:C, 0:C4].rearrange("p (h c) -> p h c", h=4)
            P.op('dve', lambda e, pq2v=pq2v: e.tensor_tensor(out=g_qkT[0:C, :, 0:C], in0=pq2v, in1=g_DT[0:C, :, 0:C], op=ALU.mult),
                 r=[pq2k, 'g_DT'], w=['g_egrow'])
            pnt, pntk = psum('misc')
            for h in range(4):
                P.op('pe', lambda e, h=h, pnt=pnt: e.transpose(out=pnt[0:C, h * C:(h + 1) * C], in_=g_M[0][0:C, h, 0:C], identity=ident[0:C, 0:C]),
                     r=['g_M0', 'ident'], w=[pntk])
            pntv = pnt[0:C, 0:C4].rearrange("p (h c) -> p h c", h=4)
            P.op('act', lambda e, pntv=pntv: e.copy(out=g_MT[0][0:C, :, 0:C], in_=pntv), r=[pntk], w=['g_MT0'])
            P.op('dve', lambda e, pntv=pntv: e.tensor_tensor(out=g_PT[0][0:C, :, 0:C], in0=pntv,
                                                            in1=ident[0:C, 0:C].unsqueeze(1).to_broadcast([C, 4, C]), op=ALU.add),
                 r=[pntk, 'ident'], w=['g_PT0'])
            cur = 0
            for j in range(1, J + 1):
                nxt = 1 - cur
                last = (j == J)
                pm, pmk = psum('misc')
                for h in range(4):
                    P.op('pe', lambda e, h=h, pm=pm, cur=cur: e.matmul(pm[0:C, h * C:(h + 1) * C], lhsT=g_MT[cur][0:C, h, 0:C], rhs=g_M[cur][0:C, h, 0:C],
                                                                      start=True, stop=True), r=['g_MT%d' % cur, 'g_M%d' % cur], w=[pmk])
                pmv = pm[0:C, 0:C4].rearrange("p (h c) -> p h c", h=4)
                if not last:
                    pmt, pmtk = psum('misc')
                    for h in range(4):
                        P.op('pe', lambda e, h=h, pmt=pmt, cur=cur: e.matmul(pmt[0:C, h * C:(h + 1) * C], lhsT=g_M[cur][0:C, h, 0:C], rhs=g_MT[cur][0:C, h, 0:C],
                                                                            start=True, stop=True), r=['g_MT%d' % cur, 'g_M%d' % cur], w=[pmtk])
                    pmtv = pmt[0:C, 0:C4].rearrange("p (h c) -> p h c", h=4)
                P.op('dve', lambda e, pmv=pmv, nxt=nxt: e.tensor_copy(out=g_M[nxt][0:C, :, 0:C], in_=pmv), r=[pmk], w=['g_M%d' % nxt])
                if not last:
                    P.op('act', lambda e, pmtv=pmtv, nxt=nxt: e.copy(out=g_MT[nxt][0:C, :, 0:C], in_=pmtv), r=[pmtk], w=['g_MT%d' % nxt])
                pp, ppk = psum('misc')
                for h in range(4):
                    P.op('pe', lambda e, h=h, pp=pp, cur=cur, nxt=nxt: e.matmul(pp[0:C, h * C:(h + 1) * C], lhsT=g_M[nxt][0:C, h, 0:C], rhs=g_PT[cur][0:C, h, 0:C],
                                                                               start=True, stop=True), r=['g_M%d' % nxt, 'g_PT%d' % cur], w=[ppk])
                ppv = pp[0:C, 0:C4].rearrange("p (h c) -> p h c", h=4)
                P.op('dve', lambda e, ppv=ppv, cur=cur, nxt=nxt: e.tensor_tensor(out=g_PT[nxt][0:C, :, 0:C], in0=ppv, in1=g_PT[cur][0:C, :, 0:C], op=ALU.add),
                     r=[ppk, 'g_PT%d' % cur], w=['g_PT%d' % nxt])
                cur = nxt
            TT = g_PT[cur]
            TTk = 'g_PT%d' % cur
            pu, puk = psum('misc')
            for h in range(4):
                P.op('pe', lambda e, h=h, pu=pu: e.matmul(pu[0:C, h * 128:(h + 1) * 128], lhsT=TT[0:C, h, 0:C], rhs=g_vb[0:C, h, :], start=True, stop=True),
                     r=[TTk, 'g_vb'], w=[puk])
            P.op('act', lambda e, pu=pu: e.copy(out=g_u[0:C, :, :], in_=pu[0:C, :].rearrange("p (h d) -> p h d", h=4)), r=[puk], w=['g_NB'])
            pw, pwk = psum('misc')
            for h in range(4):
                P.op('pe', lambda e, h=h, pw=pw: e.matmul(pw[:, h * C:(h + 1) * C], lhsT=g_kbg[0:C, h, :], rhs=TT[0:C, h, 0:C], start=True, stop=True),
                     r=[TTk, 'g_kbg'], w=[pwk])
            P.op('dve', lambda e, pw=pw: e.tensor_copy(out=g_wT[:, :, 0:C], in_=pw[:, 0:C4].rearrange("p (h c) -> p h c", h=4)), r=[pwk], w=['g_rk'])
            S = S_st[l]
            Sk = 'S%d' % l
            pws, pwsk = psum('misc')
            for h in range(4):
                P.op('pe', lambda e, h=h, pws=pws: e.matmul(pws[0:C, h * 128:(h + 1) * 128], lhsT=g_wT[:, h, 0:C], rhs=S[:, h, :], start=True, stop=True),
                     r=['g_rk', Sk], w=[pwsk])
            P.op('dve', lambda e, pws=pws: e.tensor_tensor(out=g_vnew[0:C, :, :], in0=g_u[0:C, :, :], in1=pws[0:C, :].rearrange("p (h d) -> p h d", h=4),
                                                          op=ALU.subtract), r=[pwsk, 'g_NB'], w=['g_rq'])
            po, pok = psum('misc')
            for h in range(4):
                P.op('pe', lambda e, h=h, po=po: e.matmul(po[0:C, h * 128:(h + 1) * 128], lhsT=g_qgT[:, h, 0:C], rhs=S[:, h, :], start=True, stop=False),
                     r=['g_qgT', Sk], w=[pok])
                P.op('pe', lambda e, h=h, po=po: e.matmul(po[0:C, h * 128:(h + 1) * 128], lhsT=g_qkT[0:C, h, 0:C], rhs=g_vnew[0:C, h, :], start=False, stop=True),
                     r=['g_egrow', 'g_rq'], w=[pok])
            psn, psnk = psum('misc')
            for h in range(4):
                P.op('pe', lambda e, h=h, psn=psn: e.matmul(psn[:, h * 128:(h + 1) * 128], lhsT=g_kg[0:C, h, :], rhs=g_vnew[0:C, h, :], start=True, stop=True),
                     r=['g_kg', 'g_rq'], w=[psnk])
            P.op('dve', lambda e: e.tensor_tensor(out=S[:, :, :], in0=S[:, :, :], in1=g_gl[:, :].unsqueeze(2).to_broadcast([128, 4, 128]), op=ALU.mult),
                 r=[Sk, 'g_gl'], w=[Sk])
            P.op('dve', lambda e, psn=psn: e.tensor_tensor(out=S[:, :, :], in0=psn[:, :].rearrange("p (h d) -> p h d", h=4), in1=S[:, :, :], op=ALU.add),
                 r=[Sk, psnk], w=[Sk])
            pov = po[0:C, :].rearrange("p (h d) -> p h d", h=4)
            P.op('act', lambda e, pov=pov: e.copy(out=g_o[0:C, :, :], in_=pov), r=[pok], w=['g_A'])
            P.op('dve', lambda e: e.tensor_tensor(out=g_o2[0:C, :, :], in0=g_o[0:C, :, :], in1=g_o[0:C, :, :], op=ALU.mult), r=['g_A'], w=['g_trig'])
            P.op('dve', lambda e: e.reduce_sum(out=g_ss[0:C, :], in_=g_o2[0:C, :, :], axis=AX.X), r=['g_trig'], w=['g_ss'])
            P.op('act', lambda e: e.activation(out=g_ss[0:C, :], in_=g_ss[0:C, :], func=AF.Ln, bias=EPS, scale=1.0 / 128), r=['g_ss'], w=['g_ss'])
            P.op('act', lambda e: e.activation(out=g_ss[0:C, :], in_=g_ss[0:C, :], func=AF.Exp, scale=-0.5), r=['g_ss'], w=['g_ss'])
            P.op('dve', lambda e: e.tensor_tensor(out=g_o2[0:C, :, :], in0=g_o[0:C, :, :], in1=g_ss[0:C, :].unsqueeze(2).to_broadcast([C, 4, 128]), op=ALU.mult),
                 r=['g_A', 'g_ss'], w=['g_trig'])
            pot, potk = psum('misc')
            for h in range(4):
                P.op('pe', lambda e, h=h, pot=pot: e.transpose(out=pot[:, h * C:(h + 1) * C], in_=g_o2[0:C, h, :], identity=ident[0:C, 0:C]),
                     r=['g_trig', 'ident'], w=[potk])
            P.op('dve', lambda e, pot=pot: e.scalar_tensor_tensor(out=mix[:, 4:8, c0:c0 + C], in0=pot[:, 0:C4].rearrange("p (h c) -> p h c", h=4),
                                                                 scalar=vcol[l][:, VC_ONORM:VC_ONORM + 1], in1=sz[:, :, c0:c0 + C],
                                                                 op0=ALU.mult, op1=ALU.mult), r=[potk, 'sz', 'vcol%d' % l], w=['mix_c'])

        def mixer(l, T, C, first_tile, is_sample, s_idx, par=0):
            hn = hns[par]
            hk = 'hn%d' % par
            rmsnorm(T, l, VC_MIX, hn, hk, par)
            for j in range(6):
                nch = 4 if j < 5 else 2
                wv, wk = wload('w_in', l, j, (8, nch * 128))
                for c in range(nch):
                    n = j * 4 + c
                    pp, ppk = psum('big')
                    for k in range(KC):
                        P.op('pe', lambda e, k=k, c=c, pp=pp, wv=wv: e.matmul(pp[:, 0:T], lhsT=wv[:, k, c * 128:(c + 1) * 128], rhs=hn[:, k, 0:T],
                                                                             start=(k == 0), stop=(k == KC - 1)), r=[hk, wk], w=[ppk])
                    if n < 4:
                        P.op('act', lambda e, n=n, pp=pp: e.copy(out=uv[:, n, 0:T], in_=pp[:, 0:T]), r=[ppk], w=['uv'])
                    elif n < 6:
                        P.op('act', lambda e, n=n, pp=pp: e.copy(out=pbuf[:, n - 4, 16:16 + T], in_=pp[:, 0:T]), r=[ppk], w=['pbuf'])
                    elif n < 18:
                        P.op('act', lambda e, n=n, pp=pp: e.copy(out=cbuf[:, n - 6, 3:3 + T], in_=pp[:, 0:T]), r=[ppk], w=['cbuf'])
                    else:
                        P.op('act', lambda e, n=n, pp=pp: e.copy(out=sz[:, n - 18, 0:T], in_=pp[:, 0:T]), r=[ppk], w=['sz'])
            if MIXSTOP <= 1:
                return
            P.mark('HEAVY1')
            def sgu_stop(n):
                if SGU_STOP <= n:
                    P.op = lambda *a, **k: None
                    P.dma = lambda *a, **k: None
            P.op('dve', lambda e: e.tensor_tensor(out=gt[:, :, 0:T], in0=uv[:, :, 0:T], in1=uv[:, :, 0:T], op=ALU.mult), r=['uv'], w=['gt'])
            P.op('dve', lambda e: e.tensor_scalar(out=gt[:, :, 0:T], in0=gt[:, :, 0:T], scalar1=0.044715, scalar2=1.0, op0=ALU.mult, op1=ALU.add),
                 r=['gt'], w=['gt'])
            P.op('dve', lambda e: e.tensor_tensor(out=gt[:, :, 0:T], in0=gt[:, :, 0:T], in1=uv[:, :, 0:T], op=ALU.mult), r=['gt', 'uv'], w=['gt'])
            P.op('act', lambda e: e.activation(out=gt[:, :, 0:T], in_=gt[:, :, 0:T], func=AF.Exp, scale=-1.5957691216057308), r=['gt'], w=['gt'])
            P.op('act', lambda e: e.activation(out=gt[:, :, 0:T], in_=gt[:, :, 0:T], func=AF.Ln, bias=1.0), r=['gt'], w=['gt'])
            P.op('act', lambda e: e.activation(out=gt[:, :, 0:T], in_=gt[:, :, 0:T], func=AF.Exp, scale=-1.0), r=['gt'], w=['gt'])
            P.op('dve', lambda e: e.tensor_tensor(out=uv[:, :, 0:T], in0=uv[:, :, 0:T], in1=gt[:, :, 0:T], op=ALU.mult), r=['gt', 'uv'], w=['uv'])
            sgu_stop(1)
            for c in range(2):
                pm, pmk = psum('misc')
                P.op('pe', lambda e, c=c, pm=pm: e.matmul(pm[:, 0:T], lhsT=bavg[:, :], rhs=uv[:, 2 + c, 0:T], start=True, stop=True), r=['bavg', 'uv'], w=[pmk])
                P.op('dve', lambda e, c=c, pm=pm: e.tensor_tensor(out=vc[:, c, 0:T], in0=uv[:, 2 + c, 0:T], in1=pm[:, 0:T], op=ALU.subtract),
                     r=[pmk, 'uv'], w=['vc'])
                P.op('dve', lambda e, c=c: e.tensor_tensor(out=gt[:, c, 0:T], in0=vc[:, c, 0:T], in1=vc[:, c, 0:T], op=ALU.mult), r=['vc'], w=['gt'])
                pv, pvk = psum('misc')
                P.op('pe', lambda e, c=c, pv=pv: e.matmul(pv[:, 0:T], lhsT=bavg[:, :], rhs=gt[:, c, 0:T], start=True, stop=True), r=['bavg', 'gt'], w=[pvk])
                P.op('act', lambda e, c=c, pv=pv: e.activation(out=gt[:, 2 + c, 0:T], in_=pv[:, 0:T], func=AF.Ln, bias=EPS), r=[pvk], w=['gt'])
                P.op('act', lambda e, c=c: e.activation(out=gt[:, 2 + c, 0:T], in_=gt[:, 2 + c, 0:T], func=AF.Exp, scale=-0.5), r=['gt'], w=['gt'])
                P.op('dve', lambda e, c=c: e.scalar_tensor_tensor(out=vn[:, c, 0:T], in0=vc[:, c, 0:T], scalar=vcol[l][:, VC_SGU + c:VC_SGU + c + 1],
                                                                  in1=gt[:, 2 + c, 0:T], op0=ALU.mult, op1=ALU.mult), r=['vc', 'gt', 'vcol%d' % l], w=['vn'])
            sgu_stop(2)
            CB = min(128, T)
            for b in range(T // CB):
                t0 = b * CB
                pt, pk = psum('misc')
                for c in range(2):
                    P.op('pe', lambda e, c=c, pt=pt, t0=t0: e.transpose(out=pt[0:CB, c * 128:(c + 1) * 128], in_=vn[:, c, t0:t0 + CB], identity=ident[:, :]),
                         r=['vn', 'ident'], w=[pk])
                P.op('dve', lambda e, pt=pt: e.tensor_copy(out=vtokb[0:CB, :], in_=pt[0:CB, 0:256]), r=[pk], w=['vtokb'])
                if is_sample:
                    P.op('act', lambda e, pt=pt: e.copy(out=vtok[0:CB, :], in_=pt[0:CB, 0:256]), r=[pk], w=['vtok'])
                    P.dma('pool', O['s_v'][l, s_idx], vtok[0:CB, :], r=['vtok'], w=())
                sgu_stop(3)
                for c in range(2):
                    pso, psok = psum('misc')
                    for hh in range(2):
                        P.op('pe', lambda e, c=c, hh=hh, pso=pso: e.matmul(pso[:, hh * 128:hh * 128 + CB], lhsT=vtokb[0:CB, c * 128:(c + 1) * 128],
                                                                          rhs=wsT[l][0:CB, 2 * c + hh, 0:CB], start=True, stop=True),
                             r=['vtokb', 'wsT%d' % l], w=[psok])
                    sgu_stop(4)
                    for hh in range(2):
                        pr = slice(64 * hh, 64 * hh + 64)
                        P.op('dve', lambda e, c=c, hh=hh, pso=pso, pr=pr, t0=t0: e.tensor_tensor(out=gt[pr, c, t0:t0 + CB], in0=pso[pr, hh * 128:hh * 128 + CB],
                                                                                           in1=sgub[l][pr, c, 0:CB], op=ALU.add),
                             r=[psok, 'sgub%d' % l], w=['gt'])
                    P.op('dve', lambda e, c=c, t0=t0: e.tensor_tensor(out=mix[:, c, t0:t0 + CB], in0=gt[:, c, t0:t0 + CB], in1=uv[:, c, t0:t0 + CB], op=ALU.mult),
                         r=['gt', 'uv'], w=['mix_a'])
            if 'op' in P.__dict__:
                del P.__dict__['op']
                del P.__dict__['dma']
            if MIXSTOP <= 2:
                return
            P.op('dve', lambda e: e.tensor_copy(out=pbuf[:, :, 1:16], in_=phist[l][:, :, :]), r=['phist%d' % l], w=['pbuf'])
            W_ = 16 + T
            P.op('dve', lambda e: e.tensor_tensor(out=pw1[:, :, 1:W_], in0=pbuf[:, :, 1:W_], in1=pbuf[:, :, 0:W_ - 1], op=ALU.add), r=['pbuf'], w=['pw1'])
            P.op('dve', lambda e: e.tensor_tensor(out=pw2[:, :, 3:W_], in0=pw1[:, :, 3:W_], in1=pw1[:, :, 1:W_ - 2], op=ALU.add), r=['pw1'], w=['pw2'])
            lo, hi = slice(0, 64), slice(64, 128)

            def pool_d(src, pr, c, wdw, srck):
                if first_tile:
                    P.op('dve', lambda e: e.tensor_tensor(out=gt[pr, c, 0:16], in0=src[pr, c, 16:32], in1=rcfix[pr, c, :], op=ALU.mult), r=[srck, 'rcfix'], w=['gt'])
                    P.op('dve', lambda e: e.tensor_tensor(out=gt[pr, c, 0:16], in0=gt[pr, c, 0:16], in1=pbuf[pr, c, 16:32], op=ALU.subtract), r=['gt', 'pbuf'], w=['gt'])
                    P.op('dve', lambda e: e.scalar_tensor_tensor(out=gt[pr, c, 16:T], in0=src[pr, c, 32:16 + T], scalar=1.0 / wdw, in1=pbuf[pr, c, 32:16 + T],
                                                                  op0=ALU.mult, op1=ALU.subtract), r=[srck, 'pbuf'], w=['gt'])
                else:
                    P.op('dve', lambda e: e.scalar_tensor_tensor(out=gt[pr, c, 0:T], in0=src[pr, c, 16:16 + T], scalar=1.0 / wdw, in1=pbuf[pr, c, 16:16 + T],
                                                                  op0=ALU.mult, op1=ALU.subtract), r=[srck, 'pbuf'], w=['gt'])
            pool_d(pw1, lo, 0, 2.0, 'pw1')
            pool_d(pw2, hi, 0, 4.0, 'pw2')
            P.op('dve', lambda e: e.tensor_tensor(out=pw1[:, 1, 7:W_], in0=pw2[:, 1, 7:W_], in1=pw2[:, 1, 3:W_ - 4], op=ALU.add), r=['pw2', 'pw1'], w=['pw1'])
            pool_d(pw1, lo, 1, 8.0, 'pw1')
            P.op('dve', lambda e: e.tensor_tensor(out=pw2[hi, 1, 15:W_], in0=pw1[hi, 1, 15:W_], in1=pw1[hi, 1, 7:W_ - 8], op=ALU.add), r=['pw1', 'pw2'], w=['pw2'])
            pool_d(pw2, hi, 1, 16.0, 'pw2')
            P.op('dve', lambda e: e.tensor_copy(out=dTb[:, :, 0:T], in_=gt[:, 0:2, 0:T]), r=['gt'], w=['dTb'])
            P.op('dve', lambda e: e.tensor_copy(out=phist[l][:, :, :], in_=pbuf[:, :, T + 1:T + 16]), r=['pbuf'], w=['phist%d' % l])
            for c in range(2):
                pp, ppk = psum('misc')
                P.op('pe', lambda e, c=c, pp=pp: e.matmul(pp[:, 0:T], lhsT=wpb[l][:, c, :], rhs=dTb[:, c, 0:T], start=True, stop=True), r=['wpb%d' % l, 'dTb'], w=[ppk])
                P.op('act', lambda e, c=c, pp=pp: e.mul(out=mix[:, 2 + c, 0:T], in_=pp[:, 0:T], mul=vcol[l][:, VC_PSC + c:VC_PSC + c + 1]),
                     r=[ppk, 'vcol%d' % l], w=['mix_b'])
            if MIXSTOP <= 3:
                return
            P.op('dve', lambda e: e.tensor_copy(out=cbuf[:, :, 0:3], in_=chist[l][:, :, :]), r=['chist%d' % l], w=['cbuf'])
            for n in range(12):
                for j in range(4):
                    wc = vcol[l][:, VC_CONV + j * 12 + n:VC_CONV + j * 12 + n + 1]
                    ceng = 'pool' if (CONV_POOL and n >= 8) else 'dve'
                    if j == 0:
                        P.op(ceng, lambda e, n=n, wc=wc: e.tensor_scalar(out=qkvs[:, n, 0:T], in0=cbuf[:, n, 0:T], scalar1=wc, scalar2=None, op0=ALU.mult),
                             r=['cbuf', 'vcol%d' % l], w=['qkvs'])
                    else:
                        P.op(ceng, lambda e, n=n, j=j, wc=wc: e.scalar_tensor_tensor(out=qkvs[:, n, 0:T], in0=cbuf[:, n, j:j + T], scalar=wc, in1=qkvs[:, n, 0:T],
                                                                                      op0=ALU.mult, op1=ALU.add), r=['cbuf', 'qkvs', 'vcol%d' % l], w=['qkvs'])
            P.op('dve', lambda e: e.tensor_copy(out=chist[l][:, :, :], in_=cbuf[:, :, T:T + 3]), r=['cbuf'], w=['chist%d' % l])
            P.op('act', lambda e: e.activation(out=qkvs[:, :, 0:T], in_=qkvs[:, :, 0:T], func=AF.Silu), r=['qkvs'], w=['qkvs'])
            P.op('act', lambda e: e.activation(out=sz[:, :, 0:T], in_=sz[:, :, 0:T], func=AF.Silu), r=['sz'], w=['sz'])
            P.mark('HEAVY0')
            if MIXSTOP <= 4:
                return
            for ci in range(T // C):
                gdn_chunk(l, ci * C, C, par)
            if MIXSTOP <= 5:
                return
            proj_residual(l, 'w_out', 8, mix, ['mix_a', 'mix_b', 'mix_c'], T, 2, par)

        def xattn(l, T, par=0):
            hn = hns[par]
            hk = 'hn%d' % par
            rmsnorm(T, l, VC_XATT, hn, hk, par)
            wv, wk = wload('w_mq', l, 0, (8, 512))
            for h in range(4):
                pq, pqk = psum('big')
                for k in range(KC):
                    P.op('pe', lambda e, k=k, h=h, pq=pq: e.matmul(pq[:, 0:T], lhsT=wv[:, k, h * 128:(h + 1) * 128], rhs=hn[:, k, 0:T],
                                                                 start=(k == 0), stop=(k == KC - 1)), r=[hk, wk], w=[pqk])
                P.op('act', lambda e, h=h, pq=pq: e.copy(out=qTb[:, h, 0:T], in_=pq[:, 0:T]), r=[pqk], w=['qTb'])
            for h in range(4):
                pT = pTb[h % 2]
                pTk = 'pTb%d' % (h % 2)
                for mb in range(2):
                    psc, psck = psum('big')
                    P.op('pe', lambda e, h=h, mb=mb, psc=psc: e.matmul(psc[:, 0:T], lhsT=mkT[l][:, h, mb * 128:(mb + 1) * 128], rhs=qTb[:, h, 0:T],
                                                                      start=True, stop=True), r=['mkT%d' % l, 'qTb'], w=[psck])
                    P.op('act', lambda e, mb=mb, psc=psc, pT=pT: e.activation(out=pT[:, mb, 0:T], in_=psc[:, 0:T], func=AF.Exp, scale=float(128 ** -0.5)),
                         r=[psck], w=[pTk])
                psm, psmk = psum('big')
                for mb in range(2):
                    P.op('pe', lambda e, mb=mb, psm=psm, pT=pT: e.matmul(psm[:, 0:T], lhsT=ones_b[:, :], rhs=pT[:, mb, 0:T], start=(mb == 0), stop=(mb == 1)),
                         r=['ones_b', pTk], w=[psmk])
                P.op('act', lambda e, psm=psm: e.activation(out=rsum[:, 0:T], in_=psm[:, 0:T], func=AF.Ln), r=[psmk], w=['rsum'])
                P.op('act', lambda e: e.activation(out=rsum[:, 0:T], in_=rsum[:, 0:T], func=AF.Exp, scale=-1.0), r=['rsum'], w=['rsum'])
                pov, povk = psum('big')
                for mb in range(2):
                    P.op('pe', lambda e, h=h, mb=mb, pov=pov, pT=pT: e.matmul(pov[:, 0:T], lhsT=mvb[l][:, mb, h * 128:(h + 1) * 128], rhs=pT[:, mb, 0:T],
                                                                             start=(mb == 0), stop=(mb == 1)), r=['mvb%d' % l, pTk], w=[povk])
                P.op('dve', lambda e, h=h, pov=pov: e.tensor_tensor(out=oTb[:, h, 0:T], in0=pov[:, 0:T], in1=rsum[:, 0:T], op=ALU.mult),
                     r=[povk, 'rsum'], w=['oTb'])
            proj_residual(l, 'w_mo', 4, oTb, ['oTb'], T, 2, par)

        dctr = [0]

        def seq_setup(is_sample, s_idx):
            for l in range(L):
                if not is_sample:
                    P.op('dve', lambda e, l=l: e.memset(S_st[l][:], 0.0), r=(), w=['S%d' % l])
                    P.op('dve', lambda e, l=l: e.memset(chist[l][:], 0.0), r=(), w=['chist%d' % l])
                    P.op('dve', lambda e, l=l: e.memset(phist[l][:], 0.0), r=(), w=['phist%d' % l])
                else:
                    P.dma('sp', S_st[l][:], I['sssm'][l, s_idx].rearrange("h k v -> k h v"), r=(), w=['S%d' % l])
                    P.dma('sp', sttok[0:3, :], I['sconv'][l, s_idx], r=(), w=['cbuf'])
                    pt, pk = psum('misc')
                    for n in range(12):
                        P.op('pe', lambda e, n=n, pt=pt: e.transpose(out=pt[:, n * 4:n * 4 + 3], in_=sttok[0:3, n * 128:(n + 1) * 128], identity=ident[0:3, 0:3]),
                             r=['cbuf', 'ident'], w=[pk])
                    P.op('dve', lambda e, l=l, pt=pt: e.tensor_copy(out=chist[l][:, :, :], in_=pt[:, 0:48].rearrange("p (n j) -> p n j", j=4)[:, :, 0:3]),
                         r=[pk], w=['chist%d' % l])
                    P.dma('sp', sttok[0:15, 0:256], I['spool'][l, s_idx], r=(), w=['cbuf'])
                    pt, pk = psum('misc')
                    for c in range(2):
                        P.op('pe', lambda e, c=c, pt=pt: e.transpose(out=pt[:, c * 16:c * 16 + 15], in_=sttok[0:15, c * 128:(c + 1) * 128], identity=ident[0:15, 0:15]),
                             r=['cbuf', 'ident'], w=[pk])
                    P.op('dve', lambda e, l=l, pt=pt: e.tensor_copy(out=phist[l][:, :, :], in_=pt[:, 0:32].rearrange("p (c j) -> p c j", j=16)[:, :, 0:15]),
                         r=[pk], w=['phist%d' % l])
            if is_sample:
                for l in range(L):
                    P.dma('sp', memtok[:, :, 0:512], I['cmk'][l, s_idx].rearrange("(mb p) f -> p mb f", p=128), r=(), w=['qkvs'])
                    for h in range(4):
                        pt, pk = psum('misc')
                        for mb in range(2):
                            P.op('pe', lambda e, h=h, mb=mb, pt=pt: e.transpose(out=pt[:, mb * 128:(mb + 1) * 128], in_=memtok[:, mb, h * 128:(h + 1) * 128],
                                                                               identity=ident[:, :]), r=['qkvs', 'ident'], w=[pk])
                        P.op('dve', lambda e, l=l, h=h, pt=pt: e.tensor_copy(out=mkT[l][:, h, :], in_=pt[:, 0:256]), r=[pk], w=['mkT%d' % l])
                    P.dma('sp', memtok[:, :, 512:1024], I['cmv'][l, s_idx].rearrange("(mb p) f -> p mb f", p=128), r=(), w=['qkvs'])
                    P.op('dve', lambda e, l=l: e.tensor_copy(out=mvb[l][:, :, :], in_=memtok[:, :, 512:1024]), r=['qkvs'], w=['mvb%d' % l])
            else:
                P.dma('sp', memtok[:, :, :], I['memp'].rearrange("(mb p) f -> p mb f", p=128), r=(), w=['qkvs'])
                for k in range(KC):
                    pt, pk = psum('misc')
                    for mb in range(2):
                        P.op('pe', lambda e, k=k, mb=mb, pt=pt: e.transpose(out=pt[:, mb * 128:(mb + 1) * 128], in_=memtok[:, mb, k * 128:(k + 1) * 128],
                                                                           identity=ident[:, :]), r=['qkvs', 'ident'], w=[pk])
                    P.op('dve', lambda e, k=k, pt=pt: e.tensor_copy(out=memT[:, k, :], in_=pt[:, 0:256]), r=[pk], w=['cbuf'])
                P.op('act', lambda e: e.activation(out=memn[:, :, :], in_=memT[:, :, :], func=AF.Square), r=['cbuf'], w=['hn0'])
                pt, pk = psum('misc')
                for k in range(KC):
                    P.op('pe', lambda e, k=k, pt=pt: e.matmul(pt[:, 0:MEM], lhsT=ones_b[:, :], rhs=memn[:, k, :], start=(k == 0), stop=(k == KC - 1)),
                         r=['hn0', 'ones_b'], w=[pk])
                P.op('act', lambda e, pt=pt: e.activation(out=rstd[:, 0:MEM], in_=pt[:, 0:MEM], func=AF.Ln, bias=EPS, scale=1.0 / D), r=[pk], w=['rstd'])
                P.op('act', lambda e: e.activation(out=rstd[:, 0:MEM], in_=rstd[:, 0:MEM], func=AF.Exp, scale=-0.5), r=['rstd'], w=['rstd'])
                for l in range(L):
                    for k in range(KC):
                        P.op('dve', lambda e, l=l, k=k: e.scalar_tensor_tensor(out=memn[:, k, :], in0=memT[:, k, :], scalar=vcol[l][:, VC_MEM + k:VC_MEM + k + 1],
                                                                               in1=rstd[:, 0:MEM], op0=ALU.mult, op1=ALU.mult),
                             r=['cbuf', 'rstd', 'vcol%d' % l], w=['hn0'])
                    wkv, wkk = wload('w_mk', l, 0, (8, 512))
                    for h in range(4):
                        pt, pk = psum('misc')
                        for k in range(KC):
                            P.op('pe', lambda e, k=k, h=h, pt=pt, wkv=wkv: e.matmul(pt[:, 0:MEM], lhsT=wkv[:, k, h * 128:(h + 1) * 128], rhs=memn[:, k, :],
                                                                                    start=(k == 0), stop=(k == KC - 1)), r=['hn0', wkk], w=[pk])
                        P.op('dve', lambda e, l=l, h=h, pt=pt: e.tensor_copy(out=mkT[l][:, h, :], in_=pt[:, 0:MEM]), r=[pk], w=['mkT%d' % l])
                    for mb in range(2):
                        pt, pk = psum('misc')
                        for k in range(KC):
                            P.op('pe', lambda e, k=k, mb=mb, pt=pt, wkv=wkv: e.matmul(pt[:, 0:512], lhsT=memn[:, k, mb * 128:(mb + 1) * 128], rhs=wkv[:, k, :],
                                                                                     start=(k == 0), stop=(k == KC - 1)), r=['hn0', wkk], w=[pk])
                        P.op('act', lambda e, pt=pt: e.copy(out=kvst[:, :], in_=pt[:, 0:512]), r=[pk], w=['gt'])
                        P.dma('pool', O['p_mk'][l, mb * 128:(mb + 1) * 128, :], kvst[:, :], r=['gt'], w=())
                    wvv, wvk = wload('w_mv', l, 0, (8, 512))
                    for mb in range(2):
                        pt, pk = psum('misc')
                        for k in range(KC):
                            P.op('pe', lambda e, k=k, mb=mb, pt=pt, wvv=wvv: e.matmul(pt[:, 0:512], lhsT=memn[:, k, mb * 128:(mb + 1) * 128], rhs=wvv[:, k, :],
                                                                                     start=(k == 0), stop=(k == KC - 1)), r=['hn0', wvk], w=[pk])
                        P.op('act', lambda e, pt=pt: e.copy(out=kvst[:, :], in_=pt[:, 0:512]), r=[pk], w=['gt'])
                        P.op('dve', lambda e, l=l, mb=mb, pt=pt: e.tensor_copy(out=mvb[l][:, mb, :], in_=pt[:, 0:512]), r=[pk], w=['mvb%d' % l])
                        P.dma('pool', O['p_mv'][l, mb * 128:(mb + 1) * 128, :], kvst[:, :], r=['gt'], w=())

        def seq_finish(is_sample, s_idx):
            for l in range(L):
                o_ssm = O['s_ssm'][l, s_idx] if is_sample else O['p_ssm'][l]
                o_conv = O['s_conv'][l, s_idx] if is_sample else O['p_conv'][l]
                o_pool = O['s_pool'][l, s_idx] if is_sample else O['p_pool'][l]
                P.dma('pool', o_ssm.rearrange("h k v -> k h v"), S_st[l][:], r=['S%d' % l], w=())
                pt, pk = psum('misc')
                for n in range(4):
                    P.op('pe', lambda e, l=l, n=n, pt=pt: e.transpose(out=pt[0:3, n * 128:(n + 1) * 128], in_=chist[l][:, n, :], identity=ident[:, :]),
                         r=['chist%d' % l, 'ident'], w=[pk])
                P.op('dve', lambda e, pt=pt: e.tensor_copy(out=sttok[0:3, 0:512], in_=pt[0:3, 0:512]), r=[pk], w=['cbuf'])
                for q in range(1, 3):
                    pt, pk = psum('misc')
                    for n in range(4):
                        P.op('pe', lambda e, l=l, n=n, q=q, pt=pt: e.transpose(out=pt[0:3, n * 128:(n + 1) * 128], in_=chist[l][:, 4 * q + n, :], identity=ident[:, :]),
                             r=['chist%d' % l, 'ident'], w=[pk])
                    P.op('dve', lambda e, q=q, pt=pt: e.tensor_copy(out=sttok[0:3, 512 * q:512 * q + 512], in_=pt[0:3, 0:512]), r=[pk], w=['cbuf'])
                P.dma('pool', o_conv, sttok[0:3, :], r=['cbuf'], w=())
                pt, pk = psum('misc')
                for c in range(2):
                    P.op('pe', lambda e, l=l, c=c, pt=pt: e.transpose(out=pt[0:15, c * 128:(c + 1) * 128], in_=phist[l][:, c, :], identity=ident[:, :]),
                         r=['phist%d' % l, 'ident'], w=[pk])
                P.op('dve', lambda e, pt=pt: e.tensor_copy(out=kvst[0:15, 0:256], in_=pt[0:15, 0:256]), r=[pk], w=['gt'])
                P.dma('pool', o_pool, kvst[0:15, 0:256], r=['gt'], w=())

        tctr = [0]

        def run_tile(xsrc, ydst, t0, T, C, first_tile, is_sample, s_idx, par=0):
            xT = xTs[par]
            cur_par[0] = par
            P.mark('X')
            nb = (T + 127) // 128
            stg = []
            for b in range(nb):
                nt = min(128, T - b * 128)
                i = tctr[0] % 2
                tctr[0] += 1
                P.dma('sp', xst[i][0:nt, :], xsrc[t0 + b * 128:t0 + b * 128 + nt, :], r=(), w=['xst%d' % i])
                stg.append((i, nt))
            for k in range(KC):
                pt, pk = psum('misc')
                for b, (i, nt) in enumerate(stg):
                    P.op('pe', lambda e, k=k, b=b, i=i, nt=nt, pt=pt: e.transpose(out=pt[:, b * 128:b * 128 + nt], in_=xst[i][0:nt, k * 128:(k + 1) * 128],
                                                                                 identity=ident[0:nt, 0:nt]), r=['xst%d' % i, 'ident'], w=[pk])
                P.op('act', lambda e, k=k, pt=pt: e.copy(out=xT[:, k, 0:T], in_=pt[:, 0:T]), r=[pk], w=['xT%d_%d' % (par, k)])
            for l in range(L):
                if l == L - 1:
                    P.mark('START_NEXT')
                if DBG is None or (l, 'ffn1') in DBG:
                    P.mark('F')
                    ffn(l, 1, T, par)
                if DBG is None or (l, 'mixer') in DBG:
                    P.mark('M')
                    mixer(l, T, C, first_tile, is_sample, s_idx, par)
                if DBG is None or (l, 'xattn') in DBG:
                    P.mark('A')
                    xattn(l, T, par)
                if DBG is None or (l, 'ffn2') in DBG:
                    P.mark('F')
                    ffn(l, 2, T, par)
            P.mark('X')
            P.mark('AB')
            P.op('act', lambda e: e.activation(out=sqb[:, :, 0:T], in_=xT[:, :, 0:T], func=AF.Square),
                 r=['xT%d_%d' % (par, k) for k in range(KC)], w=['sqb'])
            pt, pk = psum('misc')
            for k in range(KC):
                P.op('pe', lambda e, k=k, pt=pt: e.matmul(pt[:, 0:T], lhsT=ones_b[:, :], rhs=sqb[:, k, 0:T], start=(k == 0), stop=(k == KC - 1)),
                     r=['sqb', 'ones_b'], w=[pk])
            P.op('act', lambda e, pt=pt: e.activation(out=rstd[:, 0:T], in_=pt[:, 0:T], func=AF.Ln, bias=EPS, scale=1.0 / D), r=[pk], w=['rstd'])
            P.op('act', lambda e: e.activation(out=rstd[:, 0:T], in_=rstd[:, 0:T], func=AF.Exp, scale=-0.5), r=['rstd'], w=['rstd'])
            ysl = []
            for b_ in range(nb):
                i = tctr[0] % 2
                tctr[0] += 1
                ysl.append(i)
            for k in range(KC):
                P.op('dve', lambda e, k=k: e.scalar_tensor_tensor(out=ynk[:, 0:T], in0=xT[:, k, 0:T], scalar=vcol[0][:, VC_FINAL + k:VC_FINAL + k + 1],
                                                                     in1=rstd[:, 0:T], op0=ALU.mult, op1=ALU.mult),
                     r=['xT%d_%d' % (par, k), 'rstd', 'vcol0'], w=['ynk'])
                pt, pk = psum('misc')
                for b_ in range(nb):
                    nt = min(128, T - b_ * 128)
                    P.op('pe', lambda e, b_=b_, nt=nt, pt=pt: e.transpose(out=pt[0:nt, b_ * 128:(b_ + 1) * 128], in_=ynk[:, b_ * 128:b_ * 128 + nt],
                                                                          identity=ident[:, :]), r=['ynk', 'ident'], w=[pk])
                for b_ in range(nb):
                    nt = min(128, T - b_ * 128)
                    i = ysl[b_]
                    P.op('act', lambda e, i=i, k=k, b_=b_, nt=nt, pt=pt: e.copy(out=yst[i][0:nt, k * 128:(k + 1) * 128], in_=pt[0:nt, b_ * 128:(b_ + 1) * 128]),
                         r=[pk], w=['xst%d' % i])
            for b_ in range(nb):
                nt = min(128, T - b_ * 128)
                i = ysl[b_]
                P.dma('pool', ydst[t0 + b_ * 128:t0 + b_ * 128 + nt, :], yst[i][0:nt, :], r=['xst%d' % i], w=())
            P.mark('AE')

        for s in range(NS):
            if STAGE >= 2 + s:
                seq_setup(True, s)
                if STAGE != 20:
                    run_tile(I['xs'][s], O['y_s'][s], 0, LS, LS, False, True, s)
                if STAGE != 21:
                    seq_finish(True, s)
        if STAGE >= 4:
            seq_setup(False, 0)
        if STAGE >= 5:
            lists = []
            for ti in range(SEQ // TP):
                P.rec = []
                run_tile(I['xp'], O['y_p'], ti * TP, TP, min(128, TP), ti == 0, False, 0, ti % 2)
                lists.append(P.rec)
                P.rec = None
            class Cur:
                def __init__(self, lst):
                    self.lst = lst
                    self.pos = 0
                    self.typ = None
                    self.want_next = False
                    self.depth = 0
                    self.heavy = False

                def done(self):
                    return self.pos >= len(self.lst)

                def step(self, other):
                    while self.pos < len(self.lst):
                        rc = self.lst[self.pos]
                        if rc[0] == 'mark':
                            m = rc[1]
                            if m == 'START_NEXT':
                                self.want_next = True
                            elif m == 'HEAVY1':
                                self.heavy = True
                            elif m == 'HEAVY0':
                                self.heavy = False
                            elif m == 'AB':
                                self.depth += 1
                            elif m == 'AE':
                                self.depth -= 1
                                if self.depth == 0:
                                    self.pos += 1
                                    return True
                            else:
                                if other is not None and not other.done() and other.typ == m:
                                    return False
                                self.typ = m
                            self.pos += 1
                            continue
                        if not P.feed_one(rc):
                            assert self.depth == 0
                            return False
                        self.pos += 1
                        if rc[0] in ('op', 'dma'):
                            vt[rc[1]] = max(vt[rc[1]], min(vt.values())) + COST[rc[1]]
                        if self.depth == 0:
                            return True
                    self.typ = None
                    return True

                def next_eng(self):
                    p = self.pos
                    while p < len(self.lst):
                        rc = self.lst[p]
                        if rc[0] in ('op', 'dma'):
                            return rc[1]
                        if rc[0] == 'custom':
                            return 'sp'
                        p += 1
                    return 'sp'
            flip = [0]
            COST = {'pe': 110.0, 'dve': 420.0, 'act': 480.0, 'pool': 100.0, 'sp': 60.0}
            vt = {e: 0.0 for e in ENGS}
            active = [Cur(lists[0])]
            nxt = 1
            while active:
                if PIPELINE and len(active) == 1 and active[0].want_next and nxt < len(lists):
                    active.append(Cur(lists[nxt]))
                    nxt += 1
                if len(active) == 2:
                    a_, b_ = active
                    lag = a_.pos / len(a_.lst) - b_.pos / len(b_.lst)
                    if POLICY == 'old':
                        ra = a_.step(b_)
                        rb = b_.step(a_)
                        assert ra or rb
                        active = [c for c in active if not c.done()]
                        continue
                    if POLICY == 'ratio':
                        fa, fb = a_.typ == 'F', b_.typ == 'F'
                        wa = HRATIO if (fa and b_.heavy) else (FRATIO if (fa and not fb) else 1)
                        wb = HRATIO if (fb and a_.heavy) else (FRATIO if (fb and not fa) else 1)
                        flip[0] = (flip[0] + 1) % (wa + wb)
                        first, second = (a_, b_) if flip[0] < wa else (b_, a_)
                    elif POLICY == 'alt':
                        flip[0] ^= 1
                        first, second = (a_, b_) if flip[0] else (b_, a_)
                    elif lag > 0.56:
                        first, second = b_, a_
                    elif lag < 0.44:
                        first, second = a_, b_
                    elif vt[a_.next_eng()] <= vt[b_.next_eng()]:
                        first, second = a_, b_
                    else:
                        first, second = b_, a_
                    if not first.step(second):
                        ok = second.step(first)
                        assert ok, (first.typ, first.pos, len(first.lst), first.lst[first.pos][:2], second.typ, second.pos, len(second.lst), second.lst[second.pos][:2], first.depth, second.depth)
                else:
                    active[0].step(None)
                active = [c for c in active if not c.done()]
                if not active and nxt < len(lists):
                    active.append(Cur(lists[nxt]))
                    nxt += 1
            cur_par[0] = 0
            seq_finish(False, 0)
        P.emit()
        nops = {e: len(P.ops[e]) for e in ENGS}
    return nc, nops


_CACHE = {}


def _get_program(SEQ, TP):
    key = (SEQ, TP)
    if key not in _CACHE:
        _CACHE[key] = build_program(SEQ, TP)
    return _CACHE[key]


W_NAMES = ['ffn1_norm', 'ffn1_w_gate', 'ffn1_w_up', 'ffn1_w_down', 'mix_norm', 'w_in', 'sgu_norm', 'sgu_w', 'sgu_b',
           'pool_w', 'pool_scale', 'gdn_conv_w', 'gdn_a_log', 'gdn_dt_bias', 'gdn_out_norm', 'w_out', 'xattn_norm',
           'mem_norm', 'w_mq', 'w_mk', 'w_mv', 'w_mo', 'ffn2_norm', 'ffn2_w_gate', 'ffn2_w_up', 'ffn2_w_down', 'final_norm']

TP_DEFAULT = 256


def kernel(**inputs):
    x_prompt = np.asarray(inputs['x_prompt'], dtype=np.float32)
    B, SEQ, _ = x_prompt.shape
    x_sample = np.asarray(inputs['x_sample'], dtype=np.float32)
    NSB, LS, _ = x_sample.shape
    NS = NSB // B
    TP = min(TP_DEFAULT, SEQ)
    nc, nops = _get_program(SEQ, TP)
    f32 = lambda a: np.ascontiguousarray(np.asarray(a, dtype=np.float32))
    wts = {nm: f32(inputs[nm]) for nm in W_NAMES}
    in_maps = []
    for c in range(NCORES):
        b = c % B
        m = dict(wts)
        m['xp'] = f32(x_prompt[b])
        m['xs'] = f32(x_sample[NS * b:NS * b + NS])
        m['memp'] = f32(inputs['mem_prompt'][b])
        m['cmk'] = f32(np.asarray(inputs['cache_mem_k'])[:, NS * b:NS * b + NS].reshape(DEPTH, NS, MEM, 512))
        m['cmv'] = f32(np.asarray(inputs['cache_mem_v'])[:, NS * b:NS * b + NS].reshape(DEPTH, NS, MEM, 512))
        m['spool'] = f32(np.asarray(inputs['state_pool'])[:, NS * b:NS * b + NS])
        m['sconv'] = f32(np.asarray(inputs['state_conv'])[:, NS * b:NS * b + NS])
        m['sssm'] = f32(np.asarray(inputs['state_ssm'])[:, NS * b:NS * b + NS])
        in_maps.append(m)
    res = run_bass_kernel_spmd(nc, in_maps, core_ids=list(range(NCORES)))
    R = res.results
    R = [R[i % len(R)] for i in range(B)]
    y_prompt = np.stack([R[b]['y_p'] for b in range(B)])
    y_sample = np.concatenate([R[b]['y_s'] for b in range(B)], axis=0)
    p_pool = np.stack([R[b]['p_pool'] for b in range(B)], axis=1)
    p_conv = np.stack([R[b]['p_conv'] for b in range(B)], axis=1)
    p_ssm = np.stack([R[b]['p_ssm'] for b in range(B)], axis=1)
    p_mk = np.stack([R[b]['p_mk'] for b in range(B)], axis=1).reshape(DEPTH, B, MEM, 4, 128)
    p_mv = np.stack([R[b]['p_mv'] for b in range(B)], axis=1).reshape(DEPTH, B, MEM, 4, 128)
    s_pool = np.concatenate([R[b]['s_pool'] for b in range(B)], axis=1)
    s_conv = np.concatenate([R[b]['s_conv'] for b in range(B)], axis=1)
    s_ssm = np.concatenate([R[b]['s_ssm'] for b in range(B)], axis=1)
    s_v = np.concatenate([R[b]['s_v'] for b in range(B)], axis=1)
    outs = (y_prompt, y_sample, p_pool, p_conv, p_ssm, p_mk, p_mv, s_pool, s_conv, s_ssm, s_v)
    return tuple(np.ascontiguousarray(o, dtype=np.float32) for o in outs)
```

```python
import numpy as np
from contextlib import ExitStack
import concourse.bass as bass
import concourse.mybir as mybir
from concourse.bass_utils import run_bass_kernel_spmd

F32 = mybir.dt.float32
BF16 = mybir.dt.bfloat16
ALU = mybir.AluOpType
AF = mybir.ActivationFunctionType
AX = mybir.AxisListType

D = 1024
KC = 8
DFF = 2816
FC = 22
NIN = 2824
DEPTH = 2
NCORES = 8
EPS = 1e-6
MEM = 256
SAME_ENGINE_SYNC = True
DBG = None
STAGE = 99
MIXSTOP = 99
PIPELINE = True
POLICY = 'ratio'
FRATIO = 1
HRATIO = 3
FFN_SHARED = False
FFN_EXPLN = False
CONV_POOL = False
SGU_STOP = 99
GDN_STOP = 99
DBG_MIX = None
GDT = F32

ENGS = ['pe', 'act', 'dve', 'pool', 'sp']


class Holder:
    view = None
    key = None
    refs = 0
    slot = None


class LazyKey:
    def __init__(self, h):
        self.h = h

    @property
    def key(self):
        return self.h.key


class LazyView:
    def __init__(self, h):
        self.h = h

    def __getitem__(self, idx):
        return self.h.view[idx]


class Prog:
    def __init__(self, nc, es):
        self.nc = nc
        self.es = es
        self.ops = {e: [] for e in ENGS}
        self.last_w = {}
        self.readers = {}
        self.dma_cnt = {}
        self.dma_keys_written = {}
        self.rec = None
        self.slot_live = {}

    def sb(self, name, shape, dt):
        return self.es.enter_context(self.nc.sbuf_tensor(name, list(shape), dt))

    def ps(self, name, shape, dt):
        return self.es.enter_context(self.nc.psum_tensor(name, list(shape), dt))

    def _deps(self, r, w):
        deps = []
        for k in r:
            if k in self.last_w:
                deps.append(self.last_w[k])
        for k in w:
            if k in self.last_w:
                deps.append(self.last_w[k])
            rd = self.readers.get(k)
            if rd:
                for (kind, tgt), val in rd.items():
                    deps.append((kind, tgt, val, 'war'))
        return deps

    def _commit(self, ev, r, w):
        for k in r:
            rd = self.readers.setdefault(k, {})
            kk = (ev[0], ev[1])
            if rd.get(kk, -1) < ev[2]:
                rd[kk] = ev[2]
        for k in w:
            self.last_w[k] = ev
            self.readers[k] = {}

    def op(self, eng, fn, r=(), w=()):
        if self.rec is not None:
            for k in r:
                if isinstance(k, LazyKey):
                    k.h.refs += 1
            self.rec.append(('op', eng, fn, list(r), list(w)))
            return
        for k in r:
            if isinstance(k, LazyKey) and k.h.refs > 0:
                k.h.refs -= 1
                if k.h.refs == 0:
                    self.slot_live[k.h.slot] = False
        r = [k.key if isinstance(k, LazyKey) else k for k in r]
        w = [k.key if isinstance(k, LazyKey) else k for k in w]
        pr = [k for k in r if k.startswith('ps')]
        if pr:
            w = list(w) + pr
        deps = self._deps(r, w)
        idx = len(self.ops[eng])
        self.ops[eng].append(dict(fn=fn, deps=deps, dma=None))
        self._commit(('c', eng, idx), r, w)

    def custom(self, fn):
        if self.rec is not None:
            self.rec.append(('custom', fn))
            return
        fn()

    def mark(self, typ):
        if self.rec is not None:
            self.rec.append(('mark', typ))

    def feed_one(self, recd):
        kind = recd[0]
        if kind == 'op':
            self.op(recd[1], recd[2], recd[3], recd[4])
            return True
        if kind == 'dma':
            self.dma(recd[1], recd[2], recd[3], recd[4], recd[5], recd[6])
            return True
        if kind == 'custom':
            return recd[1]() is not False
        return False

    def dma(self, q, out, in_, r=(), w=(), key=None):
        if self.rec is not None:
            self.rec.append(('dma', q, out, in_, list(r), list(w), key))
            return
        r = [k.key if isinstance(k, LazyKey) else k for k in r]
        w = [k.key if isinstance(k, LazyKey) else k for k in w]
        deps = self._deps(r, w)
        if key is None:
            key = w[0] if w else r[0]
        key = q + ':' + key
        cnt = self.dma_cnt.get(key, 0) + 1
        self.dma_cnt[key] = cnt
        self.ops[q].append(dict(fn=(lambda e, o=out, i=in_: e.dma_start(out=o, in_=i)), deps=deps, dma=key))
        self._commit(('d', key, cnt), r, w)
        self.dma_keys_written.setdefault(key, set()).update(w)

    def finalize_group(self, key):
        key = [q + ':' + key for q in ENGS if (q + ':' + key) in self.dma_cnt][0]
        tot = self.dma_cnt.get(key, 0)
        for k in self.dma_keys_written.get(key, ()):
            self.last_w[k] = ('d', key, tot)

    def emit(self):
        nc = self.nc
        need = {e: set() for e in ENGS}
        for e in ENGS:
            for op in self.ops[e]:
                for d in op['deps']:
                    if d[0] == 'c':
                        if d[1] == e and (e == 'pe' or not SAME_ENGINE_SYNC or len(d) > 3):
                            continue
                        need[d[1]].add(d[2])
        sig = {}
        for e in ENGS:
            c = 0
            m = {}
            for i, op in enumerate(self.ops[e]):
                if i in need[e]:
                    c += 1
                    m[i] = c
            sig[e] = m
        sems = {e: self.es.enter_context(nc.semaphore('s_' + e)) for e in ENGS if e != 'sp'}
        dsems = {}
        for k in self.dma_cnt:
            dsems[k] = self.es.enter_context(nc.semaphore('d%d' % len(dsems)))
        block = self.es.enter_context(nc.Block())
        prog = self

        def run(eng_name, e):
            waited = {}
            for i, op in enumerate(prog.ops[eng_name]):
                wl = {}
                for d in op['deps']:
                    if d[0] == 'c':
                        if d[1] == eng_name and (eng_name == 'pe' or not SAME_ENGINE_SYNC or len(d) > 3):
                            continue
                        tgt = ('c', d[1])
                        val = sig[d[1]][d[2]]
                    else:
                        tgt = ('d', d[1])
                        val = 16 * d[2]
                    if waited.get(tgt, 0) >= val:
                        continue
                    if wl.get(tgt, 0) < val:
                        wl[tgt] = val
                for tgt, val in wl.items():
                    s = sems[tgt[1]] if tgt[0] == 'c' else dsems[tgt[1]]
                    e.wait_ge(s, val)
                    waited[tgt] = val
                ins = op['fn'](e)
                if op['dma'] is not None:
                    ins.then_inc(dsems[op['dma']], 16)
                elif i in sig[eng_name]:
                    ins.then_inc(sems[eng_name], 1)
            if eng_name == 'sp':
                for k, c in prog.dma_cnt.items():
                    e.wait_ge(dsems[k], 16 * c)

        @block.tensor
        def _(e):
            run('pe', e)

        @block.scalar
        def _(e):
            run('act', e)

        @block.vector
        def _(e):
            run('dve', e)

        @block.gpsimd
        def _(e):
            run('pool', e)

        @block.sync
        def _(e):
            run('sp', e)


def build_program(SEQ, TP, LS=64, NS=2):
    nc = bass.Bass("TRN2", target_bir_lowering=False)
    L = DEPTH

    def din(name, shape):
        return nc.dram_tensor(name, list(shape), F32, kind="ExternalInput").ap()

    def dout(name, shape):
        return nc.dram_tensor(name, list(shape), F32, kind="ExternalOutput").ap()

    def dscr(name, shape, dt=BF16):
        return nc.dram_tensor(name, list(shape), dt).ap()

    I = {}
    I['xp'] = din('xp', [SEQ, D])
    I['xs'] = din('xs', [NS, LS, D])
    I['memp'] = din('memp', [MEM, D])
    I['cmk'] = din('cmk', [L, NS, MEM, 512])
    I['cmv'] = din('cmv', [L, NS, MEM, 512])
    I['spool'] = din('spool', [L, NS, 15, 256])
    I['sconv'] = din('sconv', [L, NS, 3, 1536])
    I['sssm'] = din('sssm', [L, NS, 4, 128, 128])
    for nm, shp in [('ffn1_norm', [L, D]), ('ffn1_w_gate', [L, D, DFF]), ('ffn1_w_up', [L, D, DFF]),
                    ('ffn1_w_down', [L, DFF, D]), ('mix_norm', [L, D]), ('w_in', [L, D, NIN]),
                    ('sgu_norm', [L, 256]), ('sgu_w', [L, 4, 128, 128]), ('sgu_b', [L, 4, 128]),
                    ('pool_w', [L, 4, 64, 64]), ('pool_scale', [L, 256]), ('gdn_conv_w', [L, 4, 1536]),
                    ('gdn_a_log', [L, 4]), ('gdn_dt_bias', [L, 4]), ('gdn_out_norm', [L, 128]),
                    ('w_out', [L, D, D]), ('xattn_norm', [L, D]), ('mem_norm', [L, D]),
                    ('w_mq', [L, D, 512]), ('w_mk', [L, D, 512]), ('w_mv', [L, D, 512]), ('w_mo', [L, 512, D]),
                    ('ffn2_norm', [L, D]), ('ffn2_w_gate', [L, D, DFF]), ('ffn2_w_up', [L, D, DFF]),
                    ('ffn2_w_down', [L, DFF, D]), ('final_norm', [D])]:
        I[nm] = din(nm, shp)
    O = {}
    O['y_p'] = dout('y_p', [SEQ, D])
    O['y_s'] = dout('y_s', [NS, LS, D])
    O['p_pool'] = dout('p_pool', [L, 15, 256])
    O['p_conv'] = dout('p_conv', [L, 3, 1536])
    O['p_ssm'] = dout('p_ssm', [L, 4, 128, 128])
    O['p_mk'] = dout('p_mk', [L, MEM, 512])
    O['p_mv'] = dout('p_mv', [L, MEM, 512])
    O['s_pool'] = dout('s_pool', [L, NS, 15, 256])
    O['s_conv'] = dout('s_conv', [L, NS, 3, 1536])
    O['s_ssm'] = dout('s_ssm', [L, NS, 4, 128, 128])
    O['s_v'] = dout('s_v', [L, NS, LS, 256])

    SCR = {}
    for nm in ['ffn1_w_gate', 'ffn1_w_up', 'ffn2_w_gate', 'ffn2_w_up', 'w_in']:
        SCR[nm] = dscr('scr_' + nm, [L, 6, 128, 8, 512])
    for nm in ['ffn1_w_down', 'ffn2_w_down']:
        SCR[nm] = dscr('scr_' + nm, [L, 8, 128, 22, 128])
    SCR['w_out'] = dscr('scr_w_out', [L, 2, 128, 8, 512])
    for nm in ['w_mq', 'w_mk', 'w_mv']:
        SCR[nm] = dscr('scr_' + nm, [L, 1, 128, 8, 512])
    SCR['w_mo'] = dscr('scr_w_mo', [L, 1, 128, 4, 1024])

    TMAX = max(TP, LS)
    with ExitStack() as es:
        P = Prog(nc, es)
        sb, ps = P.sb, P.ps
        xTs = [sb('xT%d' % i, [128, KC, TMAX], F32) for i in range(2)]
        hns = [sb('hn%d' % i, [128, KC, TMAX], BF16) for i in range(2)]
        sqb = sb('sqb', [128, KC, TMAX], BF16)
        rstd = sb('rstd', [128, TMAX], F32)
        ynk = sb('ynk', [128, TMAX], F32)
        act = sb('act', [128, FC, TMAX], BF16)
        sgt = [sb('sgt%d' % i, [128, TMAX], F32) for i in range(2)]
        NSLOT = 4
        wsl = [sb('wsl%d' % i, [128, 4096], BF16) for i in range(NSLOT)]
        xst = [sb('xst%d' % i, [128, D], F32) for i in range(2)]
        yst = xst
        uv = sb('uv', [128, 4, TMAX], F32)
        gt = sb('gt', [128, 4, TMAX], F32)
        vc = sb('vc', [128, 2, TMAX], F32)
        vn = sb('vn', [128, 2, TMAX], F32)
        vtok = sb('vtok', [128, 256], F32)
        vtokb = sb('vtokb', [128, 256], BF16)
        pbuf = sb('pbuf', [128, 2, 16 + TMAX], F32)
        pw1 = sb('pw1', [128, 2, 16 + TMAX], F32)
        pw2 = sb('pw2', [128, 2, 16 + TMAX], F32)
        dTb = sb('dTb', [128, 2, TMAX], BF16)
        cbuf = sb('cbuf', [128, 12, 3 + TMAX], F32)
        qkvs = sb('qkvs', [128, 12, TMAX], F32)
        sz = sb('sz', [128, 4, TMAX], F32)
        mix = sb('mix', [128, 8, TMAX], BF16)
        w8 = [sb('w8_%d' % l, [128, KC, 8], BF16) for l in range(L)]
        g_bg = sb('g_bg', [128, 8], F32)
        g_beta = sb('g_beta', [128, 4], F32)
        g_nbeta = sb('g_nbeta', [128, 4], F32)
        g_g = sb('g_g', [128, 4], F32)
        g_t4 = sb('g_t4', [128, 4], F32)
        g_gc = sb('g_gc', [128, 4], F32)
        g_egc = sb('g_egc', [128, 4], F32)
        g_bexp = sb('g_bexp', [128, 4], F32)
        g_ekg = sb('g_ekg', [128, 4], F32)
        g_gl = sb('g_gl', [128, 4], F32)
        g_trig = sb('g_trig', [128, 4, 128], F32)
        g_egrow = sb('g_egrow', [128, 4, 128], F32)
        g_A = sb('g_A', [128, 4, 128], F32)
        g_DT = sb('g_DT', [128, 4, 128], F32)
        g_NB = sb('g_NB', [128, 4, 128], F32)
        g_sq = g_trig[:, :, :].rearrange("p a b -> p (a b)").bitcast(BF16).rearrange("p (a b) -> p a b", a=8)
        g_rq = sb('g_rq', [128, 4, 128], F32)
        g_rk = sb('g_rk', [128, 4, 128], F32)
        g_knT = sb('g_knT', [128, 4, 128], GDT)
        g_qnT = sb('g_qnT', [128, 4, 128], GDT)
        g_qgT = sb('g_qgT', [128, 4, 128], GDT)
        g_kbg = sb('g_kbg', [128, 4, 128], GDT)
        g_kg = sb('g_kg', [128, 4, 128], GDT)
        g_vb = sb('g_vb', [128, 4, 128], GDT)
        g_u = g_NB
        g_wT = g_rk
        g_M = [sb('g_M%d' % i, [128, 4, 128], GDT) for i in range(2)]
        g_MT = [sb('g_MT%d' % i, [128, 4, 128], GDT) for i in range(2)]
        g_PT = [sb('g_PT%d' % i, [128, 4, 128], GDT) for i in range(2)]
        g_qkT = g_egrow
        g_vnew = g_rq
        g_o = g_A
        g_o2 = g_trig
        g_ss = sb('g_ss', [128, 4], F32)
        S_st = [sb('S%d' % l, [128, 4, 128], F32) for l in range(L)]
        chist = [sb('chist%d' % l, [128, 12, 3], F32) for l in range(L)]
        phist = [sb('phist%d' % l, [128, 2, 15], F32) for l in range(L)]
        mkT = [sb('mkT%d' % l, [128, 4, MEM], BF16) for l in range(L)]
        mvb = [sb('mvb%d' % l, [128, 2, 512], BF16) for l in range(L)]
        qTb = sb('qTb', [128, 4, TMAX], BF16)
        pTb = [sb('pTb%d' % i, [128, 2, TMAX], BF16) for i in range(2)]
        oTb = sb('oTb', [128, 4, TMAX], BF16)
        rsum = sb('rsum', [128, TMAX], F32)
        assert TMAX == 256
        memtok = qkvs[:, 0:8, :].rearrange("p a b -> p (a b)").rearrange("p (m f) -> p m f", m=2)
        memT = cbuf[:, 0:8, 0:MEM]
        memn = hns[0]
        kvst = gt[:, 0:2, :].rearrange("p a b -> p (a b)")
        sttok = cbuf[0:16, 0:6, :].rearrange("p a b -> p (a b)")[:, 0:1536]
        ident = sb('ident', [128, 128], F32)
        ones_b = sb('ones_b', [128, 128], BF16)
        ones_f = sb('ones_f', [128, 128], F32)
        mskSL = sb('mskSL', [128, 128], F32)
        mskUI = sb('mskUI', [128, 128], F32)
        tri = sb('tri', [128, 128], F32)
        bavg = sb('bavg', [128, 128], F32)
        rcfix = sb('rcfix', [128, 2, 16], F32)
        vtab = [sb('vtab%d' % l, [128, 128], F32) for l in range(L)]
        vcol = [sb('vcol%d' % l, [128, 128], F32) for l in range(L)]
        wsT = [sb('wsT%d' % l, [128, 4, 128], BF16) for l in range(L)]
        wstg = sb('wstg', [128, 4, 128], F32)
        sgub = [sb('sgub%d' % l, [128, 2, 128], F32) for l in range(L)]
        wpf = sb('wpf', [128, 2, 128], F32)
        wpb = [sb('wpb%d' % l, [128, 2, 128], BF16) for l in range(L)]
        dtb = [sb('dtb%d' % l, [128, 4], F32) for l in range(L)]
        negA = [sb('negA%d' % l, [128, 4], F32) for l in range(L)]
        PSB = [ps('psb%d' % i, [128, 512], F32) for i in range(8)]

        psctr = {0: 0, 1: 0}
        cur_par = [0]

        def psum(group):
            p = cur_par[0]
            i = 4 * p + psctr[p] % 4
            psctr[p] += 1
            return PSB[i], 'ps%d' % i

        VC_FFN1, VC_MIX, VC_XATT, VC_MEM, VC_FFN2 = 0, 8, 16, 24, 32
        VC_SGU, VC_PSC, VC_CONV, VC_ONORM, VC_FINAL = 40, 42, 44, 92, 93

        def cst(eng, fn, w):
            P.op(eng, fn, r=(), w=w)

        cst('pool', lambda e: e.memset(ident[:], 0.0), ['ident'])
        cst('pool', lambda e: e.affine_select(out=ident[:], in_=ident[:], pattern=[[-1, 128]], compare_op=ALU.not_equal,
                                               fill=1.0, base=0, channel_multiplier=1), ['ident'])
        cst('pool', lambda e: e.memset(ones_b[:], 1.0), ['ones_b'])
        cst('pool', lambda e: e.memset(ones_f[:], 1.0), ['ones_f'])
        cst('pool', lambda e: e.memset(mskSL[:], 1.0), ['mskSL'])
        cst('pool', lambda e: e.affine_select(out=mskSL[:], in_=mskSL[:], pattern=[[-1, 128]], compare_op=ALU.is_gt,
                                               fill=0.0, base=0, channel_multiplier=1), ['mskSL'])
        cst('pool', lambda e: e.memset(tri[:], 1.0), ['tri'])
        cst('pool', lambda e: e.affine_select(out=tri[:], in_=tri[:], pattern=[[1, 128]], compare_op=ALU.is_ge,
                                               fill=0.0, base=0, channel_multiplier=-1), ['tri'])
        cst('pool', lambda e: e.memset(bavg[:], 0.0), ['bavg'])
        cst('pool', lambda e: e.memset(bavg[0:64, 0:64], 1.0 / 64), ['bavg'])
        cst('pool', lambda e: e.memset(bavg[64:128, 64:128], 1.0 / 64), ['bavg'])
        for c in range(2):
            cst('pool', lambda e, c=c: e.iota(rcfix[:, c, :], pattern=[[1, 16]], base=1, channel_multiplier=0,
                                              allow_small_or_imprecise_dtypes=True), ['rcfix'])
        for c in range(2):
            for hf in range(2):
                wdw = float([2, 4, 8, 16][2 * c + hf])
                cst('pool', lambda e, c=c, hf=hf, wdw=wdw: e.tensor_scalar_min(
                    out=rcfix[64 * hf:64 * hf + 64, c, :], in0=rcfix[64 * hf:64 * hf + 64, c, :], scalar1=wdw), ['rcfix'])
        P.op('dve', lambda e: e.reciprocal(out=rcfix[:], in_=rcfix[:]), r=['rcfix'], w=['rcfix'])
        for l in range(L):
            cst('pool', lambda e, l=l: e.memset(vtab[l][:], 0.0), ['vtab%d' % l])

        cst('pool', lambda e: e.memset(pbuf[:], 0.0), ['pbuf'])
        cst('pool', lambda e: e.memset(pw1[:], 0.0), ['pw1'])
        cst('pool', lambda e: e.memset(pw2[:], 0.0), ['pw2'])
        cst('pool', lambda e: e.memset(cbuf[:], 0.0), ['cbuf'])
        def sload(out, in_, w):
            P.dma('sp', out, in_, r=(), w=w, key='setup')

        for l in range(L):
            vk = 'vtab%d' % l
            for nm, r0 in [('ffn1_norm', VC_FFN1), ('mix_norm', VC_MIX), ('xattn_norm', VC_XATT),
                           ('mem_norm', VC_MEM), ('ffn2_norm', VC_FFN2)]:
                sload(vtab[l][r0:r0 + 8, :], I[nm][l].rearrange("(k p) -> k p", p=128), [vk])
            sload(vtab[l][VC_SGU:VC_SGU + 2, :], I['sgu_norm'][l].rearrange("(k p) -> k p", p=128), [vk])
            sload(vtab[l][VC_PSC:VC_PSC + 2, :], I['pool_scale'][l].rearrange("(k p) -> k p", p=128), [vk])
            sload(vtab[l][VC_CONV:VC_CONV + 48, :], I['gdn_conv_w'][l].rearrange("j (k p) -> (j k) p", p=128), [vk])
            sload(vtab[l][VC_ONORM:VC_ONORM + 1, :], I['gdn_out_norm'][l].rearrange("(k p) -> k p", p=128), [vk])
            if l == 0:
                sload(vtab[l][VC_FINAL:VC_FINAL + 8, :], I['final_norm'].rearrange("(k p) -> k p", p=128), [vk])
            sload(dtb[l][:], I['gdn_dt_bias'][l].partition_broadcast(128), ['dtb%d' % l])
            sload(negA[l][:], I['gdn_a_log'][l].partition_broadcast(128), ['negA%d' % l])
            for h in range(4):
                sload(sgub[l][64 * (h % 2):64 * (h % 2) + 64, h // 2, :], I['sgu_b'][l, h].partition_broadcast(64), ['sgub%d' % l])
        P.finalize_group('setup')

        for l in range(L):
            P.dma('pool', w8[l][:], I['w_in'][l].rearrange("(k p) n -> p k n", p=128)[:, :, 2816:2824], r=(), w=['w8_%d' % l], key='w8')
        P.finalize_group('w8')

        def cast_k1024(nm, l, ncols):
            src = I[nm][l].rearrange("(k p) n -> p k n", p=128)
            npan = (ncols + 511) // 512
            for j in range(npan):
                nw = min(512, ncols - j * 512)
                P.dma('pool', SCR[nm][l, j][:, :, 0:nw], src[:, :, j * 512:j * 512 + nw], r=(), w=['scr_%s_%d' % (nm, l)],
                      key='scr_%s_%d' % (nm, l))

        def cast_down(nm, l):
            src = I[nm][l].rearrange("(k p) n -> p k n", p=128)
            for m in range(8):
                for hf in range(2):
                    P.dma('pool', SCR[nm][l, m][:, 11 * hf:11 * hf + 11, :], src[:, 11 * hf:11 * hf + 11, m * 128:(m + 1) * 128],
                          r=(), w=['scr_%s_%d' % (nm, l)], key='scr_%s_%d' % (nm, l))

        def cast_mo(l):
            src = I['w_mo'][l].rearrange("(k p) n -> p k n", p=128)
            P.dma('pool', SCR['w_mo'][l, 0], src, r=(), w=['scr_w_mo_%d' % l], key='scr_w_mo_%d' % l)

        for l in range(L if STAGE >= 1 else 0):
            cast_k1024('ffn1_w_gate', l, DFF)
            cast_k1024('ffn1_w_up', l, DFF)
            cast_down('ffn1_w_down', l)
            cast_k1024('w_in', l, DFF)
            cast_k1024('w_mk', l, 512)
            cast_k1024('w_mv', l, 512)
            cast_k1024('w_out', l, D)
            cast_k1024('w_mq', l, 512)
            cast_mo(l)
            cast_k1024('ffn2_w_gate', l, DFF)
            cast_k1024('ffn2_w_up', l, DFF)
            cast_down('ffn2_w_down', l)
        for l in range(L if STAGE >= 1 else 0):
            for nm in ['ffn1_w_gate', 'ffn1_w_up', 'ffn1_w_down', 'w_in', 'w_mk', 'w_mv', 'w_out', 'w_mq', 'w_mo',
                       'ffn2_w_gate', 'ffn2_w_up', 'ffn2_w_down']:
                P.finalize_group('scr_%s_%d' % (nm, l))

        wctr = [0]

        def wload(nm, l, j, shape):
            h = Holder()

            def assign():
                for tries in range(NSLOT):
                    i = (wctr[0] + tries) % NSLOT
                    if not P.slot_live.get(i, False):
                        break
                else:
                    return False
                wctr[0] = i + 1
                h.slot = i
                if h.refs > 0:
                    P.slot_live[i] = True
                k, n = shape
                h.view = wsl[i][:, 0:k * n].rearrange("p (k n) -> p k n", k=k)
                h.key = 'wsl%d' % i
                P.dma('sp', h.view, SCR[nm][l, j][:, 0:k, 0:n], r=['scr_%s_%d' % (nm, l)], w=['wsl%d' % i])
            P.custom(assign)
            return LazyView(h), LazyKey(h)

        for l in range(L):
            pt, pk = psum('misc')
            P.op('pe', lambda e, l=l, pt=pt: e.transpose(out=pt[:, 0:128], in_=vtab[l][:, :], identity=ident[:, :]),
                 r=['vtab%d' % l, 'ident'], w=[pk])
            P.op('dve', lambda e, l=l, pt=pt: e.tensor_copy(out=vcol[l][:], in_=pt[:, 0:128]), r=[pk], w=['vcol%d' % l])
            P.op('act', lambda e, l=l: e.activation(out=negA[l][:], in_=negA[l][:], func=AF.Exp), r=['negA%d' % l], w=['negA%d' % l])
            P.op('dve', lambda e, l=l: e.tensor_scalar(out=negA[l][:], in0=negA[l][:], scalar1=-1.0, scalar2=None, op0=ALU.mult),
                 r=['negA%d' % l], w=['negA%d' % l])
            P.dma('sp', wstg[:], I['sgu_w'][l].rearrange("h i j -> i h j"), r=(), w=['wstg'])
            pt, pk = psum('misc')
            for h in range(4):
                P.op('pe', lambda e, h=h, pt=pt: e.transpose(out=pt[:, h * 128:(h + 1) * 128], in_=wstg[:, h, :], identity=ident[:, :]),
                     r=['wstg', 'ident'], w=[pk])
            P.op('dve', lambda e, l=l, pt=pt: e.tensor_copy(out=wsT[l][:], in_=pt[:, :].rearrange("p (h i) -> p h i", h=4)),
                 r=[pk], w=['wsT%d' % l])
            P.op('dve', lambda e, l=l: e.memset(wsT[l][64:128, :, 0:64], 0.0), r=(), w=['wsT%d' % l])
            P.op('dve', lambda e: e.memset(wpf[:], 0.0), r=(), w=['wpf'])
            for g in range(4):
                P.dma('sp', wpf[64 * (g % 2):64 * (g % 2) + 64, g // 2, 64 * (g % 2):64 * (g % 2) + 64], I['pool_w'][l, g],
                      r=(), w=['wpf'])
            P.op('dve', lambda e, l=l: e.tensor_copy(out=wpb[l][:], in_=wpf[:]), r=['wpf'], w=['wpb%d' % l])

        def rmsnorm(T, gl, gcol0, out, outkey, par=0):
            xT = xTs[par]

            def xk(k):
                return 'xT%d_%d' % (par, k)
            P.mark('AB')
            P.op('act', lambda e: e.activation(out=sqb[:, :, 0:T], in_=xT[:, :, 0:T], func=AF.Square),
                 r=[xk(k) for k in range(KC)], w=['sqb'])
            pt, pk = psum('misc')
            for k in range(KC):
                P.op('pe', lambda e, k=k, pt=pt: e.matmul(pt[:, 0:T], lhsT=ones_b[:, :], rhs=sqb[:, k, 0:T], start=(k == 0), stop=(k == KC - 1)),
                     r=['sqb', 'ones_b'], w=[pk])
            P.op('act', lambda e, pt=pt: e.activation(out=rstd[:, 0:T], in_=pt[:, 0:T], func=AF.Ln, bias=EPS, scale=1.0 / D),
                 r=[pk], w=['rstd'])
            P.op('act', lambda e: e.activation(out=rstd[:, 0:T], in_=rstd[:, 0:T], func=AF.Exp, scale=-0.5), r=['rstd'], w=['rstd'])
            for k in range(KC):
                P.op('dve', lambda e, k=k: e.scalar_tensor_tensor(out=out[:, k, 0:T], in0=xT[:, k, 0:T],
                                                                     scalar=vcol[gl][:, gcol0 + k:gcol0 + k + 1], in1=rstd[:, 0:T],
                                                                     op0=ALU.mult, op1=ALU.mult),
                     r=[xk(k), 'rstd', 'vcol%d' % gl], w=[outkey])
            P.mark('AE')

        def ffn(l, which, T, par=0):
            xT = xTs[par]
            hn = hns[par]
            hk = 'hn%d' % par

            def xk(k):
                return 'xT%d_%d' % (par, k)
            pre = 'ffn%d_' % which
            rmsnorm(T, l, VC_FFN1 if which == 1 else VC_FFN2, hn, hk, par)
            si = 0
            for j in range(6):
                nch = 4 if j < 5 else 2
                wg, wgk = wload(pre + 'w_gate', l, j, (8, nch * 128))
                wu, wuk = wload(pre + 'w_up', l, j, (8, nch * 128))
                for c in range(nch):
                    n = j * 4 + c
                    if FFN_SHARED:
                        pgu, pgk = psum('big')
                        puk = pgk
                        pg = pgu[:, 0:256]
                        pu = pgu[:, 256:512]
                    else:
                        pg, pgk = psum('big')
                        pu, puk = psum('big')
                    for k in range(KC):
                        P.op('pe', lambda e, k=k, c=c, pg=pg, wg=wg: e.matmul(pg[:, 0:T], lhsT=wg[:, k, c * 128:(c + 1) * 128], rhs=hn[:, k, 0:T],
                                                                             start=(k == 0), stop=(k == KC - 1)), r=[hk, wgk], w=[pgk])
                    for k in range(KC):
                        P.op('pe', lambda e, k=k, c=c, pu=pu, wu=wu: e.matmul(pu[:, 0:T], lhsT=wu[:, k, c * 128:(c + 1) * 128], rhs=hn[:, k, 0:T],
                                                                             start=(k == 0), stop=(k == KC - 1)), r=[hk, wuk], w=[puk])
                    st_ = sgt[si % 2]
                    sk = 'sgt%d' % (si % 2)
                    si += 1
                    if FFN_EXPLN:
                        P.op('act', lambda e, pg=pg, st_=st_: e.activation(out=st_[:, 0:T], in_=pg[:, 0:T], func=AF.Exp, scale=-1.0), r=[pgk], w=[sk])
                        P.op('act', lambda e, st_=st_: e.activation(out=st_[:, 0:T], in_=st_[:, 0:T], func=AF.Ln, bias=1.0), r=[sk], w=[sk])
                        P.op('act', lambda e, st_=st_: e.activation(out=st_[:, 0:T], in_=st_[:, 0:T], func=AF.Exp, scale=-1.0), r=[sk], w=[sk])
                        P.op('dve', lambda e, pg=pg, st_=st_: e.tensor_tensor(out=st_[:, 0:T], in0=pg[:, 0:T], in1=st_[:, 0:T], op=ALU.mult), r=[pgk, sk], w=[sk])
                    else:
                        P.op('act', lambda e, pg=pg, st_=st_: e.activation(out=st_[:, 0:T], in_=pg[:, 0:T], func=AF.Silu), r=[pgk], w=[sk])
                    P.op('dve', lambda e, n=n, pu=pu, st_=st_: e.tensor_tensor(out=act[:, n, 0:T], in0=pu[:, 0:T], in1=st_[:, 0:T], op=ALU.mult),
                         r=[puk, sk], w=['act%d' % n])
            for m in range(KC):
                wd, wdk = wload(pre + 'w_down', l, m, (22, 128))
                pd, pdk = psum('big')
                for k in range(FC):
                    P.op('pe', lambda e, k=k, pd=pd, wd=wd: e.matmul(pd[:, 0:T], lhsT=wd[:, k, :], rhs=act[:, k, 0:T],
                                                                   start=(k == 0), stop=(k == FC - 1)), r=['act%d' % k, wdk], w=[pdk])
                P.op('dve', lambda e, m=m, pd=pd: e.scalar_tensor_tensor(out=xT[:, m, 0:T], in0=pd[:, 0:T], scalar=0.5, in1=xT[:, m, 0:T],
                                                                       op0=ALU.mult, op1=ALU.add), r=[pdk, xk(m)], w=[xk(m)])

        def proj_residual(l, nm, kin, src, srckeys, T, npan, par=0):
            xT = xTs[par]

            def xk(k):
                return 'xT%d_%d' % (par, k)
            for j in range(npan):
                if nm == 'w_mo':
                    wv, wk = wload(nm, l, 0, (4, 1024)) if j == 0 else (wv, wk)
                else:
                    wv, wk = wload(nm, l, j, (8, 512))
                for c in range(4):
                    m = j * 4 + c
                    pd, pdk = psum('big')
                    kl = list(range(kin))
                    if nm == 'w_out' and DBG_MIX is not None:
                        kl = [k for k in kl if {0: 'a', 1: 'a', 2: 'b', 3: 'b'}.get(k, 'c') in DBG_MIX]
                    for k in kl:
                        P.op('pe', lambda e, k=k, m=m, c=c, pd=pd, wv=wv: e.matmul(
                            pd[:, 0:T], lhsT=(wv[:, k, m * 128:(m + 1) * 128] if nm == 'w_mo' else wv[:, k, c * 128:(c + 1) * 128]),
                            rhs=src[:, k, 0:T], start=(k == kl[0]), stop=(k == kl[-1])), r=list(srckeys) + [wk], w=[pdk])
                    P.op('dve', lambda e, m=m, pd=pd: e.tensor_tensor(out=xT[:, m, 0:T], in0=pd[:, 0:T], in1=xT[:, m, 0:T], op=ALU.add),
                         r=[pdk, xk(m)], w=[xk(m)])

        def gdn_chunk(l, c0, C, par=0):
            hn = hns[par]
            hk = 'hn%d' % par
            J = 6 if C == 128 else 5
            C4 = 4 * C
            P.op('act', lambda e: e.activation(out=g_sq[:, :, 0:C], in_=qkvs[:, 0:8, c0:c0 + C], func=AF.Square), r=['qkvs'], w=['g_trig'])
            pq, pqk = psum('misc')
            pqv = pq[:, 0:C4].rearrange("p (h c) -> p h c", h=4)
            P.op('pe', lambda e, pqv=pqv: e.matmul(pqv, lhsT=ones_b[:, :], rhs=g_sq[:, 0:4, 0:C], start=True, stop=True),
                 r=['ones_b', 'g_trig'], w=[pqk])
            pkk_, pkkk = psum('misc')
            pkv = pkk_[:, 0:C4].rearrange("p (h c) -> p h c", h=4)
            P.op('pe', lambda e, pkv=pkv: e.matmul(pkv, lhsT=ones_b[:, :], rhs=g_sq[:, 4:8, 0:C], start=True, stop=True),
                 r=['ones_b', 'g_trig'], w=[pkkk])
            P.op('act', lambda e, pqv=pqv: e.activation(out=g_rq[:, :, 0:C], in_=pqv, func=AF.Ln, bias=1e-6), r=[pqk], w=['g_rq'])
            P.op('act', lambda e, pkv=pkv: e.activation(out=g_rk[:, :, 0:C], in_=pkv, func=AF.Ln, bias=1e-6), r=[pkkk], w=['g_rk'])
            P.op('act', lambda e: e.activation(out=g_rq[:, :, 0:C], in_=g_rq[:, :, 0:C], func=AF.Exp, scale=-0.5), r=['g_rq'], w=['g_rq'])
            P.op('act', lambda e: e.activation(out=g_rk[:, :, 0:C], in_=g_rk[:, :, 0:C], func=AF.Exp, scale=-0.5), r=['g_rk'], w=['g_rk'])
            P.op('dve', lambda e: e.scalar_tensor_tensor(out=g_qnT[:, :, 0:C], in0=qkvs[:, 0:4, c0:c0 + C], scalar=float(128 ** -0.5),
                                                          in1=g_rq[:, :, 0:C], op0=ALU.mult, op1=ALU.mult), r=['qkvs', 'g_rq'], w=['g_qnT'])
            P.op('dve', lambda e: e.tensor_tensor(out=g_knT[:, :, 0:C], in0=qkvs[:, 4:8, c0:c0 + C], in1=g_rk[:, :, 0:C], op=ALU.mult),
                 r=['qkvs', 'g_rk'], w=['g_knT'])
            pt, pk = psum('misc')
            for k in range(KC):
                P.op('pe', lambda e, k=k, pt=pt: e.matmul(pt[0:C, 0:8], lhsT=hn[:, k, c0:c0 + C], rhs=w8[l][:, k, :],
                                                         start=(k == 0), stop=(k == KC - 1)), r=[hk, 'w8_%d' % l], w=[pk])
            P.op('act', lambda e, pt=pt: e.activation(out=g_beta[0:C, :], in_=pt[0:C, 0:4], func=AF.Exp, scale=-1.0), r=[pk], w=['g_beta'])
            P.op('act', lambda e: e.activation(out=g_beta[0:C, :], in_=g_beta[0:C, :], func=AF.Ln, bias=1.0), r=['g_beta'], w=['g_beta'])
            P.op('act', lambda e: e.activation(out=g_beta[0:C, :], in_=g_beta[0:C, :], func=AF.Exp, scale=-1.0), r=['g_beta'], w=['g_beta'])
            P.op('dve', lambda e, pt=pt: e.tensor_tensor(out=g_t4[0:C, :], in0=pt[0:C, 4:8], in1=dtb[l][0:C, :], op=ALU.add),
                 r=[pk, 'dtb%d' % l], w=['g_t4'])
            P.op('act', lambda e: e.activation(out=g_t4[0:C, :], in_=g_t4[0:C, :], func=AF.Exp), r=['g_t4'], w=['g_t4'])
            P.op('act', lambda e: e.activation(out=g_t4[0:C, :], in_=g_t4[0:C, :], func=AF.Ln, bias=1.0), r=['g_t4'], w=['g_t4'])
            P.op('dve', lambda e: e.tensor_tensor(out=g_g[0:C, :], in0=g_t4[0:C, :], in1=negA[l][0:C, :], op=ALU.mult),
                 r=['g_t4', 'negA%d' % l], w=['g_g'])
            P.op('dve', lambda e: e.tensor_scalar(out=g_nbeta[0:C, :], in0=g_beta[0:C, :], scalar1=-1.0, scalar2=None, op0=ALU.mult),
                 r=['g_beta'], w=['g_nbeta'])
            pgc, pgck = psum('misc')
            P.op('pe', lambda e, pgc=pgc: e.matmul(pgc[0:C, 0:4], lhsT=tri[0:C, 0:C], rhs=g_g[0:C, :], start=True, stop=True),
                 r=['tri', 'g_g'], w=[pgck])
            P.op('act', lambda e, pgc=pgc: e.copy(out=g_gc[0:C, :], in_=pgc[0:C, 0:4]), r=[pgck], w=['g_gc'])
            P.op('act', lambda e, pgc=pgc: e.activation(out=g_egc[0:C, :], in_=pgc[0:C, 0:4], func=AF.Exp), r=[pgck], w=['g_egc'])
            for h in range(4):
                P.op('dve', lambda e, h=h: e.tensor_scalar(out=g_trig[0:C, h, 0:C], in0=tri[0:C, 0:C], scalar1=g_g[0:C, h:h + 1], scalar2=None,
                                                          op0=ALU.mult), r=['tri', 'g_g'], w=['g_trig'])
            prow, prowk = psum('misc')
            prv = prow[:, 0:C4].rearrange("p (h c) -> p h c", h=4)
            P.op('pe', lambda e, prv=prv: e.matmul(prv, lhsT=ones_f[0:C, :], rhs=g_trig[0:C, :, 0:C], start=True, stop=True),
                 r=['ones_f', 'g_trig'], w=[prowk])
            P.op('act', lambda e, prv=prv: e.activation(out=g_egrow[:, :, 0:C], in_=prv, func=AF.Exp), r=[prowk], w=['g_egrow'])
            P.op('act', lambda e, prv=prv: e.activation(out=g_gl[:, :], in_=prv[:, :, C - 1], func=AF.Exp), r=[prowk], w=['g_gl'])
            P.op('dve', lambda e, prv=prv: e.tensor_tensor(out=g_ekg[0:C, :], in0=prv[0:C, :, C - 1], in1=g_gc[0:C, :], op=ALU.subtract),
                 r=[prowk, 'g_gc'], w=['g_ekg'])
            P.op('act', lambda e: e.activation(out=g_ekg[0:C, :], in_=g_ekg[0:C, :], func=AF.Exp), r=['g_ekg'], w=['g_ekg'])
            P.op('dve', lambda e: e.tensor_tensor(out=g_bexp[0:C, :], in0=g_beta[0:C, :], in1=g_egc[0:C, :], op=ALU.mult),
                 r=['g_beta', 'g_egc'], w=['g_bexp'])
            P.op('dve', lambda e, prv=prv: e.tensor_tensor(out=g_A[0:C, :, 0:C], in0=prv[0:C, :, :],
                                                          in1=g_gc[0:C, :].unsqueeze(2).to_broadcast([C, 4, C]), op=ALU.subtract),
                 r=[prowk, 'g_gc'], w=['g_A'])
            P.op('dve', lambda e: e.tensor_scalar(out=g_DT[0:C, :, 0:C], in0=g_A[0:C, :, 0:C], scalar1=0.0, scalar2=None, op0=ALU.min),
                 r=['g_A'], w=['g_DT'])
            P.op('act', lambda e: e.activation(out=g_DT[0:C, :, 0:C], in_=g_DT[0:C, :, 0:C], func=AF.Exp), r=['g_DT'], w=['g_DT'])
            P.op('dve', lambda e: e.tensor_tensor(out=g_DT[0:C, :, 0:C], in0=g_DT[0:C, :, 0:C],
                                                  in1=tri[0:C, 0:C].unsqueeze(1).to_broadcast([C, 4, C]), op=ALU.mult),
                 r=['g_DT', 'tri'], w=['g_DT'])
            P.op('dve', lambda e: e.tensor_scalar(out=g_NB[0:C, :, 0:C], in0=g_A[0:C, :, 0:C], scalar1=0.0, scalar2=None, op0=ALU.max),
                 r=['g_A'], w=['g_NB'])
            P.op('act', lambda e: e.activation(out=g_NB[0:C, :, 0:C], in_=g_NB[0:C, :, 0:C], func=AF.Exp, scale=-1.0), r=['g_NB'], w=['g_NB'])
            P.op('dve', lambda e: e.tensor_tensor(out=g_NB[0:C, :, 0:C], in0=g_NB[0:C, :, 0:C],
                                                  in1=mskSL[0:C, 0:C].unsqueeze(1).to_broadcast([C, 4, C]), op=ALU.mult),
                 r=['g_NB', 'mskSL'], w=['g_NB'])
            P.op('dve', lambda e: e.tensor_tensor(out=g_NB[0:C, :, 0:C], in0=g_NB[0:C, :, 0:C],
                                                  in1=g_nbeta[0:C, :].unsqueeze(2).to_broadcast([C, 4, C]), op=ALU.mult),
                 r=['g_NB', 'g_nbeta'], w=['g_NB'])
            P.op('dve', lambda e: e.tensor_tensor(out=g_qgT[:, :, 0:C], in0=g_qnT[:, :, 0:C], in1=g_egrow[:, :, 0:C], op=ALU.mult),
                 r=['g_qnT', 'g_egrow'], w=['g_qgT'])
            ptk, ptkk = psum('misc')
            for h in range(4):
                P.op('pe', lambda e, h=h, ptk=ptk: e.transpose(out=ptk[0:C, h * 128:(h + 1) * 128], in_=g_knT[:, h, 0:C], identity=ident[:, :]),
                     r=['g_knT', 'ident'], w=[ptkk])
            ptkv = ptk[0:C, :].rearrange("p (h d) -> p h d", h=4)
            P.op('dve', lambda e, ptkv=ptkv: e.tensor_tensor(out=g_kbg[0:C, :, :], in0=ptkv, in1=g_bexp[0:C, :].unsqueeze(2).to_broadcast([C, 4, 128]),
                                                            op=ALU.mult), r=[ptkk, 'g_bexp'], w=['g_kbg'])
            P.op('dve', lambda e, ptkv=ptkv: e.tensor_tensor(out=g_kg[0:C, :, :], in0=ptkv, in1=g_ekg[0:C, :].unsqueeze(2).to_broadcast([C, 4, 128]),
                                                            op=ALU.mult), r=[ptkk, 'g_ekg'], w=['g_kg'])
            ptv, ptvk = psum('misc')
            for h in range(4):
                P.op('pe', lambda e, h=h, ptv=ptv: e.transpose(out=ptv[0:C, h * 128:(h + 1) * 128], in_=qkvs[:, 8 + h, c0:c0 + C], identity=ident[:, :]),
                     r=['qkvs', 'ident'], w=[ptvk])
            ptvv = ptv[0:C, :].rearrange("p (h d) -> p h d", h=4)
            P.op('dve', lambda e, ptvv=ptvv: e.tensor_tensor(out=g_vb[0:C, :, :], in0=ptvv, in1=g_beta[0:C, :].unsqueeze(2).to_broadcast([C, 4, 128]),
                                                            op=ALU.mult), r=[ptvk, 'g_beta'], w=['g_vb'])
            pk2, pk2k = psum('misc')
            for h in range(4):
                P.op('pe', lambda e, h=h, pk2=pk2: e.matmul(pk2[0:C, h * C:(h + 1) * C], lhsT=g_knT[:, h, 0:C], rhs=g_knT[:, h, 0:C], start=True, stop=True),
                     r=['g_knT'], w=[pk2k])
            pk2v = pk2[0:C, 0:C4].rearrange("p (h c) -> p h c", h=4)
            P.op('dve', lambda e, pk2v=pk2v: e.tensor_tensor(out=g_M[0][0:C, :, 0:C], in0=pk2v, in1=g_NB[0:C, :, 0:C], op=ALU.mult),
                 r=[pk2k, 'g_NB'], w=['g_M0'])
            pq2, pq2k = psum('misc')
            for h in range(4):
                P.op('pe', lambda e, h=h, pq2=pq2: e.matmul(pq2[0:C, h * C:(h + 1) * C], lhsT=g_knT[:, h, 0:C], rhs=g_qnT[:, h, 0:C], start=True, stop=True),
                     r=['g_knT', 'g_qnT'], w=[pq2k])
            pq2v = pq2[0:C, 0:C4].rearrange("p (h c) -> p h c", h=4)
            P.op('dve', lambda e, pq2v=pq2v: e.tensor_tensor(out=g_qkT[0:C, :, 0:C], in0=pq2v, in1=g_DT[0:C, :, 0:C], op=ALU.mult),
                 r=[pq2k, 'g_DT'], w=['g_egrow'])
            pnt, pntk = psum('misc')
            for h in range(4):
                P.op('pe', lambda e, h=h, pnt=pnt: e.transpose(out=pnt[0:C, h * C:(h + 1) * C], in_=g_M[0][0:C, h, 0:C], identity=ident[0:C, 0:C]),
                     r=['g_M0', 'ident'], w=[pntk])
            pntv = pnt[0:C, 0:C4].rearrange("p (h c) -> p h c", h=4)
            P.op('act', lambda e, pntv=pntv: e.copy(out=g_MT[0][0:C, :, 0:C], in_=pntv), r=[pntk], w=['g_MT0'])
            P.op('dve', lambda e, pntv=pntv: e.tensor_tensor(out=g_PT[0][0:C, :, 0:C], in0=pntv,
                                                            in1=ident[0:C, 0:C].unsqueeze(1).to_broadcast([C, 4, C]), op=ALU.add),
                 r=[pntk, 'ident'], w=['g_PT0'])
            cur = 0
            for j in range(1, J + 1):
                nxt = 1 - cur
                last = (j == J)
                pm, pmk = psum('misc')
                for h in range(4):
                    P.op('pe', lambda e, h=h, pm=pm, cur=cur: e.matmul(pm[0:C, h * C:(h + 1) * C], lhsT=g_MT[cur][0:C, h, 0:C], rhs=g_M[cur][0:C, h, 0:C],
                                                                      start=True, stop=True), r=['g_MT%d' % cur, 'g_M%d' % cur], w=[pmk])
                pmv = pm[0:C, 0:C4].rearrange("p (h c) -> p h c", h=4)
                if not last:
                    pmt, pmtk = psum('misc')
                    for h in range(4):
                        P.op('pe', lambda e, h=h, pmt=pmt, cur=cur: e.matmul(pmt[0:C, h * C:(h + 1) * C], lhsT=g_M[cur][0:C, h, 0:C], rhs=g_MT[cur][0:C, h, 0:C],
                                                                            start=True, stop=True), r=['g_MT%d' % cur, 'g_M%d' % cur], w=[pmtk])
                    pmtv = pmt[0:C, 0:C4].rearrange("p (h c) -> p h c", h=4)
                P.op('dve', lambda e, pmv=pmv, nxt=nxt: e.tensor_copy(out=g_M[nxt][0:C, :, 0:C], in_=pmv), r=[pmk], w=['g_M%d' % nxt])
                if not last:
                    P.op('act', lambda e, pmtv=pmtv, nxt=nxt: e.copy(out=g_MT[nxt][0:C, :, 0:C], in_=pmtv), r=[pmtk], w=['g_MT%d' % nxt])
                pp, ppk = psum('misc')
                for h in range(4):
                    P.op('pe', lambda e, h=h, pp=pp, cur=cur, nxt=nxt: e.matmul(pp[0:C, h * C:(h + 1) * C], lhsT=g_M[nxt][0:C, h, 0:C], rhs=g_PT[cur][0:C, h, 0:C],
                                                                               start=True, stop=True), r=['g_M%d' % nxt, 'g_PT%d' % cur], w=[ppk])
                ppv = pp[0:C, 0:C4].rearrange("p (h c) -> p h c", h=4)
                P.op('dve', lambda e, ppv=ppv, cur=cur, nxt=nxt: e.tensor_tensor(out=g_PT[nxt][0:C, :, 0:C], in0=ppv, in1=g_PT[cur][0:C, :, 0:C], op=ALU.add),
                     r=[ppk, 'g_PT%d' % cur], w=['g_PT%d' % nxt])
                cur = nxt
            TT = g_PT[cur]
            TTk = 'g_PT%d' % cur
            pu, puk = psum('misc')
            for h in range(4):
                P.op('pe', lambda e, h=h, pu=pu: e.matmul(pu[0:C, h * 128:(h + 1) * 128], lhsT=TT[0:C, h, 0:C], rhs=g_vb[0:C, h, :], start=True, stop=True),
                     r=[TTk, 'g_vb'], w=[puk])
            P.op('act', lambda e, pu=pu: e.copy(out=g_u[0:C, :, :], in_=pu[0:C, :].rearrange("p (h d) -> p h d", h=4)), r=[puk], w=['g_NB'])
            pw, pwk = psum('misc')
            for h in range(4):
                P.op('pe', lambda e, h=h, pw=pw: e.matmul(pw[:, h * C:(h + 1) * C], lhsT=g_kbg[0:C, h, :], rhs=TT[0:C, h, 0:C], start=True, stop=True),
                     r=[TTk, 'g_kbg'], w=[pwk])
            P.op('dve', lambda e, pw=pw: e.tensor_copy(out=g_wT[:, :, 0:C], in_=pw[:, 0:C4].rearrange("p (h c) -> p h c", h=4)), r=[pwk], w=['g_rk'])
            S = S_st[l]
            Sk = 'S%d' % l
            pws, pwsk = psum('misc')
            for h in range(4):
                P.op('pe', lambda e, h=h, pws=pws: e.matmul(pws[0:C, h * 128:(h + 1) * 128], lhsT=g_wT[:, h, 0:C], rhs=S[:, h, :], start=True, stop=True),
                     r=['g_rk', Sk], w=[pwsk])
            P.op('dve', lambda e, pws=pws: e.tensor_tensor(out=g_vnew[0:C, :, :], in0=g_u[0:C, :, :], in1=pws[0:C, :].rearrange("p (h d) -> p h d", h=4),
                                                          op=ALU.subtract), r=[pwsk, 'g_NB'], w=['g_rq'])
            po, pok = psum('misc')
            for h in range(4):
                P.op('pe', lambda e, h=h, po=po: e.matmul(po[0:C, h * 128:(h + 1) * 128], lhsT=g_qgT[:, h, 0:C], rhs=S[:, h, :], start=True, stop=False),
                     r=['g_qgT', Sk], w=[pok])
                P.op('pe', lambda e, h=h, po=po: e.matmul(po[0:C, h * 128:(h + 1) * 128], lhsT=g_qkT[0:C, h, 0:C], rhs=g_vnew[0:C, h, :], start=False, stop=True),
                     r=['g_egrow', 'g_rq'], w=[pok])
            psn, psnk = psum('misc')
            for h in range(4):
                P.op('pe', lambda e, h=h, psn=psn: e.matmul(psn[:, h * 128:(h + 1) * 128], lhsT=g_kg[0:C, h, :], rhs=g_vnew[0:C, h, :], start=True, stop=True),
                     r=['g_kg', 'g_rq'], w=[psnk])
            P.op('dve', lambda e: e.tensor_tensor(out=S[:, :, :], in0=S[:, :, :], in1=g_gl[:, :].unsqueeze(2).to_broadcast([128, 4, 128]), op=ALU.mult),
                 r=[Sk, 'g_gl'], w=[Sk])
            P.op('dve', lambda e, psn=psn: e.tensor_tensor(out=S[:, :, :], in0=psn[:, :].rearrange("p (h d) -> p h d", h=4), in1=S[:, :, :], op=ALU.add),
                 r=[Sk, psnk], w=[Sk])
            pov = po[0:C, :].rearrange("p (h d) -> p h d", h=4)
            P.op('act', lambda e, pov=pov: e.copy(out=g_o[0:C, :, :], in_=pov), r=[pok], w=['g_A'])
            P.op('dve', lambda e: e.tensor_tensor(out=g_o2[0:C, :, :], in0=g_o[0:C, :, :], in1=g_o[0:C, :, :], op=ALU.mult), r=['g_A'], w=['g_trig'])
            P.op('dve', lambda e: e.reduce_sum(out=g_ss[0:C, :], in_=g_o2[0:C, :, :], axis=AX.X), r=['g_trig'], w=['g_ss'])
            P.op('act', lambda e: e.activation(out=g_ss[0:C, :], in_=g_ss[0:C, :], func=AF.Ln, bias=EPS, scale=1.0 / 128), r=['g_ss'], w=['g_ss'])
            P.op('act', lambda e: e.activation(out=g_ss[0:C, :], in_=g_ss[0:C, :], func=AF.Exp, scale=-0.5), r=['g_ss'], w=['g_ss'])
            P.op('dve', lambda e: e.tensor_tensor(out=g_o2[0:C, :, :], in0=g_o[0:C, :, :], in1=g_ss[0:C, :].unsqueeze(2).to_broadcast([C, 4, 128]), op=ALU.mult),
                 r=['g_A', 'g_ss'], w=['g_trig'])
            pot, potk = psum('misc')
            for h in range(4):
                P.op('pe', lambda e, h=h, pot=pot: e.transpose(out=pot[:, h * C:(h + 1) * C], in_=g_o2[0:C, h, :], identity=ident[0:C, 0:C]),
                     r=['g_trig', 'ident'], w=[potk])
            P.op('dve', lambda e, pot=pot: e.scalar_tensor_tensor(out=mix[:, 4:8, c0:c0 + C], in0=pot[:, 0:C4].rearrange("p (h c) -> p h c", h=4),
                                                                 scalar=vcol[l][:, VC_ONORM:VC_ONORM + 1], in1=sz[:, :, c0:c0 + C],
                                                                 op0=ALU.mult, op1=ALU.mult), r=[potk, 'sz', 'vcol%d' % l], w=['mix_c'])

        def mixer(l, T, C, first_tile, is_sample, s_idx, par=0):
            hn = hns[par]
            hk = 'hn%d' % par
            rmsnorm(T, l, VC_MIX, hn, hk, par)
            for j in range(6):
                nch = 4 if j < 5 else 2
                wv, wk = wload('w_in', l, j, (8, nch * 128))
                for c in range(nch):
                    n = j * 4 + c
                    pp, ppk = psum('big')
                    for k in range(KC):
                        P.op('pe', lambda e, k=k, c=c, pp=pp, wv=wv: e.matmul(pp[:, 0:T], lhsT=wv[:, k, c * 128:(c + 1) * 128], rhs=hn[:, k, 0:T],
                                                                             start=(k == 0), stop=(k == KC - 1)), r=[hk, wk], w=[ppk])
                    if n < 4:
                        P.op('act', lambda e, n=n, pp=pp: e.copy(out=uv[:, n, 0:T], in_=pp[:, 0:T]), r=[ppk], w=['uv'])
                    elif n < 6:
                        P.op('act', lambda e, n=n, pp=pp: e.copy(out=pbuf[:, n - 4, 16:16 + T], in_=pp[:, 0:T]), r=[ppk], w=['pbuf'])
                    elif n < 18:
                        P.op('act', lambda e, n=n, pp=pp: e.copy(out=cbuf[:, n - 6, 3:3 + T], in_=pp[:, 0:T]), r=[ppk], w=['cbuf'])
                    else:
                        P.op('act', lambda e, n=n, pp=pp: e.copy(out=sz[:, n - 18, 0:T], in_=pp[:, 0:T]), r=[ppk], w=['sz'])
            if MIXSTOP <= 1:
                return
            P.mark('HEAVY1')
            def sgu_stop(n):
                if SGU_STOP <= n:
                    P.op = lambda *a, **k: None
                    P.dma = lambda *a, **k: None
            P.op('dve', lambda e: e.tensor_tensor(out=gt[:, :, 0:T], in0=uv[:, :, 0:T], in1=uv[:, :, 0:T], op=ALU.mult), r=['uv'], w=['gt'])
            P.op('dve', lambda e: e.tensor_scalar(out=gt[:, :, 0:T], in0=gt[:, :, 0:T], scalar1=0.044715, scalar2=1.0, op0=ALU.mult, op1=ALU.add),
                 r=['gt'], w=['gt'])
            P.op('dve', lambda e: e.tensor_tensor(out=gt[:, :, 0:T], in0=gt[:, :, 0:T], in1=uv[:, :, 0:T], op=ALU.mult), r=['gt', 'uv'], w=['gt'])
            P.op('act', lambda e: e.activation(out=gt[:, :, 0:T], in_=gt[:, :, 0:T], func=AF.Exp, scale=-1.5957691216057308), r=['gt'], w=['gt'])
            P.op('act', lambda e: e.activation(out=gt[:, :, 0:T], in_=gt[:, :, 0:T], func=AF.Ln, bias=1.0), r=['gt'], w=['gt'])
            P.op('act', lambda e: e.activation(out=gt[:, :, 0:T], in_=gt[:, :, 0:T], func=AF.Exp, scale=-1.0), r=['gt'], w=['gt'])
            P.op('dve', lambda e: e.tensor_tensor(out=uv[:, :, 0:T], in0=uv[:, :, 0:T], in1=gt[:, :, 0:T], op=ALU.mult), r=['gt', 'uv'], w=['uv'])
            sgu_stop(1)
            for c in range(2):
                pm, pmk = psum('misc')
                P.op('pe', lambda e, c=c, pm=pm: e.matmul(pm[:, 0:T], lhsT=bavg[:, :], rhs=uv[:, 2 + c, 0:T], start=True, stop=True), r=['bavg', 'uv'], w=[pmk])
                P.op('dve', lambda e, c=c, pm=pm: e.tensor_tensor(out=vc[:, c, 0:T], in0=uv[:, 2 + c, 0:T], in1=pm[:, 0:T], op=ALU.subtract),
                     r=[pmk, 'uv'], w=['vc'])
                P.op('dve', lambda e, c=c: e.tensor_tensor(out=gt[:, c, 0:T], in0=vc[:, c, 0:T], in1=vc[:, c, 0:T], op=ALU.mult), r=['vc'], w=['gt'])
                pv, pvk = psum('misc')
                P.op('pe', lambda e, c=c, pv=pv: e.matmul(pv[:, 0:T], lhsT=bavg[:, :], rhs=gt[:, c, 0:T], start=True, stop=True), r=['bavg', 'gt'], w=[pvk])
                P.op('act', lambda e, c=c, pv=pv: e.activation(out=gt[:, 2 + c, 0:T], in_=pv[:, 0:T], func=AF.Ln, bias=EPS), r=[pvk], w=['gt'])
                P.op('act', lambda e, c=c: e.activation(out=gt[:, 2 + c, 0:T], in_=gt[:, 2 + c, 0:T], func=AF.Exp, scale=-0.5), r=['gt'], w=['gt'])
                P.op('dve', lambda e, c=c: e.scalar_tensor_tensor(out=vn[:, c, 0:T], in0=vc[:, c, 0:T], scalar=vcol[l][:, VC_SGU + c:VC_SGU + c + 1],
                                                                  in1=gt[:, 2 + c, 0:T], op0=ALU.mult, op1=ALU.mult), r=['vc', 'gt', 'vcol%d' % l], w=['vn'])
            sgu_stop(2)
            CB = min(128, T)
            for b in range(T // CB):
                t0 = b * CB
                pt, pk = psum('misc')
                for c in range(2):
                    P.op('pe', lambda e, c=c, pt=pt, t0=t0: e.transpose(out=pt[0:CB, c * 128:(c + 1) * 128], in_=vn[:, c, t0:t0 + CB], identity=ident[:, :]),
                         r=['vn', 'ident'], w=[pk])
                P.op('dve', lambda e, pt=pt: e.tensor_copy(out=vtokb[0:CB, :], in_=pt[0:CB, 0:256]), r=[pk], w=['vtokb'])
                if is_sample:
                    P.op('act', lambda e, pt=pt: e.copy(out=vtok[0:CB, :], in_=pt[0:CB, 0:256]), r=[pk], w=['vtok'])
                    P.dma('pool', O['s_v'][l, s_idx], vtok[0:CB, :], r=['vtok'], w=())
                sgu_stop(3)
                for c in range(2):
                    pso, psok = psum('misc')
                    for hh in range(2):
                        P.op('pe', lambda e, c=c, hh=hh, pso=pso: e.matmul(pso[:, hh * 128:hh * 128 + CB], lhsT=vtokb[0:CB, c * 128:(c + 1) * 128],
                                                                          rhs=wsT[l][0:CB, 2 * c + hh, 0:CB], start=True, stop=True),
                             r=['vtokb', 'wsT%d' % l], w=[psok])
                    sgu_stop(4)
                    for hh in range(2):
                        pr = slice(64 * hh, 64 * hh + 64)
                        P.op('dve', lambda e, c=c, hh=hh, pso=pso, pr=pr, t0=t0: e.tensor_tensor(out=gt[pr, c, t0:t0 + CB], in0=pso[pr, hh * 128:hh * 128 + CB],
                                                                                           in1=sgub[l][pr, c, 0:CB], op=ALU.add),
                             r=[psok, 'sgub%d' % l], w=['gt'])
                    P.op('dve', lambda e, c=c, t0=t0: e.tensor_tensor(out=mix[:, c, t0:t0 + CB], in0=gt[:, c, t0:t0 + CB], in1=uv[:, c, t0:t0 + CB], op=ALU.mult),
                         r=['gt', 'uv'], w=['mix_a'])
            if 'op' in P.__dict__:
                del P.__dict__['op']
                del P.__dict__['dma']
            if MIXSTOP <= 2:
                return
            P.op('dve', lambda e: e.tensor_copy(out=pbuf[:, :, 1:16], in_=phist[l][:, :, :]), r=['phist%d' % l], w=['pbuf'])
            W_ = 16 + T
            P.op('dve', lambda e: e.tensor_tensor(out=pw1[:, :, 1:W_], in0=pbuf[:, :, 1:W_], in1=pbuf[:, :, 0:W_ - 1], op=ALU.add), r=['pbuf'], w=['pw1'])
            P.op('dve', lambda e: e.tensor_tensor(out=pw2[:, :, 3:W_], in0=pw1[:, :, 3:W_], in1=pw1[:, :, 1:W_ - 2], op=ALU.add), r=['pw1'], w=['pw2'])
            lo, hi = slice(0, 64), slice(64, 128)

            def pool_d(src, pr, c, wdw, srck):
                if first_tile:
                    P.op('dve', lambda e: e.tensor_tensor(out=gt[pr, c, 0:16], in0=src[pr, c, 16:32], in1=rcfix[pr, c, :], op=ALU.mult), r=[srck, 'rcfix'], w=['gt'])
                    P.op('dve', lambda e: e.tensor_tensor(out=gt[pr, c, 0:16], in0=gt[pr, c, 0:16], in1=pbuf[pr, c, 16:32], op=ALU.subtract), r=['gt', 'pbuf'], w=['gt'])
                    P.op('dve', lambda e: e.scalar_tensor_tensor(out=gt[pr, c, 16:T], in0=src[pr, c, 32:16 + T], scalar=1.0 / wdw, in1=pbuf[pr, c, 32:16 + T],
                                                                  op0=ALU.mult, op1=ALU.subtract), r=[srck, 'pbuf'], w=['gt'])
                else:
                    P.op('dve', lambda e: e.scalar_tensor_tensor(out=gt[pr, c, 0:T], in0=src[pr, c, 16:16 + T], scalar=1.0 / wdw, in1=pbuf[pr, c, 16:16 + T],
                                                                  op0=ALU.mult, op1=ALU.subtract), r=[srck, 'pbuf'], w=['gt'])
            pool_d(pw1, lo, 0, 2.0, 'pw1')
            pool_d(pw2, hi, 0, 4.0, 'pw2')
            P.op('dve', lambda e: e.tensor_tensor(out=pw1[:, 1, 7:W_], in0=pw2[:, 1, 7:W_], in1=pw2[:, 1, 3:W_ - 4], op=ALU.add), r=['pw2', 'pw1'], w=['pw1'])
            pool_d(pw1, lo, 1, 8.0, 'pw1')
            P.op('dve', lambda e: e.tensor_tensor(out=pw2[hi, 1, 15:W_], in0=pw1[hi, 1, 15:W_], in1=pw1[hi, 1, 7:W_ - 8], op=ALU.add), r=['pw1', 'pw2'], w=['pw2'])
            pool_d(pw2, hi, 1, 16.0, 'pw2')
            P.op('dve', lambda e: e.tensor_copy(out=dTb[:, :, 0:T], in_=gt[:, 0:2, 0:T]), r=['gt'], w=['dTb'])
            P.op('dve', lambda e: e.tensor_copy(out=phist[l][:, :, :], in_=pbuf[:, :, T + 1:T + 16]), r=['pbuf'], w=['phist%d' % l])
            for c in range(2):
                pp, ppk = psum('misc')
                P.op('pe', lambda e, c=c, pp=pp: e.matmul(pp[:, 0:T], lhsT=wpb[l][:, c, :], rhs=dTb[:, c, 0:T], start=True, stop=True), r=['wpb%d' % l, 'dTb'], w=[ppk])
                P.op('act', lambda e, c=c, pp=pp: e.mul(out=mix[:, 2 + c, 0:T], in_=pp[:, 0:T], mul=vcol[l][:, VC_PSC + c:VC_PSC + c + 1]),
                     r=[ppk, 'vcol%d' % l], w=['mix_b'])
            if MIXSTOP <= 3:
                return
            P.op('dve', lambda e: e.tensor_copy(out=cbuf[:, :, 0:3], in_=chist[l][:, :, :]), r=['chist%d' % l], w=['cbuf'])
            for n in range(12):
                for j in range(4):
                    wc = vcol[l][:, VC_CONV + j * 12 + n:VC_CONV + j * 12 + n + 1]
                    ceng = 'pool' if (CONV_POOL and n >= 8) else 'dve'
                    if j == 0:
                        P.op(ceng, lambda e, n=n, wc=wc: e.tensor_scalar(out=qkvs[:, n, 0:T], in0=cbuf[:, n, 0:T], scalar1=wc, scalar2=None, op0=ALU.mult),
                             r=['cbuf', 'vcol%d' % l], w=['qkvs'])
                    else:
                        P.op(ceng, lambda e, n=n, j=j, wc=wc: e.scalar_tensor_tensor(out=qkvs[:, n, 0:T], in0=cbuf[:, n, j:j + T], scalar=wc, in1=qkvs[:, n, 0:T],
                                                                                      op0=ALU.mult, op1=ALU.add), r=['cbuf', 'qkvs', 'vcol%d' % l], w=['qkvs'])
            P.op('dve', lambda e: e.tensor_copy(out=chist[l][:, :, :], in_=cbuf[:, :, T:T + 3]), r=['cbuf'], w=['chist%d' % l])
            P.op('act', lambda e: e.activation(out=qkvs[:, :, 0:T], in_=qkvs[:, :, 0:T], func=AF.Silu), r=['qkvs'], w=['qkvs'])
            P.op('act', lambda e: e.activation(out=sz[:, :, 0:T], in_=sz[:, :, 0:T], func=AF.Silu), r=['sz'], w=['sz'])
            P.mark('HEAVY0')
            if MIXSTOP <= 4:
                return
            for ci in range(T // C):
                gdn_chunk(l, ci * C, C, par)
            if MIXSTOP <= 5:
                return
            proj_residual(l, 'w_out', 8, mix, ['mix_a', 'mix_b', 'mix_c'], T, 2, par)

        def xattn(l, T, par=0):
            hn = hns[par]
            hk = 'hn%d' % par
            rmsnorm(T, l, VC_XATT, hn, hk, par)
            wv, wk = wload('w_mq', l, 0, (8, 512))
            for h in range(4):
                pq, pqk = psum('big')
                for k in range(KC):
                    P.op('pe', lambda e, k=k, h=h, pq=pq: e.matmul(pq[:, 0:T], lhsT=wv[:, k, h * 128:(h + 1) * 128], rhs=hn[:, k, 0:T],
                                                                 start=(k == 0), stop=(k == KC - 1)), r=[hk, wk], w=[pqk])
                P.op('act', lambda e, h=h, pq=pq: e.copy(out=qTb[:, h, 0:T], in_=pq[:, 0:T]), r=[pqk], w=['qTb'])
            for h in range(4):
                pT = pTb[h % 2]
                pTk = 'pTb%d' % (h % 2)
                for mb in range(2):
                    psc, psck = psum('big')
                    P.op('pe', lambda e, h=h, mb=mb, psc=psc: e.matmul(psc[:, 0:T], lhsT=mkT[l][:, h, mb * 128:(mb + 1) * 128], rhs=qTb[:, h, 0:T],
                                                                      start=True, stop=True), r=['mkT%d' % l, 'qTb'], w=[psck])
                    P.op('act', lambda e, mb=mb, psc=psc, pT=pT: e.activation(out=pT[:, mb, 0:T], in_=psc[:, 0:T], func=AF.Exp, scale=float(128 ** -0.5)),
                         r=[psck], w=[pTk])
                psm, psmk = psum('big')
                for mb in range(2):
                    P.op('pe', lambda e, mb=mb, psm=psm, pT=pT: e.matmul(psm[:, 0:T], lhsT=ones_b[:, :], rhs=pT[:, mb, 0:T], start=(mb == 0), stop=(mb == 1)),
                         r=['ones_b', pTk], w=[psmk])
                P.op('act', lambda e, psm=psm: e.activation(out=rsum[:, 0:T], in_=psm[:, 0:T], func=AF.Ln), r=[psmk], w=['rsum'])
                P.op('act', lambda e: e.activation(out=rsum[:, 0:T], in_=rsum[:, 0:T], func=AF.Exp, scale=-1.0), r=['rsum'], w=['rsum'])
                pov, povk = psum('big')
                for mb in range(2):
                    P.op('pe', lambda e, h=h, mb=mb, pov=pov, pT=pT: e.matmul(pov[:, 0:T], lhsT=mvb[l][:, mb, h * 128:(h + 1) * 128], rhs=pT[:, mb, 0:T],
                                                                             start=(mb == 0), stop=(mb == 1)), r=['mvb%d' % l, pTk], w=[povk])
                P.op('dve', lambda e, h=h, pov=pov: e.tensor_tensor(out=oTb[:, h, 0:T], in0=pov[:, 0:T], in1=rsum[:, 0:T], op=ALU.mult),
                     r=[povk, 'rsum'], w=['oTb'])
            proj_residual(l, 'w_mo', 4, oTb, ['oTb'], T, 2, par)

        dctr = [0]

        def seq_setup(is_sample, s_idx):
            for l in range(L):
                if not is_sample:
                    P.op('dve', lambda e, l=l: e.memset(S_st[l][:], 0.0), r=(), w=['S%d' % l])
                    P.op('dve', lambda e, l=l: e.memset(chist[l][:], 0.0), r=(), w=['chist%d' % l])
                    P.op('dve', lambda e, l=l: e.memset(phist[l][:], 0.0), r=(), w=['phist%d' % l])
                else:
                    P.dma('sp', S_st[l][:], I['sssm'][l, s_idx].rearrange("h k v -> k h v"), r=(), w=['S%d' % l])
                    P.dma('sp', sttok[0:3, :], I['sconv'][l, s_idx], r=(), w=['cbuf'])
                    pt, pk = psum('misc')
                    for n in range(12):
                        P.op('pe', lambda e, n=n, pt=pt: e.transpose(out=pt[:, n * 4:n * 4 + 3], in_=sttok[0:3, n * 128:(n + 1) * 128], identity=ident[0:3, 0:3]),
                             r=['cbuf', 'ident'], w=[pk])
                    P.op('dve', lambda e, l=l, pt=pt: e.tensor_copy(out=chist[l][:, :, :], in_=pt[:, 0:48].rearrange("p (n j) -> p n j", j=4)[:, :, 0:3]),
                         r=[pk], w=['chist%d' % l])
                    P.dma('sp', sttok[0:15, 0:256], I['spool'][l, s_idx], r=(), w=['cbuf'])
                    pt, pk = psum('misc')
                    for c in range(2):
                        P.op('pe', lambda e, c=c, pt=pt: e.transpose(out=pt[:, c * 16:c * 16 + 15], in_=sttok[0:15, c * 128:(c + 1) * 128], identity=ident[0:15, 0:15]),
                             r=['cbuf', 'ident'], w=[pk])
                    P.op('dve', lambda e, l=l, pt=pt: e.tensor_copy(out=phist[l][:, :, :], in_=pt[:, 0:32].rearrange("p (c j) -> p c j", j=16)[:, :, 0:15]),
                         r=[pk], w=['phist%d' % l])
            if is_sample:
                for l in range(L):
                    P.dma('sp', memtok[:, :, 0:512], I['cmk'][l, s_idx].rearrange("(mb p) f -> p mb f", p=128), r=(), w=['qkvs'])
                    for h in range(4):
                        pt, pk = psum('misc')
                        for mb in range(2):
                            P.op('pe', lambda e, h=h, mb=mb, pt=pt: e.transpose(out=pt[:, mb * 128:(mb + 1) * 128], in_=memtok[:, mb, h * 128:(h + 1) * 128],
                                                                               identity=ident[:, :]), r=['qkvs', 'ident'], w=[pk])
                        P.op('dve', lambda e, l=l, h=h, pt=pt: e.tensor_copy(out=mkT[l][:, h, :], in_=pt[:, 0:256]), r=[pk], w=['mkT%d' % l])
                    P.dma('sp', memtok[:, :, 512:1024], I['cmv'][l, s_idx].rearrange("(mb p) f -> p mb f", p=128), r=(), w=['qkvs'])
                    P.op('dve', lambda e, l=l: e.tensor_copy(out=mvb[l][:, :, :], in_=memtok[:, :, 512:1024]), r=['qkvs'], w=['mvb%d' % l])
            else:
                P.dma('sp', memtok[:, :, :], I['memp'].rearrange("(mb p) f -> p mb f", p=128), r=(), w=['qkvs'])
                for k in range(KC):
                    pt, pk = psum('misc')
                    for mb in range(2):
                        P.op('pe', lambda e, k=k, mb=mb, pt=pt: e.transpose(out=pt[:, mb * 128:(mb + 1) * 128], in_=memtok[:, mb, k * 128:(k + 1) * 128],
                                                                           identity=ident[:, :]), r=['qkvs', 'ident'], w=[pk])
                    P.op('dve', lambda e, k=k, pt=pt: e.tensor_copy(out=memT[:, k, :], in_=pt[:, 0:256]), r=[pk], w=['cbuf'])
                P.op('act', lambda e: e.activation(out=memn[:, :, :], in_=memT[:, :, :], func=AF.Square), r=['cbuf'], w=['hn0'])
                pt, pk = psum('misc')
                for k in range(KC):
                    P.op('pe', lambda e, k=k, pt=pt: e.matmul(pt[:, 0:MEM], lhsT=ones_b[:, :], rhs=memn[:, k, :], start=(k == 0), stop=(k == KC - 1)),
                         r=['hn0', 'ones_b'], w=[pk])
                P.op('act', lambda e, pt=pt: e.activation(out=rstd[:, 0:MEM], in_=pt[:, 0:MEM], func=AF.Ln, bias=EPS, scale=1.0 / D), r=[pk], w=['rstd'])
                P.op('act', lambda e: e.activation(out=rstd[:, 0:MEM], in_=rstd[:, 0:MEM], func=AF.Exp, scale=-0.5), r=['rstd'], w=['rstd'])
                for l in range(L):
                    for k in range(KC):
                        P.op('dve', lambda e, l=l, k=k: e.scalar_tensor_tensor(out=memn[:, k, :], in0=memT[:, k, :], scalar=vcol[l][:, VC_MEM + k:VC_MEM + k + 1],
                                                                               in1=rstd[:, 0:MEM], op0=ALU.mult, op1=ALU.mult),
                             r=['cbuf', 'rstd', 'vcol%d' % l], w=['hn0'])
                    wkv, wkk = wload('w_mk', l, 0, (8, 512))
                    for h in range(4):
                        pt, pk = psum('misc')
                        for k in range(KC):
                            P.op('pe', lambda e, k=k, h=h, pt=pt, wkv=wkv: e.matmul(pt[:, 0:MEM], lhsT=wkv[:, k, h * 128:(h + 1) * 128], rhs=memn[:, k, :],
                                                                                    start=(k == 0), stop=(k == KC - 1)), r=['hn0', wkk], w=[pk])
                        P.op('dve', lambda e, l=l, h=h, pt=pt: e.tensor_copy(out=mkT[l][:, h, :], in_=pt[:, 0:MEM]), r=[pk], w=['mkT%d' % l])
                    for mb in range(2):
                        pt, pk = psum('misc')
                        for k in range(KC):
                            P.op('pe', lambda e, k=k, mb=mb, pt=pt, wkv=wkv: e.matmul(pt[:, 0:512], lhsT=memn[:, k, mb * 128:(mb + 1) * 128], rhs=wkv[:, k, :],
                                                                                     start=(k == 0), stop=(k == KC - 1)), r=['hn0', wkk], w=[pk])
                        P.op('act', lambda e, pt=pt: e.copy(out=kvst[:, :], in_=pt[:, 0:512]), r=[pk], w=['gt'])
                        P.dma('pool', O['p_mk'][l, mb * 128:(mb + 1) * 128, :], kvst[:, :], r=['gt'], w=())
                    wvv, wvk = wload('w_mv', l, 0, (8, 512))
                    for mb in range(2):
                        pt, pk = psum('misc')
                        for k in range(KC):
                            P.op('pe', lambda e, k=k, mb=mb, pt=pt, wvv=wvv: e.matmul(pt[:, 0:512], lhsT=memn[:, k, mb * 128:(mb + 1) * 128], rhs=wvv[:, k, :],
                                                                                     start=(k == 0), stop=(k == KC - 1)), r=['hn0', wvk], w=[pk])
                        P.op('act', lambda e, pt=pt: e.copy(out=kvst[:, :], in_=pt[:, 0:512]), r=[pk], w=['gt'])
                        P.op('dve', lambda e, l=l, mb=mb, pt=pt: e.tensor_copy(out=mvb[l][:, mb, :], in_=pt[:, 0:512]), r=[pk], w=['mvb%d' % l])
                        P.dma('pool', O['p_mv'][l, mb * 128:(mb + 1) * 128, :], kvst[:, :], r=['gt'], w=())

        def seq_finish(is_sample, s_idx):
            for l in range(L):
                o_ssm = O['s_ssm'][l, s_idx] if is_sample else O['p_ssm'][l]
                o_conv = O['s_conv'][l, s_idx] if is_sample else O['p_conv'][l]
                o_pool = O['s_pool'][l, s_idx] if is_sample else O['p_pool'][l]
                P.dma('pool', o_ssm.rearrange("h k v -> k h v"), S_st[l][:], r=['S%d' % l], w=())
                pt, pk = psum('misc')
                for n in range(4):
                    P.op('pe', lambda e, l=l, n=n, pt=pt: e.transpose(out=pt[0:3, n * 128:(n + 1) * 128], in_=chist[l][:, n, :], identity=ident[:, :]),
                         r=['chist%d' % l, 'ident'], w=[pk])
                P.op('dve', lambda e, pt=pt: e.tensor_copy(out=sttok[0:3, 0:512], in_=pt[0:3, 0:512]), r=[pk], w=['cbuf'])
                for q in range(1, 3):
                    pt, pk = psum('misc')
                    for n in range(4):
                        P.op('pe', lambda e, l=l, n=n, q=q, pt=pt: e.transpose(out=pt[0:3, n * 128:(n + 1) * 128], in_=chist[l][:, 4 * q + n, :], identity=ident[:, :]),
                             r=['chist%d' % l, 'ident'], w=[pk])
                    P.op('dve', lambda e, q=q, pt=pt: e.tensor_copy(out=sttok[0:3, 512 * q:512 * q + 512], in_=pt[0:3, 0:512]), r=[pk], w=['cbuf'])
                P.dma('pool', o_conv, sttok[0:3, :], r=['cbuf'], w=())
                pt, pk = psum('misc')
                for c in range(2):
                    P.op('pe', lambda e, l=l, c=c, pt=pt: e.transpose(out=pt[0:15, c * 128:(c + 1) * 128], in_=phist[l][:, c, :], identity=ident[:, :]),
                         r=['phist%d' % l, 'ident'], w=[pk])
                P.op('dve', lambda e, pt=pt: e.tensor_copy(out=kvst[0:15, 0:256], in_=pt[0:15, 0:256]), r=[pk], w=['gt'])
                P.dma('pool', o_pool, kvst[0:15, 0:256], r=['gt'], w=())

        tctr = [0]

        def run_tile(xsrc, ydst, t0, T, C, first_tile, is_sample, s_idx, par=0):
            xT = xTs[par]
            cur_par[0] = par
            P.mark('X')
            nb = (T + 127) // 128
            stg = []
            for b in range(nb):
                nt = min(128, T - b * 128)
                i = tctr[0] % 2
                tctr[0] += 1
                P.dma('sp', xst[i][0:nt, :], xsrc[t0 + b * 128:t0 + b * 128 + nt, :], r=(), w=['xst%d' % i])
                stg.append((i, nt))
            for k in range(KC):
                pt, pk = psum('misc')
                for b, (i, nt) in enumerate(stg):
                    P.op('pe', lambda e, k=k, b=b, i=i, nt=nt, pt=pt: e.transpose(out=pt[:, b * 128:b * 128 + nt], in_=xst[i][0:nt, k * 128:(k + 1) * 128],
                                                                                 identity=ident[0:nt, 0:nt]), r=['xst%d' % i, 'ident'], w=[pk])
                P.op('act', lambda e, k=k, pt=pt: e.copy(out=xT[:, k, 0:T], in_=pt[:, 0:T]), r=[pk], w=['xT%d_%d' % (par, k)])
            for l in range(L):
                if l == L - 1:
                    P.mark('START_NEXT')
                if DBG is None or (l, 'ffn1') in DBG:
                    P.mark('F')
                    ffn(l, 1, T, par)
                if DBG is None or (l, 'mixer') in DBG:
                    P.mark('M')
                    mixer(l, T, C, first_tile, is_sample, s_idx, par)
                if DBG is None or (l, 'xattn') in DBG:
                    P.mark('A')
                    xattn(l, T, par)
                if DBG is None or (l, 'ffn2') in DBG:
                    P.mark('F')
                    ffn(l, 2, T, par)
            P.mark('X')
            P.mark('AB')
            P.op('act', lambda e: e.activation(out=sqb[:, :, 0:T], in_=xT[:, :, 0:T], func=AF.Square),
                 r=['xT%d_%d' % (par, k) for k in range(KC)], w=['sqb'])
            pt, pk = psum('misc')
            for k in range(KC):
                P.op('pe', lambda e, k=k, pt=pt: e.matmul(pt[:, 0:T], lhsT=ones_b[:, :], rhs=sqb[:, k, 0:T], start=(k == 0), stop=(k == KC - 1)),
                     r=['sqb', 'ones_b'], w=[pk])
            P.op('act', lambda e, pt=pt: e.activation(out=rstd[:, 0:T], in_=pt[:, 0:T], func=AF.Ln, bias=EPS, scale=1.0 / D), r=[pk], w=['rstd'])
            P.op('act', lambda e: e.activation(out=rstd[:, 0:T], in_=rstd[:, 0:T], func=AF.Exp, scale=-0.5), r=['rstd'], w=['rstd'])
            ysl = []
            for b_ in range(nb):
                i = tctr[0] % 2
                tctr[0] += 1
                ysl.append(i)
            for k in range(KC):
                P.op('dve', lambda e, k=k: e.scalar_tensor_tensor(out=ynk[:, 0:T], in0=xT[:, k, 0:T], scalar=vcol[0][:, VC_FINAL + k:VC_FINAL + k + 1],
                                                                     in1=rstd[:, 0:T], op0=ALU.mult, op1=ALU.mult),
                     r=['xT%d_%d' % (par, k), 'rstd', 'vcol0'], w=['ynk'])
                pt, pk = psum('misc')
                for b_ in range(nb):
                    nt = min(128, T - b_ * 128)
                    P.op('pe', lambda e, b_=b_, nt=nt, pt=pt: e.transpose(out=pt[0:nt, b_ * 128:(b_ + 1) * 128], in_=ynk[:, b_ * 128:b_ * 128 + nt],
                                                                          identity=ident[:, :]), r=['ynk', 'ident'], w=[pk])
                for b_ in range(nb):
                    nt = min(128, T - b_ * 128)
                    i = ysl[b_]
                    P.op('act', lambda e, i=i, k=k, b_=b_, nt=nt, pt=pt: e.copy(out=yst[i][0:nt, k * 128:(k + 1) * 128], in_=pt[0:nt, b_ * 128:(b_ + 1) * 128]),
                         r=[pk], w=['xst%d' % i])
            for b_ in range(nb):
                nt = min(128, T - b_ * 128)
                i = ysl[b_]
                P.dma('pool', ydst[t0 + b_ * 128:t0 + b_ * 128 + nt, :], yst[i][0:nt, :], r=['xst%d' % i], w=())
            P.mark('AE')

        for s in range(NS):
            if STAGE >= 2 + s:
                seq_setup(True, s)
                if STAGE != 20:
                    run_tile(I['xs'][s], O['y_s'][s], 0, LS, LS, False, True, s)
                if STAGE != 21:
                    seq_finish(True, s)
        if STAGE >= 4:
            seq_setup(False, 0)
        if STAGE >= 5:
            lists = []
            for ti in range(SEQ // TP):
                P.rec = []
                run_tile(I['xp'], O['y_p'], ti * TP, TP, min(128, TP), ti == 0, False, 0, ti % 2)
                lists.append(P.rec)
                P.rec = None
            class Cur:
                def __init__(self, lst):
                    self.lst = lst
                    self.pos = 0
                    self.typ = None
                    self.want_next = False
                    self.depth = 0
                    self.heavy = False

                def done(self):
                    return self.pos >= len(self.lst)

                def step(self, other):
                    while self.pos < len(self.lst):
                        rc = self.lst[self.pos]
                        if rc[0] == 'mark':
                            m = rc[1]
                            if m == 'START_NEXT':
                                self.want_next = True
                            elif m == 'HEAVY1':
                                self.heavy = True
                            elif m == 'HEAVY0':
                                self.heavy = False
                            elif m == 'AB':
                                self.depth += 1
                            elif m == 'AE':
                                self.depth -= 1
                                if self.depth == 0:
                                    self.pos += 1
                                    return True
                            else:
                                if other is not None and not other.done() and other.typ == m:
                                    return False
                                self.typ = m
                            self.pos += 1
                            continue
                        if not P.feed_one(rc):
                            assert self.depth == 0
                            return False
                        self.pos += 1
                        if rc[0] in ('op', 'dma'):
                            vt[rc[1]] = max(vt[rc[1]], min(vt.values())) + COST[rc[1]]
                        if self.depth == 0:
                            return True
                    self.typ = None
                    return True

                def next_eng(self):
                    p = self.pos
                    while p < len(self.lst):
                        rc = self.lst[p]
                        if rc[0] in ('op', 'dma'):
                            return rc[1]
                        if rc[0] == 'custom':
                            return 'sp'
                        p += 1
                    return 'sp'
            flip = [0]
            COST = {'pe': 110.0, 'dve': 420.0, 'act': 480.0, 'pool': 100.0, 'sp': 60.0}
            vt = {e: 0.0 for e in ENGS}
            active = [Cur(lists[0])]
            nxt = 1
            while active:
                if PIPELINE and len(active) == 1 and active[0].want_next and nxt < len(lists):
                    active.append(Cur(lists[nxt]))
                    nxt += 1
                if len(active) == 2:
                    a_, b_ = active
                    lag = a_.pos / len(a_.lst) - b_.pos / len(b_.lst)
                    if POLICY == 'old':
                        ra = a_.step(b_)
                        rb = b_.step(a_)
                        assert ra or rb
                        active = [c for c in active if not c.done()]
                        continue
                    if POLICY == 'ratio':
                        fa, fb = a_.typ == 'F', b_.typ == 'F'
                        wa = HRATIO if (fa and b_.heavy) else (FRATIO if (fa and not fb) else 1)
                        wb = HRATIO if (fb and a_.heavy) else (FRATIO if (fb and not fa) else 1)
                        flip[0] = (flip[0] + 1) % (wa + wb)
                        first, second = (a_, b_) if flip[0] < wa else (b_, a_)
                    elif POLICY == 'alt':
                        flip[0] ^= 1
                        first, second = (a_, b_) if flip[0] else (b_, a_)
                    elif lag > 0.56:
                        first, second = b_, a_
                    elif lag < 0.44:
                        first, second = a_, b_
                    elif vt[a_.next_eng()] <= vt[b_.next_eng()]:
                        first, second = a_, b_
                    else:
                        first, second = b_, a_
                    if not first.step(second):
                        ok = second.step(first)
                        assert ok, (first.typ, first.pos, len(first.lst), first.lst[first.pos][:2], second.typ, second.pos, len(second.lst), second.lst[second.pos][:2], first.depth, second.depth)
                else:
                    active[0].step(None)
                active = [c for c in active if not c.done()]
                if not active and nxt < len(lists):
                    active.append(Cur(lists[nxt]))
                    nxt += 1
            cur_par[0] = 0
            seq_finish(False, 0)
        P.emit()
        nops = {e: len(P.ops[e]) for e in ENGS}
    return nc, nops


_CACHE = {}


def _get_program(SEQ, TP):
    key = (SEQ, TP)
    if key not in _CACHE:
        _CACHE[key] = build_program(SEQ, TP)
    return _CACHE[key]


W_NAMES = ['ffn1_norm', 'ffn1_w_gate', 'ffn1_w_up', 'ffn1_w_down', 'mix_norm', 'w_in', 'sgu_norm', 'sgu_w', 'sgu_b',
           'pool_w', 'pool_scale', 'gdn_conv_w', 'gdn_a_log', 'gdn_dt_bias', 'gdn_out_norm', 'w_out', 'xattn_norm',
           'mem_norm', 'w_mq', 'w_mk', 'w_mv', 'w_mo', 'ffn2_norm', 'ffn2_w_gate', 'ffn2_w_up', 'ffn2_w_down', 'final_norm']

TP_DEFAULT = 256


def kernel(**inputs):
    x_prompt = np.asarray(inputs['x_prompt'], dtype=np.float32)
    B, SEQ, _ = x_prompt.shape
    x_sample = np.asarray(inputs['x_sample'], dtype=np.float32)
    NSB, LS, _ = x_sample.shape
    NS = NSB // B
    TP = min(TP_DEFAULT, SEQ)
    nc, nops = _get_program(SEQ, TP)
    f32 = lambda a: np.ascontiguousarray(np.asarray(a, dtype=np.float32))
    wts = {nm: f32(inputs[nm]) for nm in W_NAMES}
    in_maps = []
    for c in range(NCORES):
        b = c % B
        m = dict(wts)
        m['xp'] = f32(x_prompt[b])
        m['xs'] = f32(x_sample[NS * b:NS * b + NS])
        m['memp'] = f32(inputs['mem_prompt'][b])
        m['cmk'] = f32(np.asarray(inputs['cache_mem_k'])[:, NS * b:NS * b + NS].reshape(DEPTH, NS, MEM, 512))
        m['cmv'] = f32(np.asarray(inputs['cache_mem_v'])[:, NS * b:NS * b + NS].reshape(DEPTH, NS, MEM, 512))
        m['spool'] = f32(np.asarray(inputs['state_pool'])[:, NS * b:NS * b + NS])
        m['sconv'] = f32(np.asarray(inputs['state_conv'])[:, NS * b:NS * b + NS])
        m['sssm'] = f32(np.asarray(inputs['state_ssm'])[:, NS * b:NS * b + NS])
        in_maps.append(m)
    res = run_bass_kernel_spmd(nc, in_maps, core_ids=list(range(NCORES)))
    R = res.results
    R = [R[i % len(R)] for i in range(B)]
    y_prompt = np.stack([R[b]['y_p'] for b in range(B)])
    y_sample = np.concatenate([R[b]['y_s'] for b in range(B)], axis=0)
    p_pool = np.stack([R[b]['p_pool'] for b in range(B)], axis=1)
    p_conv = np.stack([R[b]['p_conv'] for b in range(B)], axis=1)
    p_ssm = np.stack([R[b]['p_ssm'] for b in range(B)], axis=1)
    p_mk = np.stack([R[b]['p_mk'] for b in range(B)], axis=1).reshape(DEPTH, B, MEM, 4, 128)
    p_mv = np.stack([R[b]['p_mv'] for b in range(B)], axis=1).reshape(DEPTH, B, MEM, 4, 128)
    s_pool = np.concatenate([R[b]['s_pool'] for b in range(B)], axis=1)
    s_conv = np.concatenate([R[b]['s_conv'] for b in range(B)], axis=1)
    s_ssm = np.concatenate([R[b]['s_ssm'] for b in range(B)], axis=1)
    s_v = np.concatenate([R[b]['s_v'] for b in range(B)], axis=1)
    outs = (y_prompt, y_sample, p_pool, p_conv, p_ssm, p_mk, p_mv, s_pool, s_conv, s_ssm, s_v)
    return tuple(np.ascontiguousarray(o, dtype=np.float32) for o in outs)
```
